# Optimizing a Trainium2 kernel written in Bass

```python
import math
import jax, jax.numpy as jnp
from jax import lax
import numpy as np

D_MODEL = 2048
BATCH = 4
SEQ = 4096
DEPTH = 2

CHUNK = 64
EPS = 1e-6
MIX_WIDTH = 2 * D_MODEL
N_EVEN = (DEPTH + 1) // 2
N_ODD = DEPTH // 2
SSD_HEADDIM = 64
SSD_INNER = 3 * MIX_WIDTH // 4
SSD_HEADS = SSD_INNER // SSD_HEADDIM
SSD_GROUPS = 8
SSD_HPG = SSD_HEADS // SSD_GROUPS
SSD_STATE = 128
SSD_CONV = 4
SSD_XBC = SSD_INNER + 2 * SSD_GROUPS * SSD_STATE
S5_WIDTH = MIX_WIDTH - SSD_INNER
S5_GROUP_CH = 16
S5_GROUPS = S5_WIDTH // S5_GROUP_CH
S5_STATE = 64
IN0_WIDTH = SSD_INNER + SSD_XBC + SSD_HEADS + S5_WIDTH
RET_HEADS = 8
RET_QK = D_MODEL // RET_HEADS
RET_V = MIX_WIDTH // RET_HEADS
IN1_WIDTH = 2 * D_MODEL + 2 * MIX_WIDTH
ROPE_BASE = 10000.0
FFN_HIDDEN = ((-(-8 * D_MODEL // 3) + 255) // 256) * 256

kernel_name = "hybrid_ssd_s5_retention_adaln_trunk"


def rmsnorm(x, g):
    xf = x.astype(jnp.float32)
    y = xf * lax.rsqrt(jnp.mean(xf * xf, axis=-1, keepdims=True) + EPS)
    return (y * g.astype(jnp.float32)).astype(x.dtype)


def ada_modulation(c, w, b):
    m = (jnp.dot(jax.nn.silu(c), w) + b)[:, None, :]
    shift, scale, gate = jnp.split(m, 3, axis=-1)
    return shift, scale, gate


def causal_depthwise_conv(x, w, b):
    k = w.shape[0]
    y = lax.conv_general_dilated(x, w[:, None, :], window_strides=(1,), padding=[(k - 1, 0)],
                                 dimension_numbers=('NWC', 'WIO', 'NWC'),
                                 feature_group_count=x.shape[-1])
    return y + b


def to_chunks(t):
    b, l = t.shape[:2]
    return jnp.moveaxis(t.reshape(b, l // CHUNK, CHUNK, *t.shape[2:]), 1, 0)


def from_chunks(t):
    t = jnp.moveaxis(t, 0, 1)
    return t.reshape(t.shape[0], t.shape[1] * t.shape[2], *t.shape[3:])


def ssd_chunk_scan(xh, dt, a, bm, cm):
    b = xh.shape[0]
    causal = jnp.tril(jnp.ones((CHUNK, CHUNK), dtype=bool))

    def step(state, inp):
        xc, dtc, bc, cc = inp
        acum = jnp.cumsum(dtc * a, axis=1)
        seg = acum[:, :, None] - acum[:, None, :]
        lmat = jnp.exp(jnp.where(causal[None, :, :, None, None], seg, -jnp.inf))
        xdt = xc * dtc[..., None]
        scores = jnp.einsum('blgn,bsgn->blsg', cc, bc)
        y_diag = jnp.einsum('blsg,blsgh,bsghp->blghp', scores, lmat, xdt)
        y_off = jnp.einsum('blgn,bghpn->blghp', cc, state) * jnp.exp(acum)[..., None]
        decay = jnp.exp(acum[:, -1:] - acum)
        new_state = (state * jnp.exp(acum[:, -1])[..., None, None]
                     + jnp.einsum('blgn,blgh,blghp->bghpn', bc, decay, xdt))
        return new_state, y_diag + y_off

    state0 = jnp.zeros((b, SSD_GROUPS, SSD_HPG, SSD_HEADDIM, SSD_STATE), jnp.float32)
    _, y = lax.scan(step, state0, (to_chunks(xh), to_chunks(dt), to_chunks(bm), to_chunks(cm)))
    return from_chunks(y)


def linear_recurrence_op(e1, e2):
    a1, b1 = e1
    a2, b2 = e2
    return a1 * a2, a2 * b1 + b2


def s5_branch(u, a_re, a_im, log_dt, b_re, b_im, c_re, c_im, d, glu_w, glu_b):
    f32 = jnp.float32
    b, l, _ = u.shape
    uf = u.astype(f32)
    ug = uf.reshape(b, l, S5_GROUPS, S5_GROUP_CH)
    lam = lax.complex(a_re.astype(f32), a_im.astype(f32))
    delta = jnp.exp(log_dt.astype(f32))[:, None]
    lam_bar = jnp.exp(lam * delta)
    b_bar = ((lam_bar - 1) / lam)[..., None] * lax.complex(b_re.astype(f32), b_im.astype(f32))
    bu = jnp.einsum('gpc,blgc->blgp', b_bar, ug.astype(jnp.complex64))
    a_seq = jnp.broadcast_to(lam_bar, (1, l) + lam_bar.shape)
    _, states = lax.associative_scan(linear_recurrence_op, (a_seq, bu), axis=1)
    c_mat = lax.complex(c_re.astype(f32), c_im.astype(f32))
    y = jnp.real(jnp.einsum('gcp,blgp->blgc', c_mat, states)).reshape(b, l, S5_WIDTH)
    y = jax.nn.gelu(y + d.astype(f32) * uf)
    y = y * jax.nn.sigmoid(y @ glu_w.astype(f32) + glu_b.astype(f32))
    return y.astype(u.dtype)


def ssd_s5_mixer(h, w_in, conv_w, conv_b, dt_bias, a_log, d_skip, norm_g,
                 a_re, a_im, log_dt, b_re, b_im, c_re, c_im, s5_d, glu_w, glu_b, w_out):
    f32 = jnp.float32
    dtype = h.dtype
    b, l, _ = h.shape
    proj = h @ w_in
    z, xbc, dt_raw, u = jnp.split(proj, [SSD_INNER, SSD_INNER + SSD_XBC,
                                         SSD_INNER + SSD_XBC + SSD_HEADS], axis=-1)
    xbc = jax.nn.silu(causal_depthwise_conv(xbc, conv_w, conv_b))
    xs, bm, cm = jnp.split(xbc, [SSD_INNER, SSD_INNER + SSD_GROUPS * SSD_STATE], axis=-1)
    xs = xs.astype(f32).reshape(b, l, SSD_GROUPS, SSD_HPG, SSD_HEADDIM)
    bm = bm.astype(f32).reshape(b, l, SSD_GROUPS, SSD_STATE)
    cm = cm.astype(f32).reshape(b, l, SSD_GROUPS, SSD_STATE)
    dt = jax.nn.softplus(dt_raw.astype(f32) + dt_bias.astype(f32)).reshape(b, l, SSD_GROUPS, SSD_HPG)
    a = -jnp.exp(a_log.astype(f32)).reshape(SSD_GROUPS, SSD_HPG)
    y = ssd_chunk_scan(xs, dt, a, bm, cm) + d_skip.astype(f32).reshape(SSD_GROUPS, SSD_HPG, 1) * xs
    y = y.reshape(b, l, SSD_GROUPS, SSD_HPG * SSD_HEADDIM) * \
        jax.nn.silu(z.astype(f32)).reshape(b, l, SSD_GROUPS, SSD_HPG * SSD_HEADDIM)
    y = y * lax.rsqrt(jnp.mean(y * y, axis=-1, keepdims=True) + EPS)
    y_a = (y.reshape(b, l, SSD_INNER) * norm_g.astype(f32)).astype(dtype)
    y_b = s5_branch(u, a_re, a_im, log_dt, b_re, b_im, c_re, c_im, s5_d, glu_w, glu_b)
    return jnp.concatenate([y_a, y_b], axis=-1) @ w_out


def rotary(t, cos, sin):
    t1, t2 = jnp.split(t, 2, axis=-1)
    return jnp.concatenate([t1 * cos - t2 * sin, t1 * sin + t2 * cos], axis=-1)


def retention_chunkwise(q, k, v):
    f32 = jnp.float32
    b = q.shape[0]
    log_g = jnp.log(1.0 - 2.0 ** (-5.0 - jnp.arange(RET_HEADS, dtype=f32)))
    idx = jnp.arange(CHUNK, dtype=f32)
    d_intra = jnp.exp(log_g[:, None, None] * jnp.abs(idx[:, None] - idx[None, :]))
    xi = jnp.exp(log_g[:, None] * (idx + 1.0)).T[None, :, :, None]
    zeta = jnp.exp(log_g[:, None] * (CHUNK - 1.0 - idx)).T[None, :, :, None]
    g_chunk = jnp.exp(log_g * CHUNK)[None, :, None, None]

    def step(r, inp):
        qc, kc, vc = inp
        s = jnp.einsum('bihd,bjhd->bhij', qc, kc) * d_intra
        o = jnp.einsum('bhij,bjhe->bihe', s, vc)
        o = o + jnp.einsum('bihd,bhde->bihe', qc * xi, r)
        r = r * g_chunk + jnp.einsum('bjhd,bjhe->bhde', kc * zeta, vc)
        return r, o

    r0 = jnp.zeros((b, RET_HEADS, RET_QK, RET_V), f32)
    _, o = lax.scan(step, r0, (to_chunks(q), to_chunks(k), to_chunks(v)))
    return from_chunks(o)


def retention_mixer(h, w_in, gn_g, w_out):
    f32 = jnp.float32
    dtype = h.dtype
    b, l, _ = h.shape
    proj = h @ w_in
    q, k, v, g = jnp.split(proj, [D_MODEL, 2 * D_MODEL, 2 * D_MODEL + MIX_WIDTH], axis=-1)
    q = q.astype(f32).reshape(b, l, RET_HEADS, RET_QK)
    k = k.astype(f32).reshape(b, l, RET_HEADS, RET_QK) * (RET_QK ** -0.5)
    v = v.astype(f32).reshape(b, l, RET_HEADS, RET_V)
    inv_freq = ROPE_BASE ** (-jnp.arange(0, RET_QK, 2, dtype=f32) / RET_QK)
    ang = jnp.arange(l, dtype=f32)[:, None] * inv_freq[None, :]
    cos, sin = jnp.cos(ang)[:, None, :], jnp.sin(ang)[:, None, :]
    q = rotary(q, cos, sin)
    k = rotary(k, cos, sin)
    y = retention_chunkwise(q, k, v)
    mu = jnp.mean(y, axis=-1, keepdims=True)
    yc = y - mu
    y = yc * lax.rsqrt(jnp.mean(yc * yc, axis=-1, keepdims=True) + EPS)
    y = y.reshape(b, l, MIX_WIDTH) * gn_g.astype(f32) * jax.nn.silu(g.astype(f32))
    return y.astype(dtype) @ w_out


def swiglu(h, w_in, w_out):
    gate, up = jnp.split(h @ w_in, 2, axis=-1)
    return (jax.nn.silu(gate) * up) @ w_out


def setup_inputs(seed: int = 0) -> dict:
    key = jax.random.key(seed)
    ks = iter(jax.random.split(key, 48))
    f32 = jnp.float32
    D = D_MODEL

    def nrm(shape, std):
        return std * jax.random.normal(next(ks), shape, f32)

    def gain(shape):
        return 1.0 + nrm(shape, 0.02)

    x = nrm((BATCH, SEQ, D), 1.0)
    c = nrm((BATCH, D), 1.0)
    norm_mix_g = gain((DEPTH, D))
    ada_mix_w = nrm((DEPTH, D, 3 * D), 0.5 * D ** -0.5)
    ada_mix_b = nrm((DEPTH, 3 * D), 0.01)
    norm_ffn_g = gain((DEPTH, D))
    ada_ffn_w = nrm((DEPTH, D, 3 * D), 0.5 * D ** -0.5)
    ada_ffn_b = nrm((DEPTH, 3 * D), 0.01)
    ffn_w_in = nrm((DEPTH, D, 2 * FFN_HIDDEN), D ** -0.5)
    ffn_w_out = nrm((DEPTH, FFN_HIDDEN, D), FFN_HIDDEN ** -0.5)
    ab_w_in = nrm((N_EVEN, D, IN0_WIDTH), D ** -0.5)
    ssd_conv_w = nrm((N_EVEN, SSD_CONV, SSD_XBC), SSD_CONV ** -0.5)
    ssd_conv_b = nrm((N_EVEN, SSD_XBC), 0.01)
    dt0 = jnp.exp(jax.random.uniform(next(ks), (N_EVEN, SSD_HEADS), f32, math.log(1e-3), math.log(1e-1)))
    ssd_dt_bias = dt0 + jnp.log(-jnp.expm1(-dt0))
    ssd_a_log = jnp.log(jax.random.uniform(next(ks), (N_EVEN, SSD_HEADS), f32, 1.0, 16.0))
    ssd_d = gain((N_EVEN, SSD_HEADS))
    ssd_norm_g = gain((N_EVEN, SSD_INNER))
    s5_a_re = -0.5 + nrm((N_EVEN, S5_GROUPS, S5_STATE), 1e-3)
    s5_a_im = jnp.broadcast_to(math.pi * jnp.arange(S5_STATE, dtype=f32), (N_EVEN, S5_GROUPS, S5_STATE))
    s5_log_dt = jax.random.uniform(next(ks), (N_EVEN, S5_GROUPS), f32, math.log(1e-3), math.log(1e-1))
    s5_b_re = nrm((N_EVEN, S5_GROUPS, S5_STATE, S5_GROUP_CH), (2 * S5_GROUP_CH) ** -0.5)
    s5_b_im = nrm((N_EVEN, S5_GROUPS, S5_STATE, S5_GROUP_CH), (2 * S5_GROUP_CH) ** -0.5)
    s5_c_re = nrm((N_EVEN, S5_GROUPS, S5_GROUP_CH, S5_STATE), (2 * S5_STATE) ** -0.5)
    s5_c_im = nrm((N_EVEN, S5_GROUPS, S5_GROUP_CH, S5_STATE), (2 * S5_STATE) ** -0.5)
    s5_d = nrm((N_EVEN, S5_WIDTH), 1.0)
    s5_glu_w = nrm((N_EVEN, S5_WIDTH, S5_WIDTH), S5_WIDTH ** -0.5)
    s5_glu_b = nrm((N_EVEN, S5_WIDTH), 0.01)
    ab_w_out = nrm((N_EVEN, MIX_WIDTH, D), MIX_WIDTH ** -0.5)
    ret_w_in = nrm((N_ODD, D, IN1_WIDTH), D ** -0.5)
    ret_gn_g = gain((N_ODD, MIX_WIDTH))
    ret_w_out = nrm((N_ODD, MIX_WIDTH, D), MIX_WIDTH ** -0.5)
    final_norm_g = gain((D,))
    return {
        "x": x, "c": c,
        "norm_mix_g": norm_mix_g, "ada_mix_w": ada_mix_w, "ada_mix_b": ada_mix_b,
        "norm_ffn_g": norm_ffn_g, "ada_ffn_w": ada_ffn_w, "ada_ffn_b": ada_ffn_b,
        "ffn_w_in": ffn_w_in, "ffn_w_out": ffn_w_out,
        "ab_w_in": ab_w_in, "ssd_conv_w": ssd_conv_w, "ssd_conv_b": ssd_conv_b,
        "ssd_dt_bias": ssd_dt_bias, "ssd_a_log": ssd_a_log, "ssd_d": ssd_d, "ssd_norm_g": ssd_norm_g,
        "s5_a_re": s5_a_re, "s5_a_im": s5_a_im, "s5_log_dt": s5_log_dt,
        "s5_b_re": s5_b_re, "s5_b_im": s5_b_im, "s5_c_re": s5_c_re, "s5_c_im": s5_c_im,
        "s5_d": s5_d, "s5_glu_w": s5_glu_w, "s5_glu_b": s5_glu_b, "ab_w_out": ab_w_out,
        "ret_w_in": ret_w_in, "ret_gn_g": ret_gn_g, "ret_w_out": ret_w_out,
        "final_norm_g": final_norm_g,
    }


def reference(x, c, norm_mix_g, ada_mix_w, ada_mix_b, norm_ffn_g, ada_ffn_w, ada_ffn_b,
              ffn_w_in, ffn_w_out, ab_w_in, ssd_conv_w, ssd_conv_b, ssd_dt_bias, ssd_a_log,
              ssd_d, ssd_norm_g, s5_a_re, s5_a_im, s5_log_dt, s5_b_re, s5_b_im, s5_c_re, s5_c_im,
              s5_d, s5_glu_w, s5_glu_b, ab_w_out, ret_w_in, ret_gn_g, ret_w_out, final_norm_g):
    for i in range(DEPTH):
        j = i // 2
        shift, scale, gate = ada_modulation(c, ada_mix_w[i], ada_mix_b[i])
        hn = rmsnorm(x, norm_mix_g[i]) * (1 + scale) + shift
        if i % 2 == 0:
            y = ssd_s5_mixer(hn, ab_w_in[j], ssd_conv_w[j], ssd_conv_b[j], ssd_dt_bias[j], ssd_a_log[j],
                             ssd_d[j], ssd_norm_g[j], s5_a_re[j], s5_a_im[j], s5_log_dt[j],
                             s5_b_re[j], s5_b_im[j], s5_c_re[j], s5_c_im[j], s5_d[j],
                             s5_glu_w[j], s5_glu_b[j], ab_w_out[j])
        else:
            y = retention_mixer(hn, ret_w_in[j], ret_gn_g[j], ret_w_out[j])
        x = x + gate * y
        shift, scale, gate = ada_modulation(c, ada_ffn_w[i], ada_ffn_b[i])
        hn = rmsnorm(x, norm_ffn_g[i]) * (1 + scale) + shift
        x = x + gate * swiglu(hn, ffn_w_in[i], ffn_w_out[i])
    return rmsnorm(x, final_norm_g)
```

```python
import numpy as np
import concourse.bass as bass
import concourse.mybir as mybir
from concourse.bass_utils import run_bass_kernel_spmd
from contextlib import ExitStack

F32 = mybir.dt.float32
BF16 = mybir.dt.bfloat16
ALU = mybir.AluOpType
AF = mybir.ActivationFunctionType
AX = mybir.AxisListType

D = 2048
DC = D // 128
SEQ = 4096
FFN_H = 5632
HC = FFN_H // 128
EPS = 1e-6

ENGS = ["tensor", "vector", "scalar", "gpsimd", "sync"]
SEM_CAP = 20000
DMA_RING = 6


class Prog:
    def __init__(self, nc, es, G=None):
        self.nc = nc
        self.es = es
        self.G = G if G is not None else SemState(es)
        self.ops = {e: [] for e in ENGS}
        self.last_w = {}
        self.readers = {}

    def op(self, eng, fn, reads=(), writes=(), dma=False):
        idx = len(self.ops[eng])
        tok = (eng, idx)
        waits = set()
        for k in reads:
            w = self.last_w.get(k)
            if w is not None:
                waits.add(w)
        for k in writes:
            w = self.last_w.get(k)
            if w is not None:
                waits.add(w)
            for r in self.readers.get(k, ()):
                waits.add(r)
        waits.discard(tok)
        self.ops[eng].append(dict(fn=fn, waits=waits, dma=dma, need_inc=False))
        for k in reads:
            self.readers.setdefault(k, []).append(tok)
        for k in writes:
            self.last_w[k] = tok
            self.readers[k] = []
        return tok

    def barrier(self, c):
        keys = list(set(list(self.last_w.keys()) + list(self.readers.keys())))
        if not hasattr(c, "bar_t"):
            c.bar_t = sb(c, "bar_t", [128, 8], F32)
            c.bar_p = ps(c, "bar_p", [128, 8], F32) if getattr(c, "bar_psum", False) else None
        self.op("vector", lambda e: e.memset(c.bar_t[:, 0:1], 0.0), writes=keys)
        self.op("scalar", lambda e: e.copy(out=c.bar_t[:, 2:3], in_=c.bar_t[:, 0:1]), writes=keys)
        self.op("gpsimd", lambda e: e.memset(c.bar_t[:, 4:5], 0.0), writes=keys)
        self.op("vector", lambda e: e.memset(c.bar_t[:, 6:7], 0.0), writes=keys)

    def finalize(self, final_waits=()):
        nc, es = self.nc, self.es
        ops = self.ops
        for eng in ENGS:
            seen = {}
            for i, o in enumerate(ops[eng]):
                pr = []
                for (e2, j) in sorted(o["waits"]):
                    tgt = ops[e2][j]
                    if tgt["dma"]:
                        key = ("dma", e2, j)
                        if key in seen:
                            continue
                        seen[key] = True
                        pr.append((e2, j))
                    else:
                        if e2 == eng and eng == "tensor":
                            continue
                        if seen.get(e2, -1) >= j:
                            continue
                        seen[e2] = j
                        pr.append((e2, j))
                        tgt["need_inc"] = True
                o["pw"] = pr
        G = self.G
        for eng in ENGS:
            for o in ops[eng]:
                if o["dma"] or not o["need_inc"]:
                    continue
                if G.ccount[eng] >= SEM_CAP:
                    G.cepoch[eng] += 1
                    G.ccount[eng] = 0
                G.ccount[eng] += 1
                key = (eng, G.cepoch[eng])
                if key not in G.csem:
                    G.csem[key] = G.es.enter_context(nc.semaphore(f"c_{eng}_{G.cepoch[eng]}"))
                o["sem"] = G.csem[key]
                o["val"] = G.ccount[eng]
        for eng in ENGS:
            for o in ops[eng]:
                if not o["dma"]:
                    continue
                n = G.dn[eng]
                slot = n % DMA_RING
                if (eng, slot) not in G.dsem:
                    G.dsem[(eng, slot)] = G.es.enter_context(nc.semaphore(f"d_{eng}_{slot}"))
                o["sem"] = G.dsem[(eng, slot)]
                o["val"] = 16 * (n // DMA_RING + 1)
                o["prev"] = (G.dsem[(eng, slot)], 16 * (n // DMA_RING)) if n >= DMA_RING else None
                G.dn[eng] = n + 1
        block = es.enter_context(nc.Block())

        def make(eng):
            def body(e):
                for o in ops[eng]:
                    for (e2, j) in o["pw"]:
                        t = ops[e2][j]
                        e.wait_ge(t["sem"], t["val"])
                    if o["dma"] and o["prev"] is not None:
                        e.wait_ge(o["prev"][0], o["prev"][1])
                    ins = o["fn"](e)
                    if o["dma"]:
                        ins.then_inc(o["sem"], 16)
                    elif o["need_inc"]:
                        ins.then_inc(o["sem"], 1)
                if eng == "sync":
                    for (e2, j) in final_waits:
                        t = ops[e2][j]
                        e.wait_ge(t["sem"], t["val"])
            return body

        block.tensor(make("tensor"))
        block.vector(make("vector"))
        block.scalar(make("scalar"))
        block.gpsimd(make("gpsimd"))
        block.sync(make("sync"))


class SemState:
    def __init__(self, es):
        self.es = es
        self.csem = {}
        self.dsem = {}
        self.ccount = {e: 0 for e in ENGS}
        self.cepoch = {e: 0 for e in ENGS}
        self.dn = {e: 0 for e in ENGS}


class Env:
    def __init__(self, nc, ges):
        self.nc = nc
        self.G = SemState(ges)
        self.io = {}
        self.pfx = ""
        self.shared = {}

    def tensor(self, name, shape):
        if name in self.io:
            return self.io[name]
        if name in ("ccol", "ident", "iota512"):
            if name not in self.shared:
                self.shared[name] = self.nc.dram_tensor(name, list(shape), F32, kind="ExternalInput").ap()
            return self.shared[name]
        return self.nc.dram_tensor(self.pfx + name, list(shape), F32, kind="ExternalInput").ap()


def mk_nc(env):
    return env.nc if env is not None else bass.Bass("TRN2", target_bir_lowering=False)


def mk_in(nc, env):
    if env is None:
        return lambda n, s: nc.dram_tensor(n, list(s), F32, kind="ExternalInput").ap()
    return lambda n, s: env.tensor(n, s)


def mk_out(nc, env, name, shape):
    if env is None:
        return nc.dram_tensor(name, list(shape), F32, kind="ExternalOutput").ap()
    return env.io[name]


class Ctx:
    pfx = ""


def sb(c, name, shape, dt):
    return c.es.enter_context(c.nc.sbuf_tensor("s_" + c.pfx + name, list(shape), dt))


def ps(c, name, shape, dt=F32):
    return c.es.enter_context(c.nc.psum_tensor("p_" + c.pfx + name, list(shape), dt))


def emit_adaln(c, P, tag, ccol_dram, w_dram, b_dram, g_dram, wslots, psum_t, pkey=("psum", "ada")):
    nc = c.nc
    cc = sb(c, f"cc_{tag}", [128, DC], F32)
    sc = sb(c, f"sc_{tag}", [128, DC], BF16)
    bb = sb(c, f"bb_{tag}", [128, 48], F32)
    gg = sb(c, f"gg_{tag}", [128, DC], F32)
    mod = sb(c, f"mod_{tag}", [128, 48], F32)
    A = sb(c, f"A_{tag}", [128, DC], F32)
    P.op("sync", lambda e: e.dma_start(out=cc[:], in_=ccol_dram), writes=[("cc", tag)], dma=True)
    P.op("sync", lambda e: e.dma_start(out=bb[:], in_=b_dram), writes=[("bb", tag)], dma=True)
    P.op("sync", lambda e: e.dma_start(out=gg[:], in_=g_dram), writes=[("gg", tag)], dma=True)
    P.op("scalar", lambda e: e.activation(out=sc[:], in_=cc[:], func=AF.Silu),
         reads=[("cc", tag)], writes=[("sc", tag)])
    wv = w_dram.rearrange("(kc p) n -> p kc n", p=128)
    for blk in range(12):
        slot = c.wslot_n % 2
        c.wslot_n += 1
        wt = wslots[slot]
        wtv = wt[:, 0:8192].rearrange("p (kc n) -> p kc n", kc=16)
        for half in range(2):
            P.op("gpsimd",
                 lambda e, blk=blk, wtv=wtv, half=half: e.dma_start(
                     out=wtv[:, 8 * half:8 * half + 8, :],
                     in_=wv[:, 8 * half:8 * half + 8, blk * 512:(blk + 1) * 512]),
                 writes=[("w", slot, half)], dma=True)
        for j4 in range(4):
            cb = blk * 4 + j4
            for kc in range(DC):
                P.op("tensor",
                     lambda e, kc=kc, cb=cb, j4=j4, wtv=wtv: e.matmul(
                         psum_t[:, cb:cb + 1], lhsT=wtv[:, kc, j4 * 128:(j4 + 1) * 128],
                         rhs=sc[:, kc:kc + 1], start=(kc == 0), stop=(kc == DC - 1)),
                     reads=[("w", slot, kc // 8), ("sc", tag)], writes=[pkey])
    P.op("vector", lambda e: e.tensor_tensor(out=mod[:], in0=psum_t[:, 0:48], in1=bb[:], op=ALU.add),
         reads=[pkey, ("bb", tag)], writes=[("mod", tag)])
    P.op("vector", lambda e: e.scalar_tensor_tensor(
        out=A[:], in0=mod[:, 16:32], scalar=1.0, in1=gg[:], op0=ALU.add, op1=ALU.mult),
        reads=[("mod", tag), ("gg", tag)], writes=[("A", tag)])
    return A, mod, ("A", tag), ("mod", tag)


def emit_norm_mod(c, P, x_t, xkey, T, A, Akey, S_ap_fn, Skey, hn, hnkey, sq, sqkey, rstd, ps_bank, pskey):
    ones = c.ones
    for kc in range(DC):
        P.op("scalar", lambda e, kc=kc: e.activation(out=sq[:, kc, :T], in_=x_t[:, kc, :T], func=AF.Square),
             reads=[xkey], writes=[(sqkey, kc)])
    for kc in range(DC):
        P.op("tensor", lambda e, kc=kc: e.matmul(ps_bank[:, :T], lhsT=ones[:], rhs=sq[:, kc, :T],
                                                 start=(kc == 0), stop=(kc == DC - 1)),
             reads=[(sqkey, kc)], writes=[pskey])
    P.op("scalar", lambda e: e.activation(out=rstd[:, :T], in_=ps_bank[:, :T], func=AF.Sqrt,
                                          scale=1.0 / D, bias=c.eps_col[:]),
         reads=[pskey], writes=["rstd"])
    P.op("vector", lambda e: e.reciprocal(out=rstd[:, :T], in_=rstd[:, :T]), reads=["rstd"], writes=["rstd"])
    for kc in range(DC):
        P.op("vector", lambda e, kc=kc: e.tensor_tensor(out=c.ntmp[:, kc % 2, :T], in0=x_t[:, kc, :T],
                                                        in1=rstd[:, :T], op=ALU.mult),
             reads=[xkey, "rstd"], writes=[("ntmp", kc % 2)])
        P.op("scalar", lambda e, kc=kc: e.activation(out=hn[:, kc, :T], in_=c.ntmp[:, kc % 2, :T],
                                                     func=AF.Identity, scale=A[:, kc:kc + 1],
                                                     bias=S_ap_fn(kc)),
             reads=[("ntmp", kc % 2), Akey, Skey], writes=[(hnkey, kc)])


def common_setup(c):
    nc = c.nc
    c.ones = sb(c, "ones", [128, 128], BF16)
    c.eps_col = sb(c, "eps_col", [128, 1], F32)
    c.ntmp = sb(c, "ntmp", [128, 2, 512], F32)
    c.P.op("vector", lambda e: e.memset(c.ones[:], 1.0), writes=["ones"])
    c.P.op("vector", lambda e: e.memset(c.eps_col[:], EPS), writes=["eps"])


def build_ffn(ntok, final_norm=False, T=512, env=None):
    nc = mk_nc(env)
    dt_in = mk_in(nc, env)
    WQ = "sync" if (env is not None and getattr(env, "bf16w", False)) else "gpsimd"
    xin = dt_in("xT", [D, ntok])
    ccol = dt_in("ccol", [128, DC])
    ada_w = dt_in("ada_w", [D, 3 * D])
    ada_b = dt_in("ada_b", [128, 48])
    ng = dt_in("ng", [128, DC])
    w_in = dt_in("w_in", [D, 2 * FFN_H])
    w_out = dt_in("w_out", [FFN_H, D])
    fg = dt_in("fg", [128, DC])
    xout = mk_out(nc, env, "yT", [D, ntok])
    with ExitStack() as es:
        c = Ctx()
        c.nc, c.es = nc, es
        c.pfx = env.pfx if env is not None else ""
        P = Prog(nc, es, env.G if env is not None else None)
        c.P = P
        c.wslot_n = 0
        common_setup(c)
        wslots = [sb(c, f"wslot{i}", [128, 16384], BF16) for i in range(2)]
        x_t = sb(c, "x_t", [128, DC, T], F32)
        hn = sb(c, "hn", [128, DC, T], BF16)
        hT = sb(c, "hT", [128, HC, T], BF16)
        rstd = sb(c, "rstd", [128, T], F32)
        sg = sb(c, "sg", [128, 2, T], F32)
        fgt = sb(c, "fgt", [128, DC], F32)
        pbanks = [ps(c, f"pb{i}", [128, 512]) for i in range(8)]
        A, mod, Akey, modkey = emit_adaln(c, P, "f", ccol, ada_w, ada_b, ng, wslots, pbanks[7])
        if final_norm:
            P.op("sync", lambda e: e.dma_start(out=fgt[:], in_=fg), writes=["fgt"], dma=True)
        xv = xin.rearrange("(kc p) t -> p kc t", p=128)
        ov = xout.rearrange("(kc p) t -> p kc t", p=128)
        w_in_v = w_in.rearrange("(kc p) n -> p kc n", p=128)
        w_out_v = w_out.rearrange("(hc p) n -> p hc n", p=128)
        ntile = ntok // T
        out_toks = []
        for ti in range(ntile):
            t0 = ti * T
            for q in range(4):
                P.op("sync", lambda e, q=q, t0=t0: e.dma_start(out=x_t[:, 4 * q:4 * q + 4, :],
                                                             in_=xv[:, 4 * q:4 * q + 4, t0:t0 + T]),
                     writes=["x"], dma=True)
            emit_norm_mod(c, P, x_t, "x", T, A, Akey, lambda kc: mod[:, kc:kc + 1], modkey,
                          hn, "hn", hT, "hT", rstd, pbanks[6], ("psum", 6))
            nblk = FFN_H // 512
            for hb in range(nblk):
                slot_g = c.wslot_n % 2
                c.wslot_n += 1
                wg = wslots[slot_g]
                wgv = wg[:, 0:8192].rearrange("p (kc n) -> p kc n", kc=16)
                wuv = wg[:, 8192:16384].rearrange("p (kc n) -> p kc n", kc=16)
                for half in range(2):
                    P.op(WQ, lambda e, hb=hb, wgv=wgv, half=half: e.dma_start(
                        out=wgv[:, 8 * half:8 * half + 8, :],
                        in_=w_in_v[:, 8 * half:8 * half + 8, hb * 512:(hb + 1) * 512]),
                        writes=[("w", slot_g, 0)], dma=True)
                for half in range(2):
                    P.op(WQ, lambda e, hb=hb, wuv=wuv, half=half: e.dma_start(
                        out=wuv[:, 8 * half:8 * half + 8, :],
                        in_=w_in_v[:, 8 * half:8 * half + 8, FFN_H + hb * 512:FFN_H + (hb + 1) * 512]),
                        writes=[("w", slot_g, 1)], dma=True)
                for j in range(4):
                    hc = hb * 4 + j
                    pg = pbanks[(hc % 2) * 2]
                    pu = pbanks[(hc % 2) * 2 + 1]
                    kg = ("psum", (hc % 2) * 2)
                    ku = ("psum", (hc % 2) * 2 + 1)
                    for kc in range(DC):
                        P.op("tensor", lambda e, kc=kc, j=j, pg=pg, wgv=wgv: e.matmul(
                            pg[:, :T], lhsT=wgv[:, kc, j * 128:(j + 1) * 128], rhs=hn[:, kc, :T],
                            start=(kc == 0), stop=(kc == DC - 1)),
                            reads=[("w", slot_g, 0), ("hn", kc)], writes=[kg])
                    for kc in range(DC):
                        P.op("tensor", lambda e, kc=kc, j=j, pu=pu, wuv=wuv: e.matmul(
                            pu[:, :T], lhsT=wuv[:, kc, j * 128:(j + 1) * 128], rhs=hn[:, kc, :T],
                            start=(kc == 0), stop=(kc == DC - 1)),
                            reads=[("w", slot_g, 1), ("hn", kc)], writes=[ku])
                    P.op("scalar", lambda e, pg=pg, hc=hc: e.activation(out=sg[:, hc % 2, :T], in_=pg[:, :T],
                                                                       func=AF.Silu),
                         reads=[kg], writes=[("sg", hc % 2)])
                    P.op("vector", lambda e, pu=pu, hc=hc: e.tensor_tensor(out=hT[:, hc, :T], in0=pu[:, :T],
                                                                          in1=sg[:, hc % 2, :T], op=ALU.mult),
                         reads=[ku, ("sg", hc % 2)], writes=[("hT", hc)])
            for ob in range(D // 256):
                slot = c.wslot_n % 2
                c.wslot_n += 1
                wt = wslots[slot]
                wov = wt[:, 0:HC * 256].rearrange("p (hc n) -> p hc n", hc=HC)
                for half in range(2):
                    P.op(WQ, lambda e, ob=ob, wov=wov, half=half: e.dma_start(
                        out=wov[:, 22 * half:22 * half + 22, :],
                        in_=w_out_v[:, 22 * half:22 * half + 22, ob * 256:(ob + 1) * 256]),
                        writes=[("w", slot, half)], dma=True)
                for j in range(2):
                    dc = ob * 2 + j
                    pb = pbanks[4 + dc % 2]
                    kb = ("psum", 4 + dc % 2)
                    for hc in range(HC):
                        P.op("tensor", lambda e, hc=hc, j=j, pb=pb, wov=wov: e.matmul(
                            pb[:, :T], lhsT=wov[:, hc, j * 128:(j + 1) * 128], rhs=hT[:, hc, :T],
                            start=(hc == 0), stop=(hc == HC - 1)),
                            reads=[("w", slot, hc // 22), ("hT", hc)], writes=[kb])
                    P.op("vector", lambda e, dc=dc, pb=pb: e.scalar_tensor_tensor(
                        out=x_t[:, dc, :T], in0=pb[:, :T], scalar=mod[:, 32 + dc:33 + dc], in1=x_t[:, dc, :T],
                        op0=ALU.mult, op1=ALU.add),
                        reads=[kb, modkey, "x"], writes=[("xo", dc)])
            src = x_t
            rk = [("xo", dc) for dc in range(DC)]
            if final_norm:
                for kc in range(DC):
                    P.op("scalar", lambda e, kc=kc: e.activation(out=hT[:, kc, :T], in_=x_t[:, kc, :T],
                                                                 func=AF.Square),
                         reads=[("xo", kc)], writes=[("hT", kc)])
                for kc in range(DC):
                    P.op("tensor", lambda e, kc=kc: e.matmul(pbanks[6][:, :T], lhsT=c.ones[:], rhs=hT[:, kc, :T],
                                                             start=(kc == 0), stop=(kc == DC - 1)),
                         reads=[("hT", kc)], writes=[("psum", 6)])
                P.op("scalar", lambda e: e.activation(out=rstd[:, :T], in_=pbanks[6][:, :T], func=AF.Sqrt,
                                                      scale=1.0 / D, bias=c.eps_col[:]),
                     reads=[("psum", 6)], writes=["rstd"])
                P.op("vector", lambda e: e.reciprocal(out=rstd[:, :T], in_=rstd[:, :T]),
                     reads=["rstd"], writes=["rstd"])
                for kc in range(DC):
                    P.op("vector", lambda e, kc=kc: e.scalar_tensor_tensor(
                        out=x_t[:, kc, :T], in0=x_t[:, kc, :T], scalar=fgt[:, kc:kc + 1], in1=rstd[:, :T],
                        op0=ALU.mult, op1=ALU.mult),
                        reads=[("xo", kc), "rstd", "fgt"], writes=[("xo", kc)])
            for q in range(4):
                tok = P.op("sync", lambda e, q=q, t0=t0: e.dma_start(out=ov[:, 4 * q:4 * q + 4, t0:t0 + T],
                                                                   in_=x_t[:, 4 * q:4 * q + 4, :]),
                           reads=[("xo", dc) for dc in range(4 * q, 4 * q + 4)] + ["x"], writes=["x_st"], dma=True)
                out_toks.append(tok)
        P.finalize(final_waits=out_toks[-8:])
    return nc


def col_layout(v):
    v = np.asarray(v, dtype=np.float32)
    return np.ascontiguousarray(v.reshape(-1, 128).T)


RET_H = 8
import math
LOGG = [math.log(1.0 - 2.0 ** (-5.0 - h)) for h in range(RET_H)]


def ret_consts():
    J = np.arange(128)[:, None]
    I = np.arange(128)[None, :]
    E = np.abs(I - J).astype(np.float32)
    M01 = np.ones((128, 128), np.float32)
    M01[64:, :64] = 0.0
    return {
        "ident": np.eye(128, dtype=np.float32),
        "Emat": E, "M01": M01,
        "irow1": np.broadcast_to((np.arange(128) + 1).astype(np.float32), (128, 128)).copy(),
        "jcol": np.repeat((127 - np.arange(128)).astype(np.float32).reshape(128, 1), 16, axis=1),
        "pidx": np.repeat(np.arange(128, dtype=np.float32).reshape(128, 1), 16, axis=1),
        "iota512": np.broadcast_to(np.arange(512, dtype=np.float32), (128, 512)).copy(),
    }


def build_ret(ntok, T=256, dbg=9, env=None):
    nc = mk_nc(env)
    dt_in = mk_in(nc, env)
    WQ = "sync" if (env is not None and getattr(env, "bf16w", False)) else "gpsimd"
    xin = dt_in("xT", [D, ntok])
    ccol = dt_in("ccol", [128, DC])
    ada_w = dt_in("ada_w", [D, 3 * D])
    ada_b = dt_in("ada_b", [128, 48])
    ng = dt_in("ng", [128, DC])
    w_in = dt_in("w_in", [D, 12288])
    w_out = dt_in("w_out", [4096, D])
    gng = dt_in("gng", [1, 4096])
    ident_d = dt_in("ident", [128, 128])
    E_d = dt_in("Emat", [128, 128])
    M01_d = dt_in("M01", [128, 128])
    irow1_d = dt_in("irow1", [128, 128])
    jcol_d = dt_in("jcol", [128, 16])
    pidx_d = dt_in("pidx", [128, 16])
    iota_d = dt_in("iota512", [128, 512])
    xout = mk_out(nc, env, "yT", [D, ntok])
    NS = T // 128
    with ExitStack() as es:
        c = Ctx()
        c.nc, c.es = nc, es
        c.pfx = env.pfx if env is not None else ""
        P = Prog(nc, es, env.G if env is not None else None)
        c.P = P
        c.wslot_n = 0
        common_setup(c)
        wslots = [sb(c, f"wslot{i}", [128, 8192], BF16) for i in range(3)]
        x_t = sb(c, "x_t", [128, DC, T], F32)
        yT = sb(c, "yT_sb", [128, 32, T], BF16)
        hn = sb(c, "hn", [128, DC, T], BF16)
        rstd = sb(c, "rstd", [128, T], F32)
        r32 = sb(c, "r32", [128, RET_H * 2, 512], F32)
        rbf = sb(c, "rbf", [128, RET_H * 2, 512], BF16)
        ident32 = sb(c, "ident32", [128, 128], F32)
        ident = sb(c, "ident", [128, 128], BF16)
        Et = sb(c, "Et", [128, 128], F32)
        M01 = sb(c, "M01", [128, 128], F32)
        irow1 = sb(c, "irow1", [128, 128], F32)
        jcol = sb(c, "jcol", [128, 16], F32)
        pidx = sb(c, "pidx", [128, 16], F32)
        iota = sb(c, "iota", [128, 512], F32)
        invf = sb(c, "invf", [128, 1], F32)
        pi_col = sb(c, "pi_col", [128, 1], F32)
        maskT = sb(c, "maskT", [128, RET_H, 128], F32)
        xi = sb(c, "xi", [128, RET_H, 128], F32)
        zeta = sb(c, "zeta", [128, RET_H], F32)
        gngt = sb(c, "gngt", [128, 4096], F32)
        cosT = sb(c, "cosT", [128, T], F32)
        sinT = sb(c, "sinT", [128, T], F32)
        ang = sb(c, "ang", [128, 2, T], F32)
        rt = sb(c, "rt", [128, 4, T], F32)
        qT = sb(c, "qT", [128, 2, T], BF16)
        kT = sb(c, "kT", [128, 2, T], BF16)
        qxT = sb(c, "qxT", [128, 2, T], BF16)
        ktok = sb(c, "ktok", [128, NS, 256], BF16)
        vtok = sb(c, "vtok", [128, NS, 512], BF16)
        gsil = sb(c, "gsil", [128, NS, 512], BF16)
        sm2 = [sb(c, "sm_" + str(_q), [128, 128], BF16) for _q in range(2)]
        osb2 = [sb(c, "osb_" + str(_q), [128, 512], F32) for _q in range(2)]
        osq2 = [sb(c, "osq_" + str(_q), [128, 512], F32) for _q in range(2)]
        y12 = [sb(c, "y1_" + str(_q), [128, 512], F32) for _q in range(2)]
        y22 = [sb(c, "y2_" + str(_q), [128, 512], F32) for _q in range(2)]
        y32 = [sb(c, "y3_" + str(_q), [128, 512], BF16) for _q in range(2)]
        st2 = [sb(c, "st_" + str(_q), [128, 8], F32) for _q in range(2)]
        xres = sb(c, "xres", [128, 2, T], F32)
        pb = [ps(c, f"pb{i}", [128, 512]) for i in range(3)]
        pTs = [ps(c, f"pT{i}", [128, 1024], BF16) for i in range(2)]
        po = ps(c, "po", [128, 512])
        pst = [ps(c, f"pst{i}", [128, 512]) for i in range(2)]
        A, mod, Akey, modkey = emit_adaln(c, P, "m", ccol, ada_w, ada_b, ng, wslots, pb[2], ("ps", 2))
        for (t_, d_, k_) in [(ident32, ident_d, "ident32"), (Et, E_d, "Et"), (M01, M01_d, "M01"),
                             (irow1, irow1_d, "irow1"), (jcol, jcol_d, "jcol"), (pidx, pidx_d, "pidx"),
                             (iota, iota_d, "iota")] if dbg >= -4 else []:
            P.op("sync", lambda e, t_=t_, d_=d_: e.dma_start(out=t_[:], in_=d_), writes=[k_], dma=True)
        for q in range(4 if dbg != 0 else 0):
            P.op("sync", lambda e, q=q: e.dma_start(out=gngt[:, q * 1024:(q + 1) * 1024],
                                                   in_=gng[0:1, q * 1024:(q + 1) * 1024].partition_broadcast(128)),
                 writes=[("gngt", q)], dma=True)
        if dbg >= -4:
            P.op("vector", lambda e: e.tensor_copy(out=ident[:], in_=ident32[:]), reads=["ident32"], writes=["ident"])
        if dbg >= -1:
            P.op("vector", lambda e: e.memset(pi_col[:], math.pi), writes=["pi"])
            P.op("vector", lambda e: e.memset(r32[:], 0.0), writes=["r32"])
            P.op("vector", lambda e: e.memset(rbf[:], 0.0), writes=["rbf"])
        if dbg >= -2:
            P.op("scalar", lambda e: e.activation(out=invf[:], in_=pidx[:, 0:1], func=AF.Exp,
                                              scale=-math.log(10000.0) / 128.0),
                 reads=["pidx"], writes=["invf"])
        for h in range(RET_H if dbg >= -3 else 0):
            P.op("scalar", lambda e, h=h: e.activation(out=maskT[:, h, :], in_=Et[:], func=AF.Exp, scale=LOGG[h]),
                 reads=["Et"], writes=[("mask", h)])
            P.op("vector", lambda e, h=h: e.tensor_tensor(out=maskT[:, h, :], in0=maskT[:, h, :], in1=M01[:],
                                                          op=ALU.mult),
                 reads=[("mask", h), "M01"], writes=[("mask", h)])
            P.op("scalar", lambda e, h=h: e.activation(out=xi[:, h, :], in_=irow1[:], func=AF.Exp, scale=LOGG[h]),
                 reads=["irow1"], writes=[("xi", h)])
            P.op("scalar", lambda e, h=h: e.activation(out=zeta[:, h:h + 1], in_=jcol[:, 0:1], func=AF.Exp,
                                                       scale=LOGG[h]),
                 reads=["jcol"], writes=[("zeta", h)])
        xv = xin.rearrange("(kc p) t -> p kc t", p=128)
        ov = xout.rearrange("(kc p) t -> p kc t", p=128)
        w_in_v = w_in.rearrange("(kc p) n -> p kc n", p=128)
        w_out_v = w_out.rearrange("(mc p) n -> p mc n", p=128)
        out_toks = []
        c.chain_n = 0
        TWO_PI = 2.0 * math.pi

        def load_w(c0, ncols, key_extra):
            slot = c.wslot_n % 3
            c.wslot_n += 1
            wt = wslots[slot]
            v = wt[:, 0:16 * ncols].rearrange("p (kc n) -> p kc n", kc=16)
            for half in range(2):
                P.op(WQ, lambda e, v=v, half=half, c0=c0, ncols=ncols: e.dma_start(
                    out=v[:, 8 * half:8 * half + 8, :], in_=w_in_v[:, 8 * half:8 * half + 8, c0:c0 + ncols]),
                    writes=[("w", slot, half)], dma=True)
            return v, slot

        for ti in range(ntok // T):
            t0 = ti * T
            for q in range(4):
                P.op("sync", lambda e, q=q, t0=t0: e.dma_start(out=x_t[:, 4 * q:4 * q + 4, :],
                                                             in_=xv[:, 4 * q:4 * q + 4, t0:t0 + T]),
                     writes=["x"], dma=True)
            emit_norm_mod(c, P, x_t, "x", T, A, Akey, lambda kc: mod[:, kc:kc + 1], modkey,
                          hn, "hn", yT, "yT", rstd, pb[2], ("ps", 2))
            if dbg >= 1:
              P.op("vector", lambda e, t0=t0: e.tensor_scalar(out=ang[:, 0, :], in0=iota[:, :T], scalar1=float(t0),
                                                            scalar2=invf[:, 0:1], op0=ALU.add, op1=ALU.mult),
                 reads=["iota", "invf"], writes=[("ang", 0)])
            if dbg >= 1:
              P.op("vector", lambda e: e.tensor_scalar(out=ang[:, 1, :], in0=ang[:, 0, :], scalar1=0.5 * math.pi,
                                                     scalar2=None, op0=ALU.add),
                 reads=[("ang", 0)], writes=[("ang", 1)])
            MAGIC = 12582912.0
            for w_, dst_, dk_ in [(0, sinT, "sinT"), (1, cosT, "cosT")] if dbg >= 1 else []:
                P.op("vector", lambda e, w_=w_: e.tensor_scalar(out=rt[:, w_, :], in0=ang[:, w_, :],
                                                                scalar1=1.0 / TWO_PI, scalar2=MAGIC,
                                                                op0=ALU.mult, op1=ALU.add),
                     reads=[("ang", w_)], writes=[("rt", w_)])
                P.op("vector", lambda e, w_=w_: e.tensor_scalar(out=rt[:, w_, :], in0=rt[:, w_, :],
                                                                scalar1=MAGIC, scalar2=None, op0=ALU.subtract),
                     reads=[("rt", w_)], writes=[("rt", w_)])
                P.op("vector", lambda e, w_=w_: e.scalar_tensor_tensor(out=rt[:, w_, :], in0=rt[:, w_, :],
                                                                       scalar=-TWO_PI, in1=ang[:, w_, :],
                                                                       op0=ALU.mult, op1=ALU.add),
                     reads=[("rt", w_), ("ang", w_)], writes=[("rt", w_)])
                P.op("vector", lambda e, w_=w_: e.tensor_scalar(out=rt[:, w_, :], in0=rt[:, w_, :],
                                                                scalar1=-math.pi, scalar2=math.pi,
                                                                op0=ALU.max, op1=ALU.min),
                     reads=[("rt", w_)], writes=[("rt", w_)])
                P.op("scalar", lambda e, w_=w_, dst_=dst_: e.activation(out=dst_[:], in_=rt[:, w_, :], func=AF.Sin),
                     reads=[("rt", w_)], writes=[dk_])
            hnkeys = [("hn", kc) for kc in range(DC)]
            for h in range(RET_H if dbg >= 2 else 0):
                wqk, sqk = load_w(h * 256, 256, None)
                wk_, sk_ = load_w(2048 + h * 256, 256, None)
                for which, (wv_, sl_) in enumerate([(wqk, sqk), (wk_, sk_)]):
                    for cch in range(2):
                        bank = pb[cch]
                        for kc in range(DC):
                            P.op("tensor", lambda e, kc=kc, cch=cch, wv_=wv_, bank=bank: e.matmul(
                                bank[:, :T], lhsT=wv_[:, kc, cch * 128:(cch + 1) * 128], rhs=hn[:, kc, :T],
                                start=(kc == 0), stop=(kc == DC - 1)),
                                reads=[("w", sl_, kc // 8), ("hn", kc)], writes=[("ps", cch)])
                    scl = 1.0 if which == 0 else 1.0 / 16.0
                    dst = qT if which == 0 else kT
                    dkey = "qT" if which == 0 else "kT"
                    P.op("vector", lambda e, scl=scl: e.scalar_tensor_tensor(
                        out=rt[:, 0, :], in0=pb[0][:, :T], scalar=scl, in1=cosT[:], op0=ALU.mult, op1=ALU.mult),
                        reads=[("ps", 0), "cosT"], writes=[("rt", 0)])
                    P.op("vector", lambda e, scl=scl: e.scalar_tensor_tensor(
                        out=rt[:, 1, :], in0=pb[1][:, :T], scalar=scl, in1=sinT[:], op0=ALU.mult, op1=ALU.mult),
                        reads=[("ps", 1), "sinT"], writes=[("rt", 1)])
                    P.op("vector", lambda e, scl=scl: e.scalar_tensor_tensor(
                        out=rt[:, 2, :], in0=pb[0][:, :T], scalar=scl, in1=sinT[:], op0=ALU.mult, op1=ALU.mult),
                        reads=[("ps", 0), "sinT"], writes=[("rt", 2)])
                    P.op("vector", lambda e, scl=scl: e.scalar_tensor_tensor(
                        out=rt[:, 3, :], in0=pb[1][:, :T], scalar=scl, in1=cosT[:], op0=ALU.mult, op1=ALU.mult),
                        reads=[("ps", 1), "cosT"], writes=[("rt", 3)])
                    P.op("gpsimd", lambda e, dst=dst: e.tensor_tensor(out=dst[:, 0, :], in0=rt[:, 0, :],
                                                                      in1=rt[:, 1, :], op=ALU.subtract),
                         reads=[("rt", 0), ("rt", 1)], writes=[(dkey, 0)])
                    P.op("gpsimd", lambda e, dst=dst: e.tensor_tensor(out=dst[:, 1, :], in0=rt[:, 2, :],
                                                                      in1=rt[:, 3, :], op=ALU.add),
                         reads=[("rt", 2), ("rt", 3)], writes=[(dkey, 1)])
                for cch in range(2):
                    P.op("gpsimd", lambda e, cch=cch, h=h: e.tensor_tensor(
                        out=qxT[:, cch, :].rearrange("p (s i) -> p s i", s=NS),
                        in0=qT[:, cch, :].rearrange("p (s i) -> p s i", s=NS),
                        in1=xi[:, h:h + 1, :].broadcast_to([128, NS, 128]), op=ALU.mult),
                        reads=[("qT", cch), ("xi", h)], writes=[("qxT", cch)])
                for s in range(NS):
                    for cch in range(2):
                        P.op("tensor", lambda e, s=s, cch=cch: e.transpose(
                            pTs[cch][:, 0:128], kT[:, cch, s * 128:(s + 1) * 128], ident[:]),
                            reads=[("kT", cch), "ident"], writes=[("pT", cch)])
                        P.op("scalar", lambda e, s=s, cch=cch, h=h: e.activation(
                            out=ktok[:, s, cch * 128:(cch + 1) * 128], in_=pTs[cch][:, 0:128], func=AF.Copy,
                            scale=zeta[:, h:h + 1]),
                            reads=[("pT", cch), ("zeta", h)], writes=[("ktok", s)])
                if dbg < 3:
                    continue
                wv2, sv2 = load_w(4096 + h * 512, 512, None)
                wg2, sg2 = load_w(8192 + h * 512, 512, None)
                for s in range(NS):
                    bank = pb[2]
                    for kc in range(DC):
                        P.op("tensor", lambda e, kc=kc, s=s, bank=bank, wv2=wv2: e.matmul(
                            bank[:, :], lhsT=hn[:, kc, s * 128:(s + 1) * 128], rhs=wv2[:, kc, :],
                            start=(kc == 0), stop=(kc == DC - 1)),
                            reads=[("w", sv2, kc // 8), ("hn", kc)], writes=[("ps", 2)])
                    P.op("scalar", lambda e, s=s, bank=bank: e.copy(out=vtok[:, s, :], in_=bank[:, :]),
                         reads=[("ps", 2)], writes=[("vtok", s)])
                for s in range(NS):
                    bank = pb[2]
                    for kc in range(DC):
                        P.op("tensor", lambda e, kc=kc, s=s, bank=bank, wg2=wg2: e.matmul(
                            bank[:, :], lhsT=hn[:, kc, s * 128:(s + 1) * 128], rhs=wg2[:, kc, :],
                            start=(kc == 0), stop=(kc == DC - 1)),
                            reads=[("w", sg2, kc // 8), ("hn", kc)], writes=[("ps", 2)])
                    P.op("scalar", lambda e, s=s, bank=bank: e.activation(out=gsil[:, s, :], in_=bank[:, :],
                                                                         func=AF.Silu),
                         reads=[("ps", 2)], writes=[("gsil", s)])
                for s in range(NS if dbg >= 4 else 0):
                    sl = slice(s * 128, (s + 1) * 128)
                    cq = c.chain_n % 2
                    c.chain_n += 1
                    sm, osb, osq, y1, y2, y3, st = sm2[cq], osb2[cq], osq2[cq], y12[cq], y22[cq], y32[cq], st2[cq]
                    for cch in range(2):
                        P.op("tensor", lambda e, cch=cch, sl=sl, sm=sm, osb=osb, osq=osq, y1=y1, y2=y2, y3=y3, st=st: e.matmul(
                            pb[0][:, 0:128], lhsT=kT[:, cch, sl], rhs=qT[:, cch, sl],
                            start=(cch == 0), stop=(cch == 1)),
                            reads=[("kT", cch), ("qT", cch)], writes=[("ps", 0)])
                    P.op("vector", lambda e, h=h, sm=sm, osb=osb, osq=osq, y1=y1, y2=y2, y3=y3, st=st: e.tensor_tensor(out=sm[:], in0=pb[0][:, 0:128],
                                                                  in1=maskT[:, h, :], op=ALU.mult),
                         reads=[("ps", 0), ("mask", h)], writes=[("sm", cq)])
                    P.op("tensor", lambda e, s=s, sm=sm, osb=osb, osq=osq, y1=y1, y2=y2, y3=y3, st=st: e.matmul(po[:, :], lhsT=sm[:], rhs=vtok[:, s, :],
                                                          start=True, stop=False),
                         reads=[("sm", cq), ("vtok", s)], writes=["po"])
                    for cch in range(2):
                        P.op("tensor", lambda e, cch=cch, sl=sl, h=h, sm=sm, osb=osb, osq=osq, y1=y1, y2=y2, y3=y3, st=st: e.matmul(
                            po[:, :], lhsT=qxT[:, cch, sl], rhs=rbf[:, h * 2 + cch, :],
                            start=False, stop=(cch == 1)),
                            reads=[("qxT", cch), ("rbf", h, cch)], writes=["po"])
                    for cch in range(2):
                        P.op("tensor", lambda e, cch=cch, s=s, sm=sm, osb=osb, osq=osq, y1=y1, y2=y2, y3=y3, st=st: e.matmul(
                            pst[cch][:, :], lhsT=ktok[:, s, cch * 128:(cch + 1) * 128], rhs=vtok[:, s, :],
                            start=True, stop=True),
                            reads=[("ktok", s), ("vtok", s)], writes=[("pst", cch)])
                        P.op("vector", lambda e, cch=cch, h=h, sm=sm, osb=osb, osq=osq, y1=y1, y2=y2, y3=y3, st=st: e.scalar_tensor_tensor(
                            out=r32[:, h * 2 + cch, :], in0=r32[:, h * 2 + cch, :], scalar=math.exp(LOGG[h] * 128),
                            in1=pst[cch][:, :], op0=ALU.mult, op1=ALU.add),
                            reads=[("pst", cch), ("r32", h, cch), "r32"], writes=[("r32", h, cch)])
                        P.op("scalar", lambda e, cch=cch, h=h, sm=sm, osb=osb, osq=osq, y1=y1, y2=y2, y3=y3, st=st: e.copy(out=rbf[:, h * 2 + cch, :],
                                                                    in_=r32[:, h * 2 + cch, :]),
                             reads=[("r32", h, cch), "rbf"], writes=[("rbf", h, cch)])
                    P.op("scalar", lambda e, sm=sm, osb=osb, osq=osq, y1=y1, y2=y2, y3=y3, st=st: e.copy(out=osb[:], in_=po[:, :]), reads=["po"], writes=[("osb", cq)])
                    P.op("vector", lambda e, sm=sm, osb=osb, osq=osq, y1=y1, y2=y2, y3=y3, st=st: e.reduce_sum(out=st[:, 0:1], in_=osb[:], axis=AX.X),
                         reads=[("osb", cq)], writes=[("st", cq, 0)])
                    P.op("scalar", lambda e, sm=sm, osb=osb, osq=osq, y1=y1, y2=y2, y3=y3, st=st: e.activation(out=osq[:], in_=osb[:], func=AF.Square),
                         reads=[("osb", cq)], writes=[("osq", cq)])
                    P.op("vector", lambda e, sm=sm, osb=osb, osq=osq, y1=y1, y2=y2, y3=y3, st=st: e.reduce_sum(out=st[:, 1:2], in_=osq[:], axis=AX.X),
                         reads=[("osq", cq)], writes=[("st", cq, 1)])
                    P.op("vector", lambda e, sm=sm, osb=osb, osq=osq, y1=y1, y2=y2, y3=y3, st=st: e.tensor_scalar(out=st[:, 2:3], in0=st[:, 0:1], scalar1=1.0 / 512,
                                                             scalar2=None, op0=ALU.mult),
                         reads=[("st", cq, 0)], writes=[("st", cq, 2)])
                    P.op("vector", lambda e, sm=sm, osb=osb, osq=osq, y1=y1, y2=y2, y3=y3, st=st: e.tensor_tensor(out=st[:, 3:4], in0=st[:, 2:3], in1=st[:, 2:3],
                                                             op=ALU.mult),
                         reads=[("st", cq, 2)], writes=[("st", cq, 3)])
                    P.op("vector", lambda e, sm=sm, osb=osb, osq=osq, y1=y1, y2=y2, y3=y3, st=st: e.scalar_tensor_tensor(out=st[:, 4:5], in0=st[:, 1:2],
                                                                    scalar=1.0 / 512, in1=st[:, 3:4],
                                                                    op0=ALU.mult, op1=ALU.subtract),
                         reads=[("st", cq, 1), ("st", cq, 3)], writes=[("st", cq, 4)])
                    P.op("scalar", lambda e, sm=sm, osb=osb, osq=osq, y1=y1, y2=y2, y3=y3, st=st: e.activation(out=st[:, 5:6], in_=st[:, 4:5], func=AF.Sqrt,
                                                          bias=c.eps_col[:]),
                         reads=[("st", cq, 4), "eps"], writes=[("st", cq, 5)])
                    P.op("vector", lambda e, sm=sm, osb=osb, osq=osq, y1=y1, y2=y2, y3=y3, st=st: e.reciprocal(out=st[:, 6:7], in_=st[:, 5:6]),
                         reads=[("st", cq, 5)], writes=[("st", cq, 6)])
                    P.op("vector", lambda e, sm=sm, osb=osb, osq=osq, y1=y1, y2=y2, y3=y3, st=st: e.tensor_scalar(out=y1[:], in0=osb[:], scalar1=st[:, 2:3],
                                                             scalar2=st[:, 6:7], op0=ALU.subtract, op1=ALU.mult),
                         reads=[("osb", cq), ("st", cq, 2), ("st", cq, 6)], writes=[("y1", cq)])
                    P.op("gpsimd", lambda e, h=h, sm=sm, osb=osb, osq=osq, y1=y1, y2=y2, y3=y3, st=st: e.tensor_tensor(out=y2[:], in0=y1[:],
                                                                  in1=gngt[:, h * 512:(h + 1) * 512], op=ALU.mult),
                         reads=[("y1", cq), ("gngt", h // 2)], writes=[("y2", cq)])
                    P.op("gpsimd", lambda e, s=s, sm=sm, osb=osb, osq=osq, y1=y1, y2=y2, y3=y3, st=st: e.tensor_tensor(out=y3[:], in0=y2[:], in1=gsil[:, s, :],
                                                                  op=ALU.mult),
                         reads=[("y2", cq), ("gsil", s)], writes=[("y3", cq)])
                    for ec in range(4):
                        P.op("tensor", lambda e, ec=ec, sm=sm, osb=osb, osq=osq, y1=y1, y2=y2, y3=y3, st=st: e.transpose(pTs[ec % 2][:, 0:128],
                                                                     y3[:, ec * 128:(ec + 1) * 128], ident[:]),
                             reads=[("y3", cq), "ident"], writes=[("pT", ec % 2)])
                        P.op("vector", lambda e, ec=ec, h=h, sl=sl, sm=sm, osb=osb, osq=osq, y1=y1, y2=y2, y3=y3, st=st: e.tensor_copy(
                            out=yT[:, h * 4 + ec, sl], in_=pTs[ec % 2][:, 0:128]),
                            reads=[("pT", ec % 2)], writes=[("yT", h * 4 + ec)])
            if dbg < 5:
                for dc in range(DC):
                    P.op("vector", lambda e, dc=dc: e.tensor_copy(out=xres[:, dc % 2, :], in_=x_t[:, dc, :T]),
                         reads=["x"], writes=[("xres", dc % 2)])
                    tok = P.op("sync", lambda e, dc=dc, t0=t0: e.dma_start(out=ov[:, dc, t0:t0 + T],
                                                                         in_=xres[:, dc % 2, :]),
                               reads=[("xres", dc % 2)], writes=["x_st"], dma=True)
                    out_toks.append(tok)
            for ob in range(D // 256 if dbg >= 5 else 0):
                slot = c.wslot_n % 3
                c.wslot_n += 1
                wt = wslots[slot]
                wov = wt[:, 0:32 * 256].rearrange("p (mc n) -> p mc n", mc=32)
                for half in range(2):
                    P.op(WQ, lambda e, ob=ob, wov=wov, half=half: e.dma_start(
                        out=wov[:, 16 * half:16 * half + 16, :],
                        in_=w_out_v[:, 16 * half:16 * half + 16, ob * 256:(ob + 1) * 256]),
                        writes=[("w", slot, half)], dma=True)
                for j in range(2):
                    dc = ob * 2 + j
                    bank = pb[dc % 2]
                    for mc in range(32):
                        P.op("tensor", lambda e, mc=mc, j=j, bank=bank, wov=wov: e.matmul(
                            bank[:, :T], lhsT=wov[:, mc, j * 128:(j + 1) * 128], rhs=yT[:, mc, :T],
                            start=(mc == 0), stop=(mc == 31)),
                            reads=[("w", slot, mc // 16), ("yT", mc)], writes=[("ps", dc % 2)])
                    P.op("vector", lambda e, dc=dc, bank=bank: e.scalar_tensor_tensor(
                        out=xres[:, dc % 2, :], in0=bank[:, :T], scalar=mod[:, 32 + dc:33 + dc],
                        in1=x_t[:, dc, :T], op0=ALU.mult, op1=ALU.add),
                        reads=[("ps", dc % 2), modkey, "x"], writes=[("xres", dc % 2)])
                    tok = P.op("sync", lambda e, dc=dc, t0=t0: e.dma_start(out=ov[:, dc, t0:t0 + T],
                                                                         in_=xres[:, dc % 2, :]),
                               reads=[("xres", dc % 2)], writes=["x_st"], dma=True)
                    out_toks.append(tok)
        P.finalize(final_waits=out_toks[-8:])
    return nc


SSD_H = 48
IN0 = 9264


def ssd_consts():
    k = np.arange(128)[:, None]
    l = np.arange(128)[None, :]
    triU = (k <= l).astype(np.float32)
    negm = np.where(l >= k, 0.0, -30000.0).astype(np.float32)
    return {"ident": np.eye(128, dtype=np.float32), "triU": triU,
            "negm4": np.ascontiguousarray(np.tile(negm, (1, 4)))}


def bc3(ap, n_outer, n_inner):
    return ap.rearrange("p (h o) -> p h o", o=1).broadcast_to([128, n_outer, n_inner])


def build_ssd(ntok, T=256, use_barrier=False, max_tiles=None, env=None):
    nc = mk_nc(env)
    dt_in = mk_in(nc, env)
    WQ = "sync" if (env is not None and getattr(env, "bf16w", False)) else "gpsimd"
    xin = dt_in("xT", [D, ntok])
    ccol = dt_in("ccol", [128, DC])
    ada_w = dt_in("ada_w", [D, 3 * D])
    ada_b = dt_in("ada_b", [128, 48])
    ng = dt_in("ng", [128, DC])
    w_in = dt_in("w_in", [D, IN0])
    cw_d = dt_in("cw", [128, 40 * 4])
    cb_d = dt_in("cb", [128, 40])
    dtb_d = dt_in("dtb", [1, 48])
    alog_d = dt_in("alog", [1, 48])
    dsk_d = dt_in("dsk", [1, 48])
    sng_d = dt_in("sng", [128, 24])
    ident_d = dt_in("ident", [128, 128])
    triU_d = dt_in("triU", [128, 128])
    negm_d = dt_in("negm4", [128, 512])
    yout = mk_out(nc, env, "yT", [3072, ntok])
    NS = T // 128
    with ExitStack() as es:
        c = Ctx()
        c.nc, c.es = nc, es
        c.pfx = env.pfx if env is not None else ""
        P = Prog(nc, es, env.G if env is not None else None)
        c.P = P
        c.wslot_n = 0
        common_setup(c)
        wslots = [sb(c, f"wslot{i}", [128, 8192], BF16) for i in range(3)]
        x_t = sb(c, "x_t", [128, DC, T], F32)
        hn = sb(c, "hn", [128, DC, T], BF16)
        sq = sb(c, "sq", [128, DC, T], BF16)
        rstd = sb(c, "rstd", [128, T], F32)
        ya = sb(c, "ya", [128, 24, T], F32)
        ST32 = sb(c, "ST32", [128, 3072], F32)
        STbf = sb(c, "STbf", [128, 3072], BF16)
        xs_tok = sb(c, "xs_tok", [128, NS, 3072], BF16)
        zs_tok = sb(c, "zs_tok", [128, NS, 3072], BF16)
        BT = sb(c, "BT", [128, 8, T], BF16)
        CT = sb(c, "CT", [128, 8, T], BF16)
        Btok = sb(c, "Btok", [128, NS, 8, 128], BF16)
        xsT = sb(c, "xsT", [128, 2, T], BF16)
        hist = sb(c, "hist", [128, 40, 3], F32)
        pre = sb(c, "pre", [128, 2, T + 3], F32)
        acc = sb(c, "acc", [128, 2, T], F32)
        cw = sb(c, "cw", [128, 160], F32)
        cb = sb(c, "cb", [128, 40], F32)
        dtb = sb(c, "dtb", [128, 48], F32)
        abc = sb(c, "abc", [128, 48], F32)
        dsk = sb(c, "dsk", [128, 48], F32)
        sng = sb(c, "sng", [128, 24], F32)
        ident32 = sb(c, "ident32", [128, 128], F32)
        ident = sb(c, "identb", [128, 128], BF16)
        triU = sb(c, "triU", [128, 128], F32)
        negm = sb(c, "negm", [128, 512], F32)
        ones32 = sb(c, "ones32", [128, 128], F32)
        sm48 = sb(c, "sm48", [128, 10, 48], F32)
        R4 = sb(c, "R4", [128, 2, 512], F32)
        Eh = sb(c, "Eh", [128, 2, 128], F32)
        SL = sb(c, "SL", [128, 2, 6, 128], BF16)
        xdt = sb(c, "xdt", [128, 2, 384], BF16)
        xdd = sb(c, "xdd", [128, 2, 384], BF16)
        tmpa = sb(c, "tmpa", [128, 2, 384], F32)
        tmpb = sb(c, "tmpb", [128, 2, 384], F32)
        tmpc = sb(c, "tmpc", [128, 2, 384], F32)
        y3 = sb(c, "y3", [128, 2, 384], BF16)
        st = sb(c, "st", [128, 2, 4], F32)
        pb = [ps(c, f"pb{i}", [128, 512]) for i in range(7)]
        pT = ps(c, "pT", [128, 1024], BF16)
        A, mod, Akey, modkey = emit_adaln(c, P, "m", ccol, ada_w, ada_b, ng, wslots, pb[2], ("ps", 2))
        for (t_, d_, k_) in [(ident32, ident_d, "ident32"), (triU, triU_d, "triU"), (negm, negm_d, "negm"),
                             (cw, cw_d, "cw"), (cb, cb_d, "cb"), (sng, sng_d, "sng")]:
            P.op("sync", lambda e, t_=t_, d_=d_: e.dma_start(out=t_[:], in_=d_), writes=[k_], dma=True)
        for (t_, d_, k_) in [(dtb, dtb_d, "dtb"), (abc, alog_d, "abc"), (dsk, dsk_d, "dsk")]:
            P.op("sync", lambda e, t_=t_, d_=d_: e.dma_start(out=t_[:], in_=d_.partition_broadcast(128)),
                 writes=[k_], dma=True)
        P.op("vector", lambda e: e.tensor_copy(out=ident[:], in_=ident32[:]), reads=["ident32"], writes=["ident"])
        P.op("vector", lambda e: e.memset(ones32[:], 1.0), writes=["ones32"])
        P.op("vector", lambda e: e.memset(hist[:], 0.0), writes=["hist"])
        P.op("vector", lambda e: e.memset(ST32[:], 0.0), writes=["ST32"])
        P.op("vector", lambda e: e.memset(STbf[:], 0.0), writes=["STbf"])
        P.op("scalar", lambda e: e.activation(out=abc[:], in_=abc[:], func=AF.Exp), reads=["abc"], writes=["abc"])
        P.op("vector", lambda e: e.tensor_scalar(out=abc[:], in0=abc[:], scalar1=-1.0, scalar2=None, op0=ALU.mult),
             reads=["abc"], writes=["abc"])
        xv = xin.rearrange("(kc p) t -> p kc t", p=128)
        ov = yout.rearrange("(mc p) t -> p mc t", p=128)
        w_in_v = w_in.rearrange("(kc p) n -> p kc n", p=128)
        out_toks = []
        c.use_barrier = use_barrier

        def load_w(c0, ncols):
            slot = c.wslot_n % 3
            c.wslot_n += 1
            wt = wslots[slot]
            v = wt[:, 0:16 * ncols].rearrange("p (kc n) -> p kc n", kc=16)
            for half in range(2):
                P.op(WQ, lambda e, v=v, half=half, c0=c0, ncols=ncols: e.dma_start(
                    out=v[:, 8 * half:8 * half + 8, :], in_=w_in_v[:, 8 * half:8 * half + 8, c0:c0 + ncols]),
                    writes=[("w", slot, half)], dma=True)
            return v, slot

        def transpose_to(src_ap, dst_ap, rkeys, wkeys, scale=None, eng="scalar"):
            P.op("tensor", lambda e: e.transpose(pT[:, 0:128], src_ap, ident[:]),
                 reads=list(rkeys) + ["ident"], writes=["pT"])
            if scale is None:
                if eng == "scalar":
                    P.op("scalar", lambda e: e.copy(out=dst_ap, in_=pT[:, 0:128]), reads=["pT"], writes=list(wkeys))
                else:
                    P.op("vector", lambda e: e.tensor_copy(out=dst_ap, in_=pT[:, 0:128]), reads=["pT"],
                         writes=list(wkeys))
            else:
                P.op("scalar", lambda e: e.activation(out=dst_ap, in_=pT[:, 0:128], func=AF.Copy, scale=scale),
                     reads=["pT", "sng"], writes=list(wkeys))

        for ti in range(ntok // T if max_tiles is None else max_tiles):
            t0 = ti * T
            for q in range(4):
                P.op("sync", lambda e, q=q, t0=t0: e.dma_start(out=x_t[:, 4 * q:4 * q + 4, :],
                                                             in_=xv[:, 4 * q:4 * q + 4, t0:t0 + T]),
                     writes=["x"], dma=True)
            emit_norm_mod(c, P, x_t, "x", T, A, Akey, lambda kc: mod[:, kc:kc + 1], modkey,
                          hn, "hn", sq, "sq", rstd, pb[2], ("ps", 2))
            for blk in range(10):
                wv_, sl_ = load_w(3072 + blk * 512, 512)
                for j in range(4):
                    ch = blk * 4 + j
                    bank = pb[ch % 2]
                    pk = ("ps", ch % 2)
                    b2 = ch % 2
                    for kc in range(DC):
                        P.op("tensor", lambda e, kc=kc, j=j, wv_=wv_, bank=bank: e.matmul(
                            bank[:, :T], lhsT=wv_[:, kc, j * 128:(j + 1) * 128], rhs=hn[:, kc, :T],
                            start=(kc == 0), stop=(kc == DC - 1)),
                            reads=[("w", sl_, kc // 8), ("hn", kc)], writes=[pk])
                    P.op("gpsimd", lambda e, ch=ch, b2=b2: e.tensor_copy(out=pre[:, b2, 0:3], in_=hist[:, ch, :]),
                         reads=["hist", ("hist", ch)], writes=[("pre", b2)])
                    P.op("scalar", lambda e, bank=bank, b2=b2: e.copy(out=pre[:, b2, 3:3 + T], in_=bank[:, :T]),
                         reads=[pk, ("pre", b2)], writes=[("pre", b2)])
                    P.op("vector", lambda e, ch=ch, b2=b2: e.tensor_scalar(
                        out=acc[:, b2, :], in0=pre[:, b2, 0:T], scalar1=cw[:, ch * 4:ch * 4 + 1], scalar2=None,
                        op0=ALU.mult), reads=[("pre", b2), "cw"], writes=[("acc", b2)])
                    for k in range(1, 4):
                        P.op("vector", lambda e, ch=ch, b2=b2, k=k: e.scalar_tensor_tensor(
                            out=acc[:, b2, :], in0=pre[:, b2, k:k + T], scalar=cw[:, ch * 4 + k:ch * 4 + k + 1],
                            in1=acc[:, b2, :], op0=ALU.mult, op1=ALU.add),
                            reads=[("pre", b2), "cw", ("acc", b2)], writes=[("acc", b2)])
                    P.op("gpsimd", lambda e, ch=ch, b2=b2: e.tensor_copy(out=hist[:, ch, :], in_=pre[:, b2, T:T + 3]),
                         reads=[("pre", b2)], writes=[("hist", ch)])
                    if ch < 24:
                        dst, dkey = xsT[:, b2, :], ("xsT", b2)
                    elif ch < 32:
                        dst, dkey = BT[:, ch - 24, :], ("BT", ch - 24)
                    else:
                        dst, dkey = CT[:, ch - 32, :], ("CT", ch - 32)
                    P.op("scalar", lambda e, ch=ch, b2=b2, dst=dst: e.activation(
                        out=dst, in_=acc[:, b2, :], func=AF.Silu, bias=cb[:, ch:ch + 1]),
                        reads=[("acc", b2), "cb"], writes=[dkey])
                    if ch < 24:
                        for s in range(NS):
                            transpose_to(xsT[:, b2, s * 128:(s + 1) * 128], xs_tok[:, s, ch * 128:(ch + 1) * 128],
                                         [dkey], [("xs_tok", s, ch // 3)], eng="vector")
                    elif ch < 32:
                        for s in range(NS):
                            transpose_to(BT[:, ch - 24, s * 128:(s + 1) * 128], Btok[:, s, ch - 24, :],
                                         [dkey], [("Btok", s, ch - 24)], eng="vector")
            for zb in range(6):
                wv_, sl_ = load_w(zb * 512, 512)
                for s in range(NS):
                    bank = pb[(zb * NS + s) % 2]
                    pk = ("ps", (zb * NS + s) % 2)
                    for kc in range(DC):
                        P.op("tensor", lambda e, kc=kc, s=s, wv_=wv_, bank=bank: e.matmul(
                            bank[:, :], lhsT=hn[:, kc, s * 128:(s + 1) * 128], rhs=wv_[:, kc, :],
                            start=(kc == 0), stop=(kc == DC - 1)),
                            reads=[("w", sl_, kc // 8), ("hn", kc)], writes=[pk])
                    P.op("scalar", lambda e, s=s, zb=zb, bank=bank: e.activation(
                        out=zs_tok[:, s, zb * 512:(zb + 1) * 512], in_=bank[:, :], func=AF.Silu),
                        reads=[pk], writes=[("zs_tok", s, zb)])
            wdt, sdt = load_w(8192, 48)
            for s in range(NS):
                sl = slice(s * 128, (s + 1) * 128)
                for kc in range(DC):
                    P.op("tensor", lambda e, kc=kc, sl=sl, wdt=wdt: e.matmul(
                        pb[2][:, 0:48], lhsT=hn[:, kc, sl], rhs=wdt[:, kc, :], start=(kc == 0), stop=(kc == DC - 1)),
                        reads=[("w", sdt, kc // 8), ("hn", kc)], writes=[("ps", 2)])
                P.op("vector", lambda e: e.tensor_tensor(out=sm48[:, 0, :], in0=pb[2][:, 0:48], in1=dtb[:], op=ALU.add),
                     reads=[("ps", 2), "dtb"], writes=[("sm", 0)])
                P.op("scalar", lambda e: e.activation(out=sm48[:, 1, :], in_=sm48[:, 0, :], func=AF.Exp),
                     reads=[("sm", 0)], writes=[("sm", 1)])
                P.op("scalar", lambda e: e.activation(out=sm48[:, 2, :], in_=sm48[:, 1, :], func=AF.Ln, bias=1.0),
                     reads=[("sm", 1)], writes=[("sm", 2)])
                P.op("vector", lambda e: e.tensor_tensor(out=sm48[:, 3, :], in0=sm48[:, 2, :], in1=abc[:], op=ALU.mult),
                     reads=[("sm", 2), "abc"], writes=[("sm", 3)])
                P.op("tensor", lambda e: e.matmul(pb[2][:, 64:112], lhsT=triU[:], rhs=sm48[:, 3, :],
                                                  start=True, stop=True),
                     reads=["triU", ("sm", 3)], writes=[("ps", 2)])
                P.op("tensor", lambda e: e.matmul(pb[2][:, 128:176], lhsT=ones32[:], rhs=sm48[:, 3, :],
                                                  start=True, stop=True),
                     reads=["ones32", ("sm", 3)], writes=[("ps", 2)])
                P.op("scalar", lambda e: e.copy(out=sm48[:, 4, :], in_=pb[2][:, 64:112]),
                     reads=[("ps", 2)], writes=[("sm", 4)])
                P.op("scalar", lambda e: e.copy(out=sm48[:, 7, :], in_=pb[2][:, 128:176]),
                     reads=[("ps", 2)], writes=[("sm", 7)])
                P.op("vector", lambda e: e.tensor_scalar(out=sm48[:, 5, :], in0=sm48[:, 4, :], scalar1=-1.0,
                                                         scalar2=None, op0=ALU.mult),
                     reads=[("sm", 4)], writes=[("sm", 5)])
                P.op("scalar", lambda e: e.activation(out=sm48[:, 6, :], in_=sm48[:, 4, :], func=AF.Exp),
                     reads=[("sm", 4)], writes=[("sm", 6)])
                P.op("vector", lambda e: e.tensor_tensor(out=sm48[:, 8, :], in0=sm48[:, 7, :], in1=sm48[:, 4, :],
                                                         op=ALU.subtract),
                     reads=[("sm", 7), ("sm", 4)], writes=[("sm", 8)])
                P.op("scalar", lambda e: e.activation(out=sm48[:, 8, :], in_=sm48[:, 8, :], func=AF.Exp),
                     reads=[("sm", 8)], writes=[("sm", 8)])
                P.op("scalar", lambda e: e.activation(out=sm48[:, 9, :], in_=sm48[:, 7, :], func=AF.Exp),
                     reads=[("sm", 7)], writes=[("sm", 9)])
                for g in range(8):
                    if g % 4 == 0:
                        for gg_ in range(4):
                            g2 = g + gg_
                            P.op("tensor", lambda e, g2=g2, gg_=gg_, sl=sl: e.matmul(
                                pb[4][:, gg_ * 128:(gg_ + 1) * 128], lhsT=BT[:, g2, sl], rhs=CT[:, g2, sl],
                                start=True, stop=True),
                                reads=[("BT", g2), ("CT", g2)], writes=[("ps", 4)])
                    gb = g % 2
                    for hh in range(6):
                        h = g * 6 + hh
                        hq, hj = h // 4, h % 4
                        if hj == 0:
                            rb = hq % 2
                            P.op("vector", lambda e, hq=hq, rb=rb: e.tensor_tensor(
                                out=R4[:, rb, :].rearrange("p (a l) -> p a l", a=4),
                                in0=triU[:].rearrange("p (o l) -> p o l", o=1).broadcast_to([128, 4, 128]),
                                in1=bc3(sm48[:, 3, 4 * hq:4 * hq + 4], 4, 128), op=ALU.mult),
                                reads=["triU", ("sm", 3)], writes=[("R4", rb)])
                            P.op("tensor", lambda e, rb=rb: e.matmul(pb[3][:, :], lhsT=ones32[:], rhs=R4[:, rb, :],
                                                                   start=True, stop=False),
                                 reads=["ones32", ("R4", rb)], writes=[("ps", 3)])
                            P.op("tensor", lambda e: e.matmul(pb[3][:, :], lhsT=ident32[:], rhs=negm[:],
                                                              start=False, stop=True),
                                 reads=["ident32", "negm"], writes=[("ps", 3)])
                        eb = h % 2
                        P.op("scalar", lambda e, hj=hj, h=h, eb=eb: e.activation(
                            out=Eh[:, eb, :], in_=pb[3][:, hj * 128:(hj + 1) * 128], func=AF.Exp,
                            bias=sm48[:, 5, h:h + 1]),
                            reads=[("ps", 3), ("sm", 5)], writes=[("Eh", eb)])
                        P.op("vector", lambda e, g=g, hh=hh, eb=eb, gb=gb: e.tensor_tensor(
                            out=SL[:, gb, hh, :], in0=pb[4][:, (g % 4) * 128:(g % 4 + 1) * 128], in1=Eh[:, eb, :],
                            op=ALU.mult),
                            reads=[("ps", 4), ("Eh", eb)], writes=[("SL", gb, hh)])
                    gs = slice(g * 384, (g + 1) * 384)
                    hs = slice(g * 6, (g + 1) * 6)
                    v3 = lambda ap: ap.rearrange("p (h q) -> p h q", h=6)
                    P.op("gpsimd", lambda e, s=s, gs=gs, hs=hs, gb=gb: e.tensor_tensor(
                        out=v3(xdt[:, gb, :]), in0=v3(xs_tok[:, s, gs]), in1=bc3(sm48[:, 2, hs], 6, 64), op=ALU.mult),
                        reads=[("xs_tok", s, g), ("sm", 2)], writes=[("xdt", gb)])
                    P.op("gpsimd", lambda e, hs=hs, gb=gb: e.tensor_tensor(
                        out=v3(xdd[:, gb, :]), in0=v3(xdt[:, gb, :]), in1=bc3(sm48[:, 8, hs], 6, 64), op=ALU.mult),
                        reads=[("xdt", gb), ("sm", 8)], writes=[("xdd", gb)])
                    for hh in range(6):
                        P.op("tensor", lambda e, hh=hh, gb=gb: e.matmul(
                            pb[5][:, hh * 64:(hh + 1) * 64], lhsT=SL[:, gb, hh, :], rhs=xdt[:, gb, hh * 64:(hh + 1) * 64],
                            start=True, stop=True),
                            reads=[("SL", gb, hh), ("xdt", gb)], writes=[("ps", 5)])
                    P.op("tensor", lambda e, g=g, gs=gs, sl=sl: e.matmul(
                        pb[6][:, 0:384], lhsT=CT[:, g, sl], rhs=STbf[:, gs], start=True, stop=True),
                        reads=[("CT", g), ("STbf", g), "STbf"], writes=[("ps", 6)])
                    P.op("vector", lambda e, hs=hs, gb=gb: e.tensor_tensor(
                        out=v3(tmpa[:, gb, :]), in0=v3(pb[6][:, 0:384]), in1=bc3(sm48[:, 6, hs], 6, 64), op=ALU.mult),
                        reads=[("ps", 6), ("sm", 6)], writes=[("tmpa", gb)])
                    P.op("vector", lambda e, gb=gb: e.tensor_tensor(
                        out=tmpa[:, gb, :], in0=pb[5][:, 0:384], in1=tmpa[:, gb, :], op=ALU.add),
                        reads=[("ps", 5), ("tmpa", gb)], writes=[("tmpa", gb)])
                    P.op("gpsimd", lambda e, s=s, gs=gs, hs=hs, gb=gb: e.tensor_tensor(
                        out=v3(tmpb[:, gb, :]), in0=v3(xs_tok[:, s, gs]), in1=bc3(dsk[:, hs], 6, 64), op=ALU.mult),
                        reads=[("xs_tok", s, g), "dsk"], writes=[("tmpb", gb)])
                    P.op("gpsimd", lambda e, gb=gb: e.tensor_tensor(
                        out=tmpb[:, gb, :], in0=tmpb[:, gb, :], in1=tmpa[:, gb, :], op=ALU.add),
                        reads=[("tmpb", gb), ("tmpa", gb)], writes=[("tmpb", gb)])
                    P.op("gpsimd", lambda e, s=s, gs=gs, gb=gb: e.tensor_tensor(
                        out=tmpb[:, gb, :], in0=tmpb[:, gb, :], in1=zs_tok[:, s, gs], op=ALU.mult),
                        reads=[("tmpb", gb)] + [("zs_tok", s, zb) for zb in range(6)], writes=[("tmpb", gb)])
                    P.op("scalar", lambda e, gb=gb: e.activation(out=tmpc[:, gb, :], in_=tmpb[:, gb, :], func=AF.Square),
                         reads=[("tmpb", gb)], writes=[("tmpc", gb)])
                    P.op("vector", lambda e, gb=gb: e.reduce_sum(out=st[:, gb, 0:1], in_=tmpc[:, gb, :], axis=AX.X),
                         reads=[("tmpc", gb)], writes=[("st", gb)])
                    P.op("scalar", lambda e, gb=gb: e.activation(out=st[:, gb, 1:2], in_=st[:, gb, 0:1], func=AF.Sqrt,
                                                                 scale=1.0 / 384.0, bias=c.eps_col[:]),
                         reads=[("st", gb), "eps"], writes=[("st", gb)])
                    P.op("vector", lambda e, gb=gb: e.reciprocal(out=st[:, gb, 2:3], in_=st[:, gb, 1:2]),
                         reads=[("st", gb)], writes=[("st", gb)])
                    P.op("vector", lambda e, gb=gb: e.tensor_scalar(out=y3[:, gb, :], in0=tmpb[:, gb, :],
                                                                    scalar1=st[:, gb, 2:3], scalar2=None, op0=ALU.mult),
                         reads=[("tmpb", gb), ("st", gb)], writes=[("y3", gb)])
                    for j in range(3):
                        mc = g * 3 + j
                        transpose_to(y3[:, gb, j * 128:(j + 1) * 128], ya[:, mc, sl], [("y3", gb)], [("ya", mc)],
                                     scale=sng[:, mc:mc + 1])
                    P.op("tensor", lambda e, s=s, g=g, gb=gb: e.matmul(
                        pb[6][:, 0:384], lhsT=Btok[:, s, g, :], rhs=xdd[:, gb, :], start=True, stop=True),
                        reads=[("Btok", s, g), ("xdd", gb)], writes=[("ps", 6)])
                    P.op("vector", lambda e, gs=gs, hs=hs: e.tensor_tensor(
                        out=v3(ST32[:, gs]), in0=v3(ST32[:, gs]), in1=bc3(sm48[:, 9, hs], 6, 64), op=ALU.mult),
                        reads=[("ST32", g), "ST32", ("sm", 9)], writes=[("ST32", g)])
                    P.op("vector", lambda e, gs=gs: e.tensor_tensor(
                        out=ST32[:, gs], in0=ST32[:, gs], in1=pb[6][:, 0:384], op=ALU.add),
                        reads=[("ST32", g), ("ps", 6)], writes=[("ST32", g)])
                    P.op("scalar", lambda e, gs=gs, g=g: e.copy(out=STbf[:, gs], in_=ST32[:, gs]),
                         reads=[("ST32", g), "STbf"], writes=[("STbf", g)])
            for q in range(4):
                tok = P.op("sync", lambda e, q=q, t0=t0: e.dma_start(out=ov[:, 6 * q:6 * q + 6, t0:t0 + T],
                                                                   in_=ya[:, 6 * q:6 * q + 6, :]),
                           reads=[("ya", mc) for mc in range(6 * q, 6 * q + 6)], writes=["y_st"], dma=True)
                out_toks.append(tok)
                for mc in range(6 * q, 6 * q + 6):
                    P.readers.setdefault(("ya", mc), []).append(tok)
            if c.use_barrier:
                P.barrier(c)
        P.finalize(final_waits=out_toks[-8:])
    return nc


MAGIC = 12582912.0
TWO_PI = 2.0 * math.pi


def s5_layouts(a_re, a_im, log_dt, b_re, b_im, c_re, c_im):
    a_re = np.asarray(a_re, np.float32); a_im = np.asarray(a_im, np.float32)
    log_dt = np.asarray(log_dt, np.float32)
    b_re = np.asarray(b_re, np.float32); b_im = np.asarray(b_im, np.float32)
    c_re = np.asarray(c_re, np.float32); c_im = np.asarray(c_im, np.float32)
    out = {}
    i_ = np.arange(4)[:, None, None, None, None, None]
    gl = np.arange(2)[None, :, None, None, None, None]
    c_ = np.arange(16)[None, None, :, None, None, None]
    mm = np.arange(8)[None, None, None, :, None, None]
    glp = np.arange(2)[None, None, None, None, :, None]
    p_ = np.arange(64)[None, None, None, None, None, :]
    g = 2 * (4 * mm + i_) + glp
    shp = (4, 2, 16, 8, 2, 64)
    gB = np.broadcast_to(g, shp); pB = np.broadcast_to(p_, shp); cB = np.broadcast_to(c_, shp)
    out["are1"] = np.ascontiguousarray(a_re[gB, pB].reshape(128, 1024))
    out["aim1"] = np.ascontiguousarray(a_im[gB, pB].reshape(128, 1024))
    out["ldt1"] = np.ascontiguousarray(log_dt[gB].reshape(128, 1024))
    same = np.broadcast_to(gl == glp, shp)
    bre = np.where(same, b_re[gB, pB, cB], 0.0).astype(np.float32).reshape(128, 8, 128)
    bim = np.where(same, b_im[gB, pB, cB], 0.0).astype(np.float32).reshape(128, 8, 128)
    bre4 = np.zeros((4, 128, 8, 128), np.float32); bim4 = np.zeros((4, 128, 8, 128), np.float32)
    for i in range(4):
        bre4[i, i * 32:(i + 1) * 32] = bre[i * 32:(i + 1) * 32]
        bim4[i, i * 32:(i + 1) * 32] = bim[i * 32:(i + 1) * 32]
    out["bre4"] = bre4.reshape(4 * 128, 1024)
    out["bim4"] = bim4.reshape(4 * 128, 1024)
    gl2 = np.arange(2)[:, None, None]; p2 = np.arange(64)[None, :, None]; m2 = np.arange(32)[None, None, :]
    g2 = np.broadcast_to(2 * m2 + gl2, (2, 64, 32)); p2b = np.broadcast_to(p2, (2, 64, 32))
    out["are2"] = np.ascontiguousarray(a_re[g2, p2b].reshape(128, 32))
    out["aim2"] = np.ascontiguousarray(a_im[g2, p2b].reshape(128, 32))
    out["ldt2"] = np.ascontiguousarray(log_dt[g2].reshape(128, 32))
    cre = np.zeros((2, 64, 32, 4, 2, 16), np.float32); cim = np.zeros((2, 64, 32, 4, 2, 16), np.float32)
    for m in range(32):
        for gl_ in range(2):
            gg_ = 2 * m + gl_
            cre[gl_, :, m, m % 4, gl_, :] = c_re[gg_].T
            cim[gl_, :, m, m % 4, gl_, :] = c_im[gg_].T
    out["cre_l"] = cre.reshape(128, 32 * 128)
    out["cim_l"] = cim.reshape(128, 32 * 128)
    out["iota512"] = np.broadcast_to(np.arange(512, dtype=np.float32), (128, 512)).copy()
    return out


def emit_sincos(c, P, ang_ap, tmp_ap, dst_sin, dst_cos, rkeys, tkey, wkeys_sin, wkeys_cos, tmp2_ap, t2key):
    for (dst, add, wk) in [(dst_sin, 0.0, wkeys_sin), (dst_cos, 0.5 * math.pi, wkeys_cos)]:
        P.op("vector", lambda e, add=add: e.tensor_scalar(out=tmp2_ap, in0=ang_ap, scalar1=add, scalar2=None,
                                                          op0=ALU.add), reads=list(rkeys), writes=[t2key])
        P.op("vector", lambda e: e.tensor_scalar(out=tmp_ap, in0=tmp2_ap, scalar1=1.0 / TWO_PI, scalar2=MAGIC,
                                                 op0=ALU.mult, op1=ALU.add), reads=[t2key], writes=[tkey])
        P.op("vector", lambda e: e.tensor_scalar(out=tmp_ap, in0=tmp_ap, scalar1=MAGIC, scalar2=None,
                                                 op0=ALU.subtract), reads=[tkey], writes=[tkey])
        P.op("vector", lambda e: e.scalar_tensor_tensor(out=tmp_ap, in0=tmp_ap, scalar=-TWO_PI, in1=tmp2_ap,
                                                        op0=ALU.mult, op1=ALU.add),
             reads=[tkey, t2key], writes=[tkey])
        P.op("vector", lambda e: e.tensor_scalar(out=tmp_ap, in0=tmp_ap, scalar1=-math.pi, scalar2=math.pi,
                                                 op0=ALU.max, op1=ALU.min), reads=[tkey], writes=[tkey])
        P.op("scalar", lambda e, dst=dst: e.activation(out=dst, in_=tmp_ap, func=AF.Sin),
             reads=[tkey], writes=list(wk))


def build_s5(ntok, T=256, env=None):
    nc = mk_nc(env)
    dt_in = mk_in(nc, env)
    WQ = "sync" if (env is not None and getattr(env, "bf16w", False)) else "gpsimd"
    xin = dt_in("xT", [D, ntok])
    yain = dt_in("yaT", [3072, ntok])
    ccol = dt_in("ccol", [128, DC])
    ada_w = dt_in("ada_w", [D, 3 * D])
    ada_b = dt_in("ada_b", [128, 48])
    ng = dt_in("ng", [128, DC])
    w_in = dt_in("w_in", [D, IN0])
    w_out = dt_in("w_out", [4096, D])
    glu_w = dt_in("glu_w", [1024, 1024])
    glub_d = dt_in("glub", [128, 8])
    s5d_d = dt_in("s5d", [128, 8])
    are1_d = dt_in("are1", [128, 1024]); aim1_d = dt_in("aim1", [128, 1024]); ldt1_d = dt_in("ldt1", [128, 1024])
    bre4_d = dt_in("bre4", [512, 1024]); bim4_d = dt_in("bim4", [512, 1024])
    are2_d = dt_in("are2", [128, 32]); aim2_d = dt_in("aim2", [128, 32]); ldt2_d = dt_in("ldt2", [128, 32])
    cre_d = dt_in("cre_l", [128, 4096]); cim_d = dt_in("cim_l", [128, 4096])
    iota_d = dt_in("iota512", [128, 512])
    xout = mk_out(nc, env, "yT", [D, ntok])
    with ExitStack() as es:
        c = Ctx()
        c.nc, c.es = nc, es
        c.pfx = env.pfx if env is not None else ""
        P = Prog(nc, es, env.G if env is not None else None)
        c.P = P
        c.wslot_n = 0
        common_setup(c)
        wslots = [sb(c, f"wslot{i}", [128, 8192], BF16) for i in range(2)]
        x_t = sb(c, "x_t", [128, DC, T], F32)
        hn = sb(c, "hn", [128, DC, T], BF16)
        yT = sb(c, "yT_sb", [128, 32, T], BF16)
        rstd = sb(c, "rstd", [128, T], F32)
        bTre = sb(c, "bTre", [128, 32, 128], BF16)
        bTim = sb(c, "bTim", [128, 32, 128], BF16)
        CreT = sb(c, "CreT", [128, 32, 128], BF16)
        CimT = sb(c, "CimT", [128, 32, 128], BF16)
        cosj = sb(c, "cosj", [128, 32, T], BF16)
        sinj = sb(c, "sinj", [128, 32, T], BF16)
        uT = sb(c, "uT", [128, 8, T], BF16)
        yg32 = sb(c, "yg32", [128, 8, T], F32)
        ygb = sb(c, "ygb", [128, 8, T], BF16)
        tt = sb(c, "tt", [128, 12, T], F32)
        xri = sb(c, "xri", [128, 2, 4, 2, T], BF16)
        scr = sb(c, "scr", [128, 8, 512], F32)
        brei = sb(c, "brei", [128, 2, 512], F32)
        l2 = sb(c, "l2", [128, 12, 32], F32)
        iota = sb(c, "iota", [128, 512], F32)
        glub = sb(c, "glub", [128, 8], F32)
        s5d = sb(c, "s5d", [128, 8], F32)
        xres = sb(c, "xres", [128, 2, T], F32)
        pb = [ps(c, f"pb{i}", [128, 512]) for i in range(8)]
        A, mod, Akey, modkey = emit_adaln(c, P, "m", ccol, ada_w, ada_b, ng, wslots, pb[2], ("ps", 2))
        for (t_, d_, k_) in [(iota, iota_d, "iota"), (glub, glub_d, "glub"), (s5d, s5d_d, "s5d"),
                             (l2[:, 0, :], are2_d, ("l2", 0)), (l2[:, 1, :], aim2_d, ("l2", 1)),
                             (l2[:, 2, :], ldt2_d, ("l2", 2))]:
            P.op("sync", lambda e, t_=t_, d_=d_: e.dma_start(out=t_ if not hasattr(t_, "ap") else t_[:], in_=d_),
                 writes=[k_], dma=True)
        for q in range(4):
            P.op("gpsimd", lambda e, q=q: e.dma_start(out=CreT[:, 8 * q:8 * q + 8, :],
                                                     in_=cre_d[:, q * 1024:(q + 1) * 1024].rearrange(
                                                         "p (m k) -> p m k", m=8)),
                 writes=[("CreT", q)], dma=True)
            P.op("gpsimd", lambda e, q=q: e.dma_start(out=CimT[:, 8 * q:8 * q + 8, :],
                                                     in_=cim_d[:, q * 1024:(q + 1) * 1024].rearrange(
                                                         "p (m k) -> p m k", m=8)),
                 writes=[("CimT", q)], dma=True)
            P.op("vector", lambda e, q=q: e.tensor_scalar(out=CimT[:, 8 * q:8 * q + 8, :], in0=CimT[:, 8 * q:8 * q + 8, :],
                                                          scalar1=-1.0, scalar2=None, op0=ALU.mult),
                 reads=[("CimT", q)], writes=[("CimT", q)])
        P.op("scalar", lambda e: e.activation(out=l2[:, 9, :], in_=l2[:, 2, :], func=AF.Exp),
             reads=[("l2", 2)], writes=[("l2", 9)])
        P.op("vector", lambda e: e.tensor_tensor(out=l2[:, 3, :], in0=l2[:, 1, :], in1=l2[:, 9, :], op=ALU.mult),
             reads=[("l2", 1), ("l2", 9)], writes=[("l2", 3)])
        P.op("vector", lambda e: e.tensor_tensor(out=l2[:, 10, :], in0=l2[:, 0, :], in1=l2[:, 9, :], op=ALU.mult),
             reads=[("l2", 0), ("l2", 9)], writes=[("l2", 10)])
        P.op("scalar", lambda e: e.activation(out=l2[:, 4, :], in_=l2[:, 10, :], func=AF.Exp),
             reads=[("l2", 10)], writes=[("l2", 4)])
        P.op("vector", lambda e: e.memset(l2[:, 7, :], 0.0), writes=[("l2", 7)])
        P.op("vector", lambda e: e.memset(l2[:, 8, :], 0.0), writes=[("l2", 8)])
        P.op("vector", lambda e: e.tensor_scalar(out=l2[:, 11, :], in0=l2[:, 3, :], scalar1=float(T), scalar2=None,
                                                 op0=ALU.mult), reads=[("l2", 3)], writes=[("l2", 11)])
        emit_sincos(c, P, l2[:, 11, :], l2[:, 9, :], l2[:, 6, :], l2[:, 5, :], [("l2", 11)], ("l2", 9),
                    [("l2", 6)], [("l2", 5)], l2[:, 10, :], ("l2", 10))
        for m in range(32):
            P.op("vector", lambda e, m=m: e.tensor_scalar(out=tt[:, 0, :], in0=iota[:, :T], scalar1=l2[:, 3, m:m + 1],
                                                          scalar2=None, op0=ALU.mult),
                 reads=["iota", ("l2", 3)], writes=[("tt", 0, 0)])
            emit_sincos(c, P, tt[:, 0, :], tt[:, 1, :], sinj[:, m, :], cosj[:, m, :], [("tt", 0, 0)], ("tt", 0, 1),
                        [("sinj", m)], [("cosj", m)], tt[:, 2, :], ("tt", 0, 2))
        for hf in range(2):
            fs = slice(hf * 512, (hf + 1) * 512)
            S = lambda k: scr[:, k, :]
            for (k, d_) in [(0, are1_d), (1, aim1_d), (2, ldt1_d)]:
                P.op("sync", lambda e, k=k, d_=d_, fs=fs: e.dma_start(out=scr[:, k, :], in_=d_[:, fs]),
                     writes=[("scr", k)], dma=True)
            P.op("scalar", lambda e: e.activation(out=S(2), in_=S(2), func=AF.Exp), reads=[("scr", 2)],
                 writes=[("scr", 2)])
            P.op("vector", lambda e: e.tensor_tensor(out=S(3), in0=S(1), in1=S(2), op=ALU.mult),
                 reads=[("scr", 1), ("scr", 2)], writes=[("scr", 3)])
            P.op("vector", lambda e: e.tensor_tensor(out=S(4), in0=S(0), in1=S(2), op=ALU.mult),
                 reads=[("scr", 0), ("scr", 2)], writes=[("scr", 4)])
            P.op("scalar", lambda e: e.activation(out=S(4), in_=S(4), func=AF.Exp), reads=[("scr", 4)],
                 writes=[("scr", 4)])
            emit_sincos(c, P, S(3), S(7), S(5), S(6), [("scr", 3)], ("scr", 7), [("scr", 5)], [("scr", 6)],
                        S(2), ("scr", 2))
            P.op("vector", lambda e: e.tensor_tensor(out=S(5), in0=S(5), in1=S(4), op=ALU.mult),
                 reads=[("scr", 5), ("scr", 4)], writes=[("scr", 5)])
            P.op("vector", lambda e: e.tensor_tensor(out=S(6), in0=S(6), in1=S(4), op=ALU.mult),
                 reads=[("scr", 6), ("scr", 4)], writes=[("scr", 6)])
            P.op("vector", lambda e: e.tensor_scalar(out=S(6), in0=S(6), scalar1=-1.0, scalar2=None, op0=ALU.add),
                 reads=[("scr", 6)], writes=[("scr", 6)])
            P.op("vector", lambda e: e.tensor_tensor(out=S(2), in0=S(0), in1=S(0), op=ALU.mult),
                 reads=[("scr", 0)], writes=[("scr", 2)])
            P.op("vector", lambda e: e.tensor_tensor(out=S(3), in0=S(1), in1=S(1), op=ALU.mult),
                 reads=[("scr", 1)], writes=[("scr", 3)])
            P.op("vector", lambda e: e.tensor_tensor(out=S(2), in0=S(2), in1=S(3), op=ALU.add),
                 reads=[("scr", 2), ("scr", 3)], writes=[("scr", 2)])
            P.op("vector", lambda e: e.reciprocal(out=S(2), in_=S(2)), reads=[("scr", 2)], writes=[("scr", 2)])
            P.op("vector", lambda e: e.tensor_tensor(out=S(3), in0=S(6), in1=S(0), op=ALU.mult),
                 reads=[("scr", 6), ("scr", 0)], writes=[("scr", 3)])
            P.op("vector", lambda e: e.tensor_tensor(out=S(7), in0=S(5), in1=S(1), op=ALU.mult),
                 reads=[("scr", 5), ("scr", 1)], writes=[("scr", 7)])
            P.op("vector", lambda e: e.tensor_tensor(out=S(3), in0=S(3), in1=S(7), op=ALU.add),
                 reads=[("scr", 3), ("scr", 7)], writes=[("scr", 3)])
            P.op("vector", lambda e: e.tensor_tensor(out=S(3), in0=S(3), in1=S(2), op=ALU.mult),
                 reads=[("scr", 3), ("scr", 2)], writes=[("scr", 3)])
            P.op("vector", lambda e: e.tensor_tensor(out=S(4), in0=S(5), in1=S(0), op=ALU.mult),
                 reads=[("scr", 5), ("scr", 0)], writes=[("scr", 4)])
            P.op("vector", lambda e: e.tensor_tensor(out=S(7), in0=S(6), in1=S(1), op=ALU.mult),
                 reads=[("scr", 6), ("scr", 1)], writes=[("scr", 7)])
            P.op("vector", lambda e: e.tensor_tensor(out=S(4), in0=S(4), in1=S(7), op=ALU.subtract),
                 reads=[("scr", 4), ("scr", 7)], writes=[("scr", 4)])
            P.op("vector", lambda e: e.tensor_tensor(out=S(4), in0=S(4), in1=S(2), op=ALU.mult),
                 reads=[("scr", 4), ("scr", 2)], writes=[("scr", 4)])
            for i in range(4):
                P.op("sync", lambda e, i=i, fs=fs: e.dma_start(out=brei[:, 0, :], in_=bre4_d[i * 128:(i + 1) * 128, fs]),
                     writes=[("brei", 0)], dma=True)
                P.op("sync", lambda e, i=i, fs=fs: e.dma_start(out=brei[:, 1, :], in_=bim4_d[i * 128:(i + 1) * 128, fs]),
                     writes=[("brei", 1)], dma=True)
                P.op("vector", lambda e: e.tensor_tensor(out=S(0), in0=S(3), in1=brei[:, 0, :], op=ALU.mult),
                     reads=[("scr", 3), ("brei", 0)], writes=[("scr", 0)])
                P.op("vector", lambda e: e.tensor_tensor(out=S(1), in0=S(4), in1=brei[:, 1, :], op=ALU.mult),
                     reads=[("scr", 4), ("brei", 1)], writes=[("scr", 1)])
                dre = bTre[:, hf * 16:(hf + 1) * 16, :].rearrange("p (mm i) q -> p mm i q", i=4)[:, :, i, :]
                dim = bTim[:, hf * 16:(hf + 1) * 16, :].rearrange("p (mm i) q -> p mm i q", i=4)[:, :, i, :]
                P.op("vector", lambda e, dre=dre: e.tensor_tensor(
                    out=dre, in0=S(0).rearrange("p (mm q) -> p mm q", mm=4), in1=S(1).rearrange("p (mm q) -> p mm q", mm=4),
                    op=ALU.subtract), reads=[("scr", 0), ("scr", 1)], writes=["bTre"])
                P.op("vector", lambda e: e.tensor_tensor(out=S(0), in0=S(3), in1=brei[:, 1, :], op=ALU.mult),
                     reads=[("scr", 3), ("brei", 1)], writes=[("scr", 0)])
                P.op("vector", lambda e: e.tensor_tensor(out=S(1), in0=S(4), in1=brei[:, 0, :], op=ALU.mult),
                     reads=[("scr", 4), ("brei", 0)], writes=[("scr", 1)])
                P.op("vector", lambda e, dim=dim: e.tensor_tensor(
                    out=dim, in0=S(0).rearrange("p (mm q) -> p mm q", mm=4), in1=S(1).rearrange("p (mm q) -> p mm q", mm=4),
                    op=ALU.add), reads=[("scr", 0), ("scr", 1)], writes=["bTim"])
        xv = xin.rearrange("(kc p) t -> p kc t", p=128)
        yav = yain.rearrange("(mc p) t -> p mc t", p=128)
        ov = xout.rearrange("(kc p) t -> p kc t", p=128)
        w_in_v = w_in.rearrange("(kc p) n -> p kc n", p=128)
        w_out_v = w_out.rearrange("(mc p) n -> p mc n", p=128)
        glu_v = glu_w.rearrange("(kc p) n -> p kc n", p=128)
        out_toks = []
        GC = 2.0 * math.sqrt(2.0 / math.pi)
        tt2 = scr[:].rearrange("p a b -> p (a b)")[:, 0:12 * T].rearrange("p (k t) -> p k t", k=12)
        P.op("vector", lambda e: e.memset(tt2[:, 0, 0:1], 0.0), reads=[("scr", k) for k in range(8)],
             writes=[("tt", 1, k) for k in range(12)])
        for ti in range(ntok // T):
            t0 = ti * T
            for q in range(4):
                P.op("sync", lambda e, q=q, t0=t0: e.dma_start(out=x_t[:, 4 * q:4 * q + 4, :],
                                                             in_=xv[:, 4 * q:4 * q + 4, t0:t0 + T]),
                     writes=["x"], dma=True)
            emit_norm_mod(c, P, x_t, "x", T, A, Akey, lambda kc: mod[:, kc:kc + 1], modkey,
                          hn, "hn", yT, "yT", rstd, pb[2], ("ps", 2))
            for q in range(4):
                P.op("gpsimd", lambda e, q=q, t0=t0: e.dma_start(out=yT[:, 6 * q:6 * q + 6, :],
                                                               in_=yav[:, 6 * q:6 * q + 6, t0:t0 + T]),
                     reads=[("hn", kc) for kc in range(DC)], writes=[("yT", mc) for mc in range(6 * q, 6 * q + 6)],
                     dma=True)
            for ub in range(2):
                slot = c.wslot_n % 2
                c.wslot_n += 1
                wv_ = wslots[slot][:, 0:8192].rearrange("p (kc n) -> p kc n", kc=16)
                for half in range(2):
                    P.op(WQ, lambda e, wv_=wv_, half=half, ub=ub: e.dma_start(
                        out=wv_[:, 8 * half:8 * half + 8, :],
                        in_=w_in_v[:, 8 * half:8 * half + 8, 8240 + ub * 512:8240 + (ub + 1) * 512]),
                        writes=[("w", slot, half)], dma=True)
                for j in range(4):
                    uc = ub * 4 + j
                    bank = pb[uc % 2]
                    for kc in range(DC):
                        P.op("tensor", lambda e, kc=kc, j=j, wv_=wv_, bank=bank: e.matmul(
                            bank[:, :T], lhsT=wv_[:, kc, j * 128:(j + 1) * 128], rhs=hn[:, kc, :T],
                            start=(kc == 0), stop=(kc == DC - 1)),
                            reads=[("w", slot, kc // 8), ("hn", kc)], writes=[("ps", uc % 2)])
                    P.op("scalar", lambda e, uc=uc, bank=bank: e.copy(out=uT[:, uc, :], in_=bank[:, :T]),
                         reads=[("ps", uc % 2)], writes=[("uT", uc)])
            for m in range(32):
                mb, i4 = m // 4, m % 4
                par = m % 2
                ttp = tt if par == 0 else tt2
                TT = (lambda ttp: (lambda k: ttp[:, k, :]))(ttp)
                pre_, pim_ = (pb[3], pb[4]) if par == 0 else (pb[6], pb[7])
                kre, kim = (("ps", 3), ("ps", 4)) if par == 0 else (("ps", 6), ("ps", 7))
                tk = (lambda par: (lambda k: ("tt", par, k)))(par)
                xr = xri[:, mb % 2]
                xk = (lambda mbp: (lambda a, b: ("xri", mbp, a, b)))(mb % 2)
                P.op("tensor", lambda e, m=m, mb=mb, TT=TT, pre_=pre_, pim_=pim_, xr=xr: e.matmul(pre_[:, :T], lhsT=bTre[:, m, :], rhs=uT[:, mb, :],
                                                             start=True, stop=True),
                     reads=["bTre", ("uT", mb)], writes=[kre])
                P.op("tensor", lambda e, m=m, mb=mb, TT=TT, pre_=pre_, pim_=pim_, xr=xr: e.matmul(pim_[:, :T], lhsT=bTim[:, m, :], rhs=uT[:, mb, :],
                                                             start=True, stop=True),
                     reads=["bTim", ("uT", mb)], writes=[kim])
                cs, sn = cosj[:, m, :], sinj[:, m, :]
                ck, sk = ("cosj", m), ("sinj", m)
                P.op("vector", lambda e, cs=cs, TT=TT, pre_=pre_, pim_=pim_, xr=xr: e.tensor_tensor(out=TT(0), in0=pre_[:, :T], in1=cs, op=ALU.mult),
                     reads=[kre, ck], writes=[tk(0)])
                P.op("vector", lambda e, sn=sn, TT=TT, pre_=pre_, pim_=pim_, xr=xr: e.tensor_tensor(out=TT(1), in0=pim_[:, :T], in1=sn, op=ALU.mult),
                     reads=[kim, sk], writes=[tk(1)])
                P.op("gpsimd", lambda e, TT=TT, pre_=pre_, pim_=pim_, xr=xr: e.tensor_tensor(out=TT(4), in0=TT(0), in1=TT(1), op=ALU.add),
                     reads=[tk(0), tk(1)], writes=[tk(4)])
                P.op("vector", lambda e, cs=cs, TT=TT, pre_=pre_, pim_=pim_, xr=xr: e.tensor_tensor(out=TT(2), in0=pim_[:, :T], in1=cs, op=ALU.mult),
                     reads=[kim, ck], writes=[tk(2)])
                P.op("vector", lambda e, sn=sn, TT=TT, pre_=pre_, pim_=pim_, xr=xr: e.tensor_tensor(out=TT(3), in0=pre_[:, :T], in1=sn, op=ALU.mult),
                     reads=[kre, sk], writes=[tk(3)])
                P.op("gpsimd", lambda e, TT=TT, pre_=pre_, pim_=pim_, xr=xr: e.tensor_tensor(out=TT(5), in0=TT(2), in1=TT(3), op=ALU.subtract),
                     reads=[tk(2), tk(3)], writes=[tk(5)])
                P.op("vector", lambda e, m=m, TT=TT, pre_=pre_, pim_=pim_, xr=xr: e.tensor_tensor_scan(
                    out=TT(6), data0=l2[:, 4, m:m + 1].broadcast_to([128, T]), data1=TT(4),
                    initial=l2[:, 7, m:m + 1], op0=ALU.mult, op1=ALU.add),
                    reads=[tk(4), ("l2", 4), ("l2", 7)], writes=[tk(6)])
                P.op("vector", lambda e, m=m, TT=TT, pre_=pre_, pim_=pim_, xr=xr: e.tensor_tensor_scan(
                    out=TT(7), data0=l2[:, 4, m:m + 1].broadcast_to([128, T]), data1=TT(5),
                    initial=l2[:, 8, m:m + 1], op0=ALU.mult, op1=ALU.add),
                    reads=[tk(5), ("l2", 4), ("l2", 8)], writes=[tk(7)])
                P.op("scalar", lambda e, m=m, TT=TT, pre_=pre_, pim_=pim_, xr=xr: e.copy(out=l2[:, 7, m:m + 1], in_=TT(6)[:, T - 1:T]),
                     reads=[tk(6), ("l2", 7)], writes=[("l2", 7)])
                P.op("scalar", lambda e, m=m, TT=TT, pre_=pre_, pim_=pim_, xr=xr: e.copy(out=l2[:, 8, m:m + 1], in_=TT(7)[:, T - 1:T]),
                     reads=[tk(7), ("l2", 8)], writes=[("l2", 8)])
                P.op("gpsimd", lambda e, cs=cs, TT=TT, pre_=pre_, pim_=pim_, xr=xr: e.tensor_tensor(out=TT(8), in0=TT(6), in1=cs, op=ALU.mult),
                     reads=[tk(6), ck], writes=[tk(8)])
                P.op("gpsimd", lambda e, sn=sn, TT=TT, pre_=pre_, pim_=pim_, xr=xr: e.tensor_tensor(out=TT(9), in0=TT(7), in1=sn, op=ALU.mult),
                     reads=[tk(7), sk], writes=[tk(9)])
                P.op("gpsimd", lambda e, i4=i4, TT=TT, pre_=pre_, pim_=pim_, xr=xr: e.tensor_tensor(out=xr[:, i4, 0, :], in0=TT(8), in1=TT(9),
                                                                op=ALU.subtract),
                     reads=[tk(8), tk(9)], writes=[xk(i4, 0)])
                P.op("gpsimd", lambda e, sn=sn, TT=TT, pre_=pre_, pim_=pim_, xr=xr: e.tensor_tensor(out=TT(10), in0=TT(6), in1=sn, op=ALU.mult),
                     reads=[tk(6), sk], writes=[tk(10)])
                P.op("gpsimd", lambda e, cs=cs, TT=TT, pre_=pre_, pim_=pim_, xr=xr: e.tensor_tensor(out=TT(11), in0=TT(7), in1=cs, op=ALU.mult),
                     reads=[tk(7), ck], writes=[tk(11)])
                P.op("gpsimd", lambda e, i4=i4, TT=TT, pre_=pre_, pim_=pim_, xr=xr: e.tensor_tensor(out=xr[:, i4, 1, :], in0=TT(10), in1=TT(11),
                                                                op=ALU.add),
                     reads=[tk(10), tk(11)], writes=[xk(i4, 1)])
                if i4 == 3:
                    for i5 in range(4):
                        m5 = mb * 4 + i5
                        P.op("tensor", lambda e, m5=m5, i5=i5, TT=TT, pre_=pre_, pim_=pim_, xr=xr: e.matmul(
                            pb[5][:, :T], lhsT=CreT[:, m5, :], rhs=xr[:, i5, 0, :], start=(i5 == 0), stop=False),
                            reads=[("CreT", m5 // 8), xk(i5, 0)], writes=[("ps", 5)])
                        P.op("tensor", lambda e, m5=m5, i5=i5, TT=TT, pre_=pre_, pim_=pim_, xr=xr: e.matmul(
                            pb[5][:, :T], lhsT=CimT[:, m5, :], rhs=xr[:, i5, 1, :], start=False, stop=(i5 == 3)),
                            reads=[("CimT", m5 // 8), xk(i5, 1)], writes=[("ps", 5)])
                    P.op("vector", lambda e, mb=mb, TT=TT, pre_=pre_, pim_=pim_, xr=xr: e.scalar_tensor_tensor(
                        out=TT(0), in0=uT[:, mb, :], scalar=s5d[:, mb:mb + 1], in1=pb[5][:, :T],
                        op0=ALU.mult, op1=ALU.add),
                        reads=[("uT", mb), "s5d", ("ps", 5)], writes=[tk(0)])
                    P.op("scalar", lambda e, TT=TT, pre_=pre_, pim_=pim_, xr=xr: e.activation(out=TT(1), in_=TT(0), func=AF.Square),
                         reads=[tk(0)], writes=[tk(1)])
                    P.op("vector", lambda e, TT=TT, pre_=pre_, pim_=pim_, xr=xr: e.tensor_scalar(out=TT(1), in0=TT(1), scalar1=0.044715, scalar2=1.0,
                                                             op0=ALU.mult, op1=ALU.add),
                         reads=[tk(1)], writes=[tk(1)])
                    P.op("vector", lambda e, TT=TT, pre_=pre_, pim_=pim_, xr=xr: e.tensor_tensor(out=TT(1), in0=TT(1), in1=TT(0), op=ALU.mult),
                         reads=[tk(1), tk(0)], writes=[tk(1)])
                    P.op("scalar", lambda e, TT=TT, pre_=pre_, pim_=pim_, xr=xr: e.activation(out=TT(2), in_=TT(1), func=AF.Sigmoid, scale=GC),
                         reads=[tk(1)], writes=[tk(2)])
                    P.op("vector", lambda e, mb=mb, TT=TT, pre_=pre_, pim_=pim_, xr=xr: e.tensor_tensor(out=yg32[:, mb, :], in0=TT(0), in1=TT(2),
                                                                    op=ALU.mult),
                         reads=[tk(0), tk(2)], writes=[("yg32", mb)])
                    P.op("scalar", lambda e, mb=mb, TT=TT, pre_=pre_, pim_=pim_, xr=xr: e.copy(out=ygb[:, mb, :], in_=yg32[:, mb, :]),
                         reads=[("yg32", mb)], writes=[("ygb", mb)])
            L = lambda k: l2[:, k, :]
            P.op("vector", lambda e: e.tensor_tensor(out=L(9), in0=L(7), in1=L(5), op=ALU.mult),
                 reads=[("l2", 7), ("l2", 5)], writes=[("l2", 9)])
            P.op("vector", lambda e: e.tensor_tensor(out=L(10), in0=L(8), in1=L(6), op=ALU.mult),
                 reads=[("l2", 8), ("l2", 6)], writes=[("l2", 10)])
            P.op("vector", lambda e: e.tensor_tensor(out=L(9), in0=L(9), in1=L(10), op=ALU.subtract),
                 reads=[("l2", 9), ("l2", 10)], writes=[("l2", 9)])
            P.op("vector", lambda e: e.tensor_tensor(out=L(10), in0=L(7), in1=L(6), op=ALU.mult),
                 reads=[("l2", 7), ("l2", 6)], writes=[("l2", 10)])
            P.op("vector", lambda e: e.tensor_tensor(out=L(11), in0=L(8), in1=L(5), op=ALU.mult),
                 reads=[("l2", 8), ("l2", 5)], writes=[("l2", 11)])
            P.op("vector", lambda e: e.tensor_tensor(out=L(8), in0=L(10), in1=L(11), op=ALU.add),
                 reads=[("l2", 10), ("l2", 11)], writes=[("l2", 8)])
            P.op("vector", lambda e: e.tensor_copy(out=L(7), in_=L(9)), reads=[("l2", 9)], writes=[("l2", 7)])
            slot = c.wslot_n % 2
            c.wslot_n += 1
            gv = wslots[slot][:, 0:8192].rearrange("p (kc n) -> p kc n", kc=8)
            for half in range(2):
                P.op(WQ, lambda e, gv=gv, half=half: e.dma_start(out=gv[:, 4 * half:4 * half + 4, :],
                                                                      in_=glu_v[:, 4 * half:4 * half + 4, :]),
                     writes=[("w", slot, half)], dma=True)
            for ob in range(8):
                bank = pb[ob % 2]
                for kc in range(8):
                    P.op("tensor", lambda e, kc=kc, ob=ob, gv=gv, bank=bank: e.matmul(
                        bank[:, :T], lhsT=gv[:, kc, ob * 128:(ob + 1) * 128], rhs=ygb[:, kc, :],
                        start=(kc == 0), stop=(kc == 7)),
                        reads=[("w", slot, kc // 4), ("ygb", kc)], writes=[("ps", ob % 2)])
                P.op("scalar", lambda e, ob=ob, bank=bank: e.activation(out=tt[:, 3, :], in_=bank[:, :T], func=AF.Sigmoid,
                                                                       bias=glub[:, ob:ob + 1]),
                     reads=[("ps", ob % 2), "glub"], writes=[("tt", 0, 3)])
                P.op("vector", lambda e, ob=ob: e.tensor_tensor(out=yT[:, 24 + ob, :], in0=yg32[:, ob, :], in1=tt[:, 3, :],
                                                                op=ALU.mult),
                     reads=[("yg32", ob), ("tt", 0, 3)] + [("hn", kc) for kc in range(DC)], writes=[("yT", 24 + ob)])
            for ob in range(D // 256):
                slot = c.wslot_n % 2
                c.wslot_n += 1
                wov = wslots[slot][:, 0:8192].rearrange("p (mc n) -> p mc n", mc=32)
                for half in range(2):
                    P.op(WQ, lambda e, ob=ob, wov=wov, half=half: e.dma_start(
                        out=wov[:, 16 * half:16 * half + 16, :],
                        in_=w_out_v[:, 16 * half:16 * half + 16, ob * 256:(ob + 1) * 256]),
                        writes=[("w", slot, half)], dma=True)
                for j in range(2):
                    dc = ob * 2 + j
                    bank = pb[dc % 2]
                    for mc in range(32):
                        P.op("tensor", lambda e, mc=mc, j=j, bank=bank, wov=wov: e.matmul(
                            bank[:, :T], lhsT=wov[:, mc, j * 128:(j + 1) * 128], rhs=yT[:, mc, :T],
                            start=(mc == 0), stop=(mc == 31)),
                            reads=[("w", slot, mc // 16), ("yT", mc)], writes=[("ps", dc % 2)])
                    P.op("vector", lambda e, dc=dc, bank=bank: e.scalar_tensor_tensor(
                        out=xres[:, dc % 2, :], in0=bank[:, :T], scalar=mod[:, 32 + dc:33 + dc],
                        in1=x_t[:, dc, :T], op0=ALU.mult, op1=ALU.add),
                        reads=[("ps", dc % 2), modkey, "x"], writes=[("xres", dc % 2)])
                    tok = P.op("sync", lambda e, dc=dc, t0=t0: e.dma_start(out=ov[:, dc, t0:t0 + T],
                                                                         in_=xres[:, dc % 2, :]),
                               reads=[("xres", dc % 2)], writes=["x_st"], dma=True)
                    out_toks.append(tok)
        P.finalize(final_waits=out_toks[-8:])
    return nc


WSHAPES = {"ab_w_in": [D, IN0], "ab_w_out": [4096, D], "glu_w": [1024, 1024],
           "f0_w_in": [D, 2 * FFN_H], "f0_w_out": [FFN_H, D], "r_w_in": [D, 12288], "r_w_out": [4096, D],
           "f1_w_in": [D, 2 * FFN_H], "f1_w_out": [FFN_H, D]}


def build_fused(ntok):
    nc = bass.Bass("TRN2", target_bir_lowering=False)
    with ExitStack() as ges:
        env = Env(nc, ges)
        env.bf16w = True
        x0 = nc.dram_tensor("xT", [D, ntok], F32, kind="ExternalInput").ap()
        out = nc.dram_tensor("yT", [D, ntok], F32, kind="ExternalOutput").ap()
        ya = nc.dram_tensor("scr_ya", [3072, ntok], F32, kind="Internal").ap()
        x1 = nc.dram_tensor("scr_x1", [D, ntok], F32, kind="Internal").ap()
        x2 = nc.dram_tensor("scr_x2", [D, ntok], F32, kind="Internal").ap()
        x3 = nc.dram_tensor("scr_x3", [D, ntok], F32, kind="Internal").ap()
        wb = {}
        with ExitStack() as es:
            P = Prog(nc, es, env.G)
            toks = []
            for name, shp in WSHAPES.items():
                src = nc.dram_tensor(name, list(shp), F32, kind="ExternalInput").ap()
                dst = nc.dram_tensor(name + "_bf", list(shp), BF16, kind="Internal").ap()
                wb[name] = dst
                R = 256
                for r0 in range(0, shp[0], R):
                    toks.append(P.op("gpsimd", lambda e, src=src, dst=dst, r0=r0, R=R: e.dma_start(
                        out=dst[r0:r0 + R, :], in_=src[r0:r0 + R, :]), writes=[("cv", name, r0)], dma=True))
            P.finalize(final_waits=toks)
        nc.all_engine_barrier()
        phases = [("a_", build_ssd, {"xT": x0, "yT": ya, "w_in": wb["ab_w_in"]}, {}),
                  ("b_", build_s5, {"xT": x0, "yaT": ya, "yT": x1, "w_in": wb["ab_w_in"], "w_out": wb["ab_w_out"],
                                    "glu_w": wb["glu_w"]}, {}),
                  ("f0_", build_ffn, {"xT": x1, "yT": x2, "w_in": wb["f0_w_in"], "w_out": wb["f0_w_out"]},
                   {"final_norm": False}),
                  ("r_", build_ret, {"xT": x2, "yT": x3, "w_in": wb["r_w_in"], "w_out": wb["r_w_out"]}, {}),
                  ("f1_", build_ffn, {"xT": x3, "yT": out, "w_in": wb["f1_w_in"], "w_out": wb["f1_w_out"]},
                   {"final_norm": True})]
        for pfx, fn, io, kw in phases:
            env.pfx = pfx
            env.io = io
            fn(ntok, env=env, **kw)
            nc.all_engine_barrier()
    return nc


def fused_inputs(b, x, c, norm_mix_g, ada_mix_w, ada_mix_b, norm_ffn_g, ada_ffn_w, ada_ffn_b,
                 ffn_w_in, ffn_w_out, ab_w_in, ssd_conv_w, ssd_conv_b, ssd_dt_bias, ssd_a_log,
                 ssd_d, ssd_norm_g, s5_a_re, s5_a_im, s5_log_dt, s5_b_re, s5_b_im, s5_c_re, s5_c_im,
                 s5_d, s5_glu_w, s5_glu_b, ab_w_out, ret_w_in, ret_gn_g, ret_w_out, final_norm_g, shared=None):
    f = lambda a: np.ascontiguousarray(np.asarray(a, dtype=np.float32))
    if shared is None:
        sh = {}
        rc = ret_consts()
        sc_ = ssd_consts()
        s5l = s5_layouts(s5_a_re[0], s5_a_im[0], s5_log_dt[0], s5_b_re[0], s5_b_im[0], s5_c_re[0], s5_c_im[0])
        sh["ident"] = rc["ident"]
        sh["iota512"] = rc["iota512"]
        for k in ("Emat", "M01", "irow1", "jcol", "pidx"):
            sh["r_" + k] = rc[k]
        for k in ("triU", "negm4"):
            sh["a_" + k] = sc_[k]
        for k, v in s5l.items():
            if k != "iota512":
                sh["b_" + k] = v
        for p_ in ("a_", "b_"):
            sh[p_ + "ada_w"] = f(ada_mix_w[0]); sh[p_ + "ada_b"] = col_layout(ada_mix_b[0])
            sh[p_ + "ng"] = col_layout(norm_mix_g[0])
        sh["ab_w_in"] = f(ab_w_in[0])
        sh["a_cw"] = np.ascontiguousarray(f(ssd_conv_w[0]).T.reshape(40, 128, 4).transpose(1, 0, 2).reshape(128, 160))
        sh["a_cb"] = col_layout(ssd_conv_b[0])
        sh["a_dtb"] = f(ssd_dt_bias[0]).reshape(1, 48); sh["a_alog"] = f(ssd_a_log[0]).reshape(1, 48)
        sh["a_dsk"] = f(ssd_d[0]).reshape(1, 48); sh["a_sng"] = col_layout(ssd_norm_g[0])
        sh["ab_w_out"] = f(ab_w_out[0]); sh["glu_w"] = f(s5_glu_w[0]); sh["b_glub"] = col_layout(s5_glu_b[0])
        sh["b_s5d"] = col_layout(s5_d[0])
        for i, p_ in ((0, "f0_"), (1, "f1_")):
            sh[p_ + "ada_w"] = f(ada_ffn_w[i]); sh[p_ + "ada_b"] = col_layout(ada_ffn_b[i])
            sh[p_ + "ng"] = col_layout(norm_ffn_g[i]); sh[p_ + "w_in"] = f(ffn_w_in[i])
            sh[p_ + "w_out"] = f(ffn_w_out[i]); sh[p_ + "fg"] = col_layout(final_norm_g)
        sh["r_ada_w"] = f(ada_mix_w[1]); sh["r_ada_b"] = col_layout(ada_mix_b[1]); sh["r_ng"] = col_layout(norm_mix_g[1])
        sh["r_w_in"] = f(ret_w_in[0]); sh["r_w_out"] = f(ret_w_out[0]); sh["r_gng"] = f(ret_gn_g[0]).reshape(1, 4096)
        shared = sh
    m = dict(shared)
    m["xT"] = np.ascontiguousarray(f(x[b]).T)
    m["ccol"] = col_layout(np.asarray(c)[b])
    return m, shared


NCORES = 8
_CACHE = {}


def _prog(kind, ntok):
    key = (kind, ntok)
    if key not in _CACHE:
        if kind == "ffn":
            _CACHE[key] = build_ffn(ntok, final_norm=False)
        elif kind == "ffn_final":
            _CACHE[key] = build_ffn(ntok, final_norm=True)
        elif kind == "ssd":
            _CACHE[key] = build_ssd(ntok)
        elif kind == "s5":
            _CACHE[key] = build_s5(ntok)
        else:
            _CACHE[key] = build_ret(ntok)
    return _CACHE[key]


def _launch(nc, maps):
    res = run_bass_kernel_spmd(nc, maps, core_ids=list(range(NCORES)))
    return [np.asarray(r["yT"]) for r in res.results]


def kernel(**inputs):
    x = np.asarray(inputs["x"], dtype=np.float32)
    B, L, _ = x.shape
    key = ("fused", L)
    if key not in _CACHE:
        _CACHE[key] = build_fused(L)
    nc = _CACHE[key]
    maps = []
    shared = None
    per_b = []
    for b in range(B):
        m, shared = fused_inputs(b, shared=shared, **inputs)
        per_b.append(m)
    for k in range(NCORES):
        maps.append(per_b[k % B])
    res = run_bass_kernel_spmd(nc, maps, core_ids=list(range(NCORES)))
    outs = [np.asarray(res.results[b]["yT"]) for b in range(B)]
    out = np.stack([np.ascontiguousarray(o.T) for o in outs], axis=0)
    return out.astype(np.float32)
```

```python
import numpy as np
import concourse.bass as bass
import concourse.mybir as mybir
from concourse.bass_utils import run_bass_kernel_spmd
from contextlib import ExitStack

F32 = mybir.dt.float32
BF16 = mybir.dt.bfloat16
ALU = mybir.AluOpType
AF = mybir.ActivationFunctionType
AX = mybir.AxisListType

D = 2048
DC = D // 128
SEQ = 4096
FFN_H = 5632
HC = FFN_H // 128
EPS = 1e-6

ENGS = ["tensor", "vector", "scalar", "gpsimd", "sync"]
SEM_CAP = 20000
DMA_RING = 6


class Prog:
    def __init__(self, nc, es, G=None):
        self.nc = nc
        self.es = es
        self.G = G if G is not None else SemState(es)
        self.ops = {e: [] for e in ENGS}
        self.last_w = {}
        self.readers = {}

    def op(self, eng, fn, reads=(), writes=(), dma=False):
        idx = len(self.ops[eng])
        tok = (eng, idx)
        waits = set()
        for k in reads:
            w = self.last_w.get(k)
            if w is not None:
                waits.add(w)
        for k in writes:
            w = self.last_w.get(k)
            if w is not None:
                waits.add(w)
            for r in self.readers.get(k, ()):
                waits.add(r)
        waits.discard(tok)
        self.ops[eng].append(dict(fn=fn, waits=waits, dma=dma, need_inc=False))
        for k in reads:
            self.readers.setdefault(k, []).append(tok)
        for k in writes:
            self.last_w[k] = tok
            self.readers[k] = []
        return tok

    def barrier(self, c):
        keys = list(set(list(self.last_w.keys()) + list(self.readers.keys())))
        if not hasattr(c, "bar_t"):
            c.bar_t = sb(c, "bar_t", [128, 8], F32)
            c.bar_p = ps(c, "bar_p", [128, 8], F32) if getattr(c, "bar_psum", False) else None
        self.op("vector", lambda e: e.memset(c.bar_t[:, 0:1], 0.0), writes=keys)
        self.op("scalar", lambda e: e.copy(out=c.bar_t[:, 2:3], in_=c.bar_t[:, 0:1]), writes=keys)
        self.op("gpsimd", lambda e: e.memset(c.bar_t[:, 4:5], 0.0), writes=keys)
        self.op("vector", lambda e: e.memset(c.bar_t[:, 6:7], 0.0), writes=keys)

    def finalize(self, final_waits=()):
        nc, es = self.nc, self.es
        ops = self.ops
        for eng in ENGS:
            seen = {}
            for i, o in enumerate(ops[eng]):
                pr = []
                for (e2, j) in sorted(o["waits"]):
                    tgt = ops[e2][j]
                    if tgt["dma"]:
                        key = ("dma", e2, j)
                        if key in seen:
                            continue
                        seen[key] = True
                        pr.append((e2, j))
                    else:
                        if e2 == eng and eng == "tensor":
                            continue
                        if seen.get(e2, -1) >= j:
                            continue
                        seen[e2] = j
                        pr.append((e2, j))
                        tgt["need_inc"] = True
                o["pw"] = pr
        G = self.G
        for eng in ENGS:
            for o in ops[eng]:
                if o["dma"] or not o["need_inc"]:
                    continue
                if G.ccount[eng] >= SEM_CAP:
                    G.cepoch[eng] += 1
                    G.ccount[eng] = 0
                G.ccount[eng] += 1
                key = (eng, G.cepoch[eng])
                if key not in G.csem:
                    G.csem[key] = G.es.enter_context(nc.semaphore(f"c_{eng}_{G.cepoch[eng]}"))
                o["sem"] = G.csem[key]
                o["val"] = G.ccount[eng]
        for eng in ENGS:
            for o in ops[eng]:
                if not o["dma"]:
                    continue
                n = G.dn[eng]
                slot = n % DMA_RING
                if (eng, slot) not in G.dsem:
                    G.dsem[(eng, slot)] = G.es.enter_context(nc.semaphore(f"d_{eng}_{slot}"))
                o["sem"] = G.dsem[(eng, slot)]
                o["val"] = 16 * (n // DMA_RING + 1)
                o["prev"] = (G.dsem[(eng, slot)], 16 * (n // DMA_RING)) if n >= DMA_RING else None
                G.dn[eng] = n + 1
        block = es.enter_context(nc.Block())

        def make(eng):
            def body(e):
                for o in ops[eng]:
                    for (e2, j) in o["pw"]:
                        t = ops[e2][j]
                        e.wait_ge(t["sem"], t["val"])
                    if o["dma"] and o["prev"] is not None:
                        e.wait_ge(o["prev"][0], o["prev"][1])
                    ins = o["fn"](e)
                    if o["dma"]:
                        ins.then_inc(o["sem"], 16)
                    elif o["need_inc"]:
                        ins.then_inc(o["sem"], 1)
                if eng == "sync":
                    for (e2, j) in final_waits:
                        t = ops[e2][j]
                        e.wait_ge(t["sem"], t["val"])
            return body

        block.tensor(make("tensor"))
        block.vector(make("vector"))
        block.scalar(make("scalar"))
        block.gpsimd(make("gpsimd"))
        block.sync(make("sync"))


class SemState:
    def __init__(self, es):
        self.es = es
        self.csem = {}
        self.dsem = {}
        self.ccount = {e: 0 for e in ENGS}
        self.cepoch = {e: 0 for e in ENGS}
        self.dn = {e: 0 for e in ENGS}


class Env:
    def __init__(self, nc, ges):
        self.nc = nc
        self.G = SemState(ges)
        self.io = {}
        self.pfx = ""
        self.shared = {}

    def tensor(self, name, shape):
        if name in self.io:
            return self.io[name]
        if name in ("ccol", "ident", "iota512"):
            if name not in self.shared:
                self.shared[name] = self.nc.dram_tensor(name, list(shape), F32, kind="ExternalInput").ap()
            return self.shared[name]
        return self.nc.dram_tensor(self.pfx + name, list(shape), F32, kind="ExternalInput").ap()


def mk_nc(env):
    return env.nc if env is not None else bass.Bass("TRN2", target_bir_lowering=False)


def mk_in(nc, env):
    if env is None:
        return lambda n, s: nc.dram_tensor(n, list(s), F32, kind="ExternalInput").ap()
    return lambda n, s: env.tensor(n, s)


def mk_out(nc, env, name, shape):
    if env is None:
        return nc.dram_tensor(name, list(shape), F32, kind="ExternalOutput").ap()
    return env.io[name]


class Ctx:
    pfx = ""


def sb(c, name, shape, dt):
    return c.es.enter_context(c.nc.sbuf_tensor("s_" + c.pfx + name, list(shape), dt))


def ps(c, name, shape, dt=F32):
    return c.es.enter_context(c.nc.psum_tensor("p_" + c.pfx + name, list(shape), dt))


def emit_adaln(c, P, tag, ccol_dram, w_dram, b_dram, g_dram, wslots, psum_t, pkey=("psum", "ada")):
    nc = c.nc
    cc = sb(c, f"cc_{tag}", [128, DC], F32)
    sc = sb(c, f"sc_{tag}", [128, DC], BF16)
    bb = sb(c, f"bb_{tag}", [128, 48], F32)
    gg = sb(c, f"gg_{tag}", [128, DC], F32)
    mod = sb(c, f"mod_{tag}", [128, 48], F32)
    A = sb(c, f"A_{tag}", [128, DC], F32)
    P.op("sync", lambda e: e.dma_start(out=cc[:], in_=ccol_dram), writes=[("cc", tag)], dma=True)
    P.op("sync", lambda e: e.dma_start(out=bb[:], in_=b_dram), writes=[("bb", tag)], dma=True)
    P.op("sync", lambda e: e.dma_start(out=gg[:], in_=g_dram), writes=[("gg", tag)], dma=True)
    P.op("scalar", lambda e: e.activation(out=sc[:], in_=cc[:], func=AF.Silu),
         reads=[("cc", tag)], writes=[("sc", tag)])
    wv = w_dram.rearrange("(kc p) n -> p kc n", p=128)
    for blk in range(12):
        slot = c.wslot_n % 2
        c.wslot_n += 1
        wt = wslots[slot]
        wtv = wt[:, 0:8192].rearrange("p (kc n) -> p kc n", kc=16)
        for half in range(2):
            P.op("gpsimd",
                 lambda e, blk=blk, wtv=wtv, half=half: e.dma_start(
                     out=wtv[:, 8 * half:8 * half + 8, :],
                     in_=wv[:, 8 * half:8 * half + 8, blk * 512:(blk + 1) * 512]),
                 writes=[("w", slot, half)], dma=True)
        for j4 in range(4):
            cb = blk * 4 + j4
            for kc in range(DC):
                P.op("tensor",
                     lambda e, kc=kc, cb=cb, j4=j4, wtv=wtv: e.matmul(
                         psum_t[:, cb:cb + 1], lhsT=wtv[:, kc, j4 * 128:(j4 + 1) * 128],
                         rhs=sc[:, kc:kc + 1], start=(kc == 0), stop=(kc == DC - 1)),
                     reads=[("w", slot, kc // 8), ("sc", tag)], writes=[pkey])
    P.op("vector", lambda e: e.tensor_tensor(out=mod[:], in0=psum_t[:, 0:48], in1=bb[:], op=ALU.add),
         reads=[pkey, ("bb", tag)], writes=[("mod", tag)])
    P.op("vector", lambda e: e.scalar_tensor_tensor(
        out=A[:], in0=mod[:, 16:32], scalar=1.0, in1=gg[:], op0=ALU.add, op1=ALU.mult),
        reads=[("mod", tag), ("gg", tag)], writes=[("A", tag)])
    return A, mod, ("A", tag), ("mod", tag)


def emit_norm_mod(c, P, x_t, xkey, T, A, Akey, S_ap_fn, Skey, hn, hnkey, sq, sqkey, rstd, ps_bank, pskey):
    ones = c.ones
    for kc in range(DC):
        P.op("scalar", lambda e, kc=kc: e.activation(out=sq[:, kc, :T], in_=x_t[:, kc, :T], func=AF.Square),
             reads=[xkey], writes=[(sqkey, kc)])
    for kc in range(DC):
        P.op("tensor", lambda e, kc=kc: e.matmul(ps_bank[:, :T], lhsT=ones[:], rhs=sq[:, kc, :T],
                                                 start=(kc == 0), stop=(kc == DC - 1)),
             reads=[(sqkey, kc)], writes=[pskey])
    P.op("scalar", lambda e: e.activation(out=rstd[:, :T], in_=ps_bank[:, :T], func=AF.Sqrt,
                                          scale=1.0 / D, bias=c.eps_col[:]),
         reads=[pskey], writes=["rstd"])
    P.op("vector", lambda e: e.reciprocal(out=rstd[:, :T], in_=rstd[:, :T]), reads=["rstd"], writes=["rstd"])
    for kc in range(DC):
        P.op("vector", lambda e, kc=kc: e.tensor_tensor(out=c.ntmp[:, kc % 2, :T], in0=x_t[:, kc, :T],
                                                        in1=rstd[:, :T], op=ALU.mult),
             reads=[xkey, "rstd"], writes=[("ntmp", kc % 2)])
        P.op("scalar", lambda e, kc=kc: e.activation(out=hn[:, kc, :T], in_=c.ntmp[:, kc % 2, :T],
                                                     func=AF.Identity, scale=A[:, kc:kc + 1],
                                                     bias=S_ap_fn(kc)),
             reads=[("ntmp", kc % 2), Akey, Skey], writes=[(hnkey, kc)])


def common_setup(c):
    nc = c.nc
    c.ones = sb(c, "ones", [128, 128], BF16)
    c.eps_col = sb(c, "eps_col", [128, 1], F32)
    c.ntmp = sb(c, "ntmp", [128, 2, 512], F32)
    c.P.op("vector", lambda e: e.memset(c.ones[:], 1.0), writes=["ones"])
    c.P.op("vector", lambda e: e.memset(c.eps_col[:], EPS), writes=["eps"])


def build_ffn(ntok, final_norm=False, T=512, env=None):
    nc = mk_nc(env)
    dt_in = mk_in(nc, env)
    WQ = "sync" if (env is not None and getattr(env, "bf16w", False)) else "gpsimd"
    xin = dt_in("xT", [D, ntok])
    ccol = dt_in("ccol", [128, DC])
    ada_w = dt_in("ada_w", [D, 3 * D])
    ada_b = dt_in("ada_b", [128, 48])
    ng = dt_in("ng", [128, DC])
    w_in = dt_in("w_in", [D, 2 * FFN_H])
    w_out = dt_in("w_out", [FFN_H, D])
    fg = dt_in("fg", [128, DC])
    xout = mk_out(nc, env, "yT", [D, ntok])
    with ExitStack() as es:
        c = Ctx()
        c.nc, c.es = nc, es
        c.pfx = env.pfx if env is not None else ""
        P = Prog(nc, es, env.G if env is not None else None)
        c.P = P
        c.wslot_n = 0
        common_setup(c)
        wslots = [sb(c, f"wslot{i}", [128, 16384], BF16) for i in range(2)]
        x_t = sb(c, "x_t", [128, DC, T], F32)
        hn = sb(c, "hn", [128, DC, T], BF16)
        hT = sb(c, "hT", [128, HC, T], BF16)
        rstd = sb(c, "rstd", [128, T], F32)
        sg = sb(c, "sg", [128, 2, T], F32)
        fgt = sb(c, "fgt", [128, DC], F32)
        pbanks = [ps(c, f"pb{i}", [128, 512]) for i in range(8)]
        A, mod, Akey, modkey = emit_adaln(c, P, "f", ccol, ada_w, ada_b, ng, wslots, pbanks[7])
        if final_norm:
            P.op("sync", lambda e: e.dma_start(out=fgt[:], in_=fg), writes=["fgt"], dma=True)
        xv = xin.rearrange("(kc p) t -> p kc t", p=128)
        ov = xout.rearrange("(kc p) t -> p kc t", p=128)
        w_in_v = w_in.rearrange("(kc p) n -> p kc n", p=128)
        w_out_v = w_out.rearrange("(hc p) n -> p hc n", p=128)
        ntile = ntok // T
        out_toks = []
        for ti in range(ntile):
            t0 = ti * T
            for q in range(4):
                P.op("sync", lambda e, q=q, t0=t0: e.dma_start(out=x_t[:, 4 * q:4 * q + 4, :],
                                                             in_=xv[:, 4 * q:4 * q + 4, t0:t0 + T]),
                     writes=["x"], dma=True)
            emit_norm_mod(c, P, x_t, "x", T, A, Akey, lambda kc: mod[:, kc:kc + 1], modkey,
                          hn, "hn", hT, "hT", rstd, pbanks[6], ("psum", 6))
            nblk = FFN_H // 512
            for hb in range(nblk):
                slot_g = c.wslot_n % 2
                c.wslot_n += 1
                wg = wslots[slot_g]
                wgv = wg[:, 0:8192].rearrange("p (kc n) -> p kc n", kc=16)
                wuv = wg[:, 8192:16384].rearrange("p (kc n) -> p kc n", kc=16)
                for half in range(2):
                    P.op(WQ, lambda e, hb=hb, wgv=wgv, half=half: e.dma_start(
                        out=wgv[:, 8 * half:8 * half + 8, :],
                        in_=w_in_v[:, 8 * half:8 * half + 8, hb * 512:(hb + 1) * 512]),
                        writes=[("w", slot_g, 0)], dma=True)
                for half in range(2):
                    P.op(WQ, lambda e, hb=hb, wuv=wuv, half=half: e.dma_start(
                        out=wuv[:, 8 * half:8 * half + 8, :],
                        in_=w_in_v[:, 8 * half:8 * half + 8, FFN_H + hb * 512:FFN_H + (hb + 1) * 512]),
                        writes=[("w", slot_g, 1)], dma=True)
                for j in range(4):
                    hc = hb * 4 + j
                    pg = pbanks[(hc % 2) * 2]
                    pu = pbanks[(hc % 2) * 2 + 1]
                    kg = ("psum", (hc % 2) * 2)
                    ku = ("psum", (hc % 2) * 2 + 1)
                    for kc in range(DC):
                        P.op("tensor", lambda e, kc=kc, j=j, pg=pg, wgv=wgv: e.matmul(
                            pg[:, :T], lhsT=wgv[:, kc, j * 128:(j + 1) * 128], rhs=hn[:, kc, :T],
                            start=(kc == 0), stop=(kc == DC - 1)),
                            reads=[("w", slot_g, 0), ("hn", kc)], writes=[kg])
                    for kc in range(DC):
                        P.op("tensor", lambda e, kc=kc, j=j, pu=pu, wuv=wuv: e.matmul(
                            pu[:, :T], lhsT=wuv[:, kc, j * 128:(j + 1) * 128], rhs=hn[:, kc, :T],
                            start=(kc == 0), stop=(kc == DC - 1)),
                            reads=[("w", slot_g, 1), ("hn", kc)], writes=[ku])
                    P.op("scalar", lambda e, pg=pg, hc=hc: e.activation(out=sg[:, hc % 2, :T], in_=pg[:, :T],
                                                                       func=AF.Silu),
                         reads=[kg], writes=[("sg", hc % 2)])
                    P.op("vector", lambda e, pu=pu, hc=hc: e.tensor_tensor(out=hT[:, hc, :T], in0=pu[:, :T],
                                                                          in1=sg[:, hc % 2, :T], op=ALU.mult),
                         reads=[ku, ("sg", hc % 2)], writes=[("hT", hc)])
            for ob in range(D // 256):
                slot = c.wslot_n % 2
                c.wslot_n += 1
                wt = wslots[slot]
                wov = wt[:, 0:HC * 256].rearrange("p (hc n) -> p hc n", hc=HC)
                for half in range(2):
                    P.op(WQ, lambda e, ob=ob, wov=wov, half=half: e.dma_start(
                        out=wov[:, 22 * half:22 * half + 22, :],
                        in_=w_out_v[:, 22 * half:22 * half + 22, ob * 256:(ob + 1) * 256]),
                        writes=[("w", slot, half)], dma=True)
                for j in range(2):
                    dc = ob * 2 + j
                    pb = pbanks[4 + dc % 2]
                    kb = ("psum", 4 + dc % 2)
                    for hc in range(HC):
                        P.op("tensor", lambda e, hc=hc, j=j, pb=pb, wov=wov: e.matmul(
                            pb[:, :T], lhsT=wov[:, hc, j * 128:(j + 1) * 128], rhs=hT[:, hc, :T],
                            start=(hc == 0), stop=(hc == HC - 1)),
                            reads=[("w", slot, hc // 22), ("hT", hc)], writes=[kb])
                    P.op("vector", lambda e, dc=dc, pb=pb: e.scalar_tensor_tensor(
                        out=x_t[:, dc, :T], in0=pb[:, :T], scalar=mod[:, 32 + dc:33 + dc], in1=x_t[:, dc, :T],
                        op0=ALU.mult, op1=ALU.add),
                        reads=[kb, modkey, "x"], writes=[("xo", dc)])
            src = x_t
            rk = [("xo", dc) for dc in range(DC)]
            if final_norm:
                for kc in range(DC):
                    P.op("scalar", lambda e, kc=kc: e.activation(out=hT[:, kc, :T], in_=x_t[:, kc, :T],
                                                                 func=AF.Square),
                         reads=[("xo", kc)], writes=[("hT", kc)])
                for kc in range(DC):
                    P.op("tensor", lambda e, kc=kc: e.matmul(pbanks[6][:, :T], lhsT=c.ones[:], rhs=hT[:, kc, :T],
                                                             start=(kc == 0), stop=(kc == DC - 1)),
                         reads=[("hT", kc)], writes=[("psum", 6)])
                P.op("scalar", lambda e: e.activation(out=rstd[:, :T], in_=pbanks[6][:, :T], func=AF.Sqrt,
                                                      scale=1.0 / D, bias=c.eps_col[:]),
                     reads=[("psum", 6)], writes=["rstd"])
                P.op("vector", lambda e: e.reciprocal(out=rstd[:, :T], in_=rstd[:, :T]),
                     reads=["rstd"], writes=["rstd"])
                for kc in range(DC):
                    P.op("vector", lambda e, kc=kc: e.scalar_tensor_tensor(
                        out=x_t[:, kc, :T], in0=x_t[:, kc, :T], scalar=fgt[:, kc:kc + 1], in1=rstd[:, :T],
                        op0=ALU.mult, op1=ALU.mult),
                        reads=[("xo", kc), "rstd", "fgt"], writes=[("xo", kc)])
            for q in range(4):
                tok = P.op("sync", lambda e, q=q, t0=t0: e.dma_start(out=ov[:, 4 * q:4 * q + 4, t0:t0 + T],
                                                                   in_=x_t[:, 4 * q:4 * q + 4, :]),
                           reads=[("xo", dc) for dc in range(4 * q, 4 * q + 4)] + ["x"], writes=["x_st"], dma=True)
                out_toks.append(tok)
        P.finalize(final_waits=out_toks[-8:])
    return nc


def col_layout(v):
    v = np.asarray(v, dtype=np.float32)
    return np.ascontiguousarray(v.reshape(-1, 128).T)


RET_H = 8
import math
LOGG = [math.log(1.0 - 2.0 ** (-5.0 - h)) for h in range(RET_H)]


def ret_consts():
    J = np.arange(128)[:, None]
    I = np.arange(128)[None, :]
    E = np.abs(I - J).astype(np.float32)
    M01 = np.ones((128, 128), np.float32)
    M01[64:, :64] = 0.0
    return {
        "ident": np.eye(128, dtype=np.float32),
        "Emat": E, "M01": M01,
        "irow1": np.broadcast_to((np.arange(128) + 1).astype(np.float32), (128, 128)).copy(),
        "jcol": np.repeat((127 - np.arange(128)).astype(np.float32).reshape(128, 1), 16, axis=1),
        "pidx": np.repeat(np.arange(128, dtype=np.float32).reshape(128, 1), 16, axis=1),
        "iota512": np.broadcast_to(np.arange(512, dtype=np.float32), (128, 512)).copy(),
    }


def build_ret(ntok, T=256, dbg=9, env=None):
    nc = mk_nc(env)
    dt_in = mk_in(nc, env)
    WQ = "sync" if (env is not None and getattr(env, "bf16w", False)) else "gpsimd"
    xin = dt_in("xT", [D, ntok])
    ccol = dt_in("ccol", [128, DC])
    ada_w = dt_in("ada_w", [D, 3 * D])
    ada_b = dt_in("ada_b", [128, 48])
    ng = dt_in("ng", [128, DC])
    w_in = dt_in("w_in", [D, 12288])
    w_out = dt_in("w_out", [4096, D])
    gng = dt_in("gng", [1, 4096])
    ident_d = dt_in("ident", [128, 128])
    E_d = dt_in("Emat", [128, 128])
    M01_d = dt_in("M01", [128, 128])
    irow1_d = dt_in("irow1", [128, 128])
    jcol_d = dt_in("jcol", [128, 16])
    pidx_d = dt_in("pidx", [128, 16])
    iota_d = dt_in("iota512", [128, 512])
    xout = mk_out(nc, env, "yT", [D, ntok])
    NS = T // 128
    with ExitStack() as es:
        c = Ctx()
        c.nc, c.es = nc, es
        c.pfx = env.pfx if env is not None else ""
        P = Prog(nc, es, env.G if env is not None else None)
        c.P = P
        c.wslot_n = 0
        common_setup(c)
        wslots = [sb(c, f"wslot{i}", [128, 8192], BF16) for i in range(3)]
        x_t = sb(c, "x_t", [128, DC, T], F32)
        yT = sb(c, "yT_sb", [128, 32, T], BF16)
        hn = sb(c, "hn", [128, DC, T], BF16)
        rstd = sb(c, "rstd", [128, T], F32)
        r32 = sb(c, "r32", [128, RET_H * 2, 512], F32)
        rbf = sb(c, "rbf", [128, RET_H * 2, 512], BF16)
        ident32 = sb(c, "ident32", [128, 128], F32)
        ident = sb(c, "ident", [128, 128], BF16)
        Et = sb(c, "Et", [128, 128], F32)
        M01 = sb(c, "M01", [128, 128], F32)
        irow1 = sb(c, "irow1", [128, 128], F32)
        jcol = sb(c, "jcol", [128, 16], F32)
        pidx = sb(c, "pidx", [128, 16], F32)
        iota = sb(c, "iota", [128, 512], F32)
        invf = sb(c, "invf", [128, 1], F32)
        pi_col = sb(c, "pi_col", [128, 1], F32)
        maskT = sb(c, "maskT", [128, RET_H, 128], F32)
        xi = sb(c, "xi", [128, RET_H, 128], F32)
        zeta = sb(c, "zeta", [128, RET_H], F32)
        gngt = sb(c, "gngt", [128, 4096], F32)
        cosT = sb(c, "cosT", [128, T], F32)
        sinT = sb(c, "sinT", [128, T], F32)
        ang = sb(c, "ang", [128, 2, T], F32)
        rt = sb(c, "rt", [128, 4, T], F32)
        qT = sb(c, "qT", [128, 2, T], BF16)
        kT = sb(c, "kT", [128, 2, T], BF16)
        qxT = sb(c, "qxT", [128, 2, T], BF16)
        ktok = sb(c, "ktok", [128, NS, 256], BF16)
        vtok = sb(c, "vtok", [128, NS, 512], BF16)
        gsil = sb(c, "gsil", [128, NS, 512], BF16)
        sm2 = [sb(c, "sm_" + str(_q), [128, 128], BF16) for _q in range(2)]
        osb2 = [sb(c, "osb_" + str(_q), [128, 512], F32) for _q in range(2)]
        osq2 = [sb(c, "osq_" + str(_q), [128, 512], F32) for _q in range(2)]
        y12 = [sb(c, "y1_" + str(_q), [128, 512], F32) for _q in range(2)]
        y22 = [sb(c, "y2_" + str(_q), [128, 512], F32) for _q in range(2)]
        y32 = [sb(c, "y3_" + str(_q), [128, 512], BF16) for _q in range(2)]
        st2 = [sb(c, "st_" + str(_q), [128, 8], F32) for _q in range(2)]
        xres = sb(c, "xres", [128, 2, T], F32)
        pb = [ps(c, f"pb{i}", [128, 512]) for i in range(3)]
        pTs = [ps(c, f"pT{i}", [128, 1024], BF16) for i in range(2)]
        po = ps(c, "po", [128, 512])
        pst = [ps(c, f"pst{i}", [128, 512]) for i in range(2)]
        A, mod, Akey, modkey = emit_adaln(c, P, "m", ccol, ada_w, ada_b, ng, wslots, pb[2], ("ps", 2))
        for (t_, d_, k_) in [(ident32, ident_d, "ident32"), (Et, E_d, "Et"), (M01, M01_d, "M01"),
                             (irow1, irow1_d, "irow1"), (jcol, jcol_d, "jcol"), (pidx, pidx_d, "pidx"),
                             (iota, iota_d, "iota")] if dbg >= -4 else []:
            P.op("sync", lambda e, t_=t_, d_=d_: e.dma_start(out=t_[:], in_=d_), writes=[k_], dma=True)
        for q in range(4 if dbg != 0 else 0):
            P.op("sync", lambda e, q=q: e.dma_start(out=gngt[:, q * 1024:(q + 1) * 1024],
                                                   in_=gng[0:1, q * 1024:(q + 1) * 1024].partition_broadcast(128)),
                 writes=[("gngt", q)], dma=True)
        if dbg >= -4:
            P.op("vector", lambda e: e.tensor_copy(out=ident[:], in_=ident32[:]), reads=["ident32"], writes=["ident"])
        if dbg >= -1:
            P.op("vector", lambda e: e.memset(pi_col[:], math.pi), writes=["pi"])
            P.op("vector", lambda e: e.memset(r32[:], 0.0), writes=["r32"])
            P.op("vector", lambda e: e.memset(rbf[:], 0.0), writes=["rbf"])
        if dbg >= -2:
            P.op("scalar", lambda e: e.activation(out=invf[:], in_=pidx[:, 0:1], func=AF.Exp,
                                              scale=-math.log(10000.0) / 128.0),
                 reads=["pidx"], writes=["invf"])
        for h in range(RET_H if dbg >= -3 else 0):
            P.op("scalar", lambda e, h=h: e.activation(out=maskT[:, h, :], in_=Et[:], func=AF.Exp, scale=LOGG[h]),
                 reads=["Et"], writes=[("mask", h)])
            P.op("vector", lambda e, h=h: e.tensor_tensor(out=maskT[:, h, :], in0=maskT[:, h, :], in1=M01[:],
                                                          op=ALU.mult),
                 reads=[("mask", h), "M01"], writes=[("mask", h)])
            P.op("scalar", lambda e, h=h: e.activation(out=xi[:, h, :], in_=irow1[:], func=AF.Exp, scale=LOGG[h]),
                 reads=["irow1"], writes=[("xi", h)])
            P.op("scalar", lambda e, h=h: e.activation(out=zeta[:, h:h + 1], in_=jcol[:, 0:1], func=AF.Exp,
                                                       scale=LOGG[h]),
                 reads=["jcol"], writes=[("zeta", h)])
        xv = xin.rearrange("(kc p) t -> p kc t", p=128)
        ov = xout.rearrange("(kc p) t -> p kc t", p=128)
        w_in_v = w_in.rearrange("(kc p) n -> p kc n", p=128)
        w_out_v = w_out.rearrange("(mc p) n -> p mc n", p=128)
        out_toks = []
        c.chain_n = 0
        TWO_PI = 2.0 * math.pi

        def load_w(c0, ncols, key_extra):
            slot = c.wslot_n % 3
            c.wslot_n += 1
            wt = wslots[slot]
            v = wt[:, 0:16 * ncols].rearrange("p (kc n) -> p kc n", kc=16)
            for half in range(2):
                P.op(WQ, lambda e, v=v, half=half, c0=c0, ncols=ncols: e.dma_start(
                    out=v[:, 8 * half:8 * half + 8, :], in_=w_in_v[:, 8 * half:8 * half + 8, c0:c0 + ncols]),
                    writes=[("w", slot, half)], dma=True)
            return v, slot

        for ti in range(ntok // T):
            t0 = ti * T
            for q in range(4):
                P.op("sync", lambda e, q=q, t0=t0: e.dma_start(out=x_t[:, 4 * q:4 * q + 4, :],
                                                             in_=xv[:, 4 * q:4 * q + 4, t0:t0 + T]),
                     writes=["x"], dma=True)
            emit_norm_mod(c, P, x_t, "x", T, A, Akey, lambda kc: mod[:, kc:kc + 1], modkey,
                          hn, "hn", yT, "yT", rstd, pb[2], ("ps", 2))
            if dbg >= 1:
              P.op("vector", lambda e, t0=t0: e.tensor_scalar(out=ang[:, 0, :], in0=iota[:, :T], scalar1=float(t0),
                                                            scalar2=invf[:, 0:1], op0=ALU.add, op1=ALU.mult),
                 reads=["iota", "invf"], writes=[("ang", 0)])
            if dbg >= 1:
              P.op("vector", lambda e: e.tensor_scalar(out=ang[:, 1, :], in0=ang[:, 0, :], scalar1=0.5 * math.pi,
                                                     scalar2=None, op0=ALU.add),
                 reads=[("ang", 0)], writes=[("ang", 1)])
            MAGIC = 12582912.0
            for w_, dst_, dk_ in [(0, sinT, "sinT"), (1, cosT, "cosT")] if dbg >= 1 else []:
                P.op("vector", lambda e, w_=w_: e.tensor_scalar(out=rt[:, w_, :], in0=ang[:, w_, :],
                                                                scalar1=1.0 / TWO_PI, scalar2=MAGIC,
                                                                op0=ALU.mult, op1=ALU.add),
                     reads=[("ang", w_)], writes=[("rt", w_)])
                P.op("vector", lambda e, w_=w_: e.tensor_scalar(out=rt[:, w_, :], in0=rt[:, w_, :],
                                                                scalar1=MAGIC, scalar2=None, op0=ALU.subtract),
                     reads=[("rt", w_)], writes=[("rt", w_)])
                P.op("vector", lambda e, w_=w_: e.scalar_tensor_tensor(out=rt[:, w_, :], in0=rt[:, w_, :],
                                                                       scalar=-TWO_PI, in1=ang[:, w_, :],
                                                                       op0=ALU.mult, op1=ALU.add),
                     reads=[("rt", w_), ("ang", w_)], writes=[("rt", w_)])
                P.op("vector", lambda e, w_=w_: e.tensor_scalar(out=rt[:, w_, :], in0=rt[:, w_, :],
                                                                scalar1=-math.pi, scalar2=math.pi,
                                                                op0=ALU.max, op1=ALU.min),
                     reads=[("rt", w_)], writes=[("rt", w_)])
                P.op("scalar", lambda e, w_=w_, dst_=dst_: e.activation(out=dst_[:], in_=rt[:, w_, :], func=AF.Sin),
                     reads=[("rt", w_)], writes=[dk_])
            hnkeys = [("hn", kc) for kc in range(DC)]
            for h in range(RET_H if dbg >= 2 else 0):
                wqk, sqk = load_w(h * 256, 256, None)
                wk_, sk_ = load_w(2048 + h * 256, 256, None)
                for which, (wv_, sl_) in enumerate([(wqk, sqk), (wk_, sk_)]):
                    for cch in range(2):
                        bank = pb[cch]
                        for kc in range(DC):
                            P.op("tensor", lambda e, kc=kc, cch=cch, wv_=wv_, bank=bank: e.matmul(
                                bank[:, :T], lhsT=wv_[:, kc, cch * 128:(cch + 1) * 128], rhs=hn[:, kc, :T],
                                start=(kc == 0), stop=(kc == DC - 1)),
                                reads=[("w", sl_, kc // 8), ("hn", kc)], writes=[("ps", cch)])
                    scl = 1.0 if which == 0 else 1.0 / 16.0
                    dst = qT if which == 0 else kT
                    dkey = "qT" if which == 0 else "kT"
                    P.op("vector", lambda e, scl=scl: e.scalar_tensor_tensor(
                        out=rt[:, 0, :], in0=pb[0][:, :T], scalar=scl, in1=cosT[:], op0=ALU.mult, op1=ALU.mult),
                        reads=[("ps", 0), "cosT"], writes=[("rt", 0)])
                    P.op("vector", lambda e, scl=scl: e.scalar_tensor_tensor(
                        out=rt[:, 1, :], in0=pb[1][:, :T], scalar=scl, in1=sinT[:], op0=ALU.mult, op1=ALU.mult),
                        reads=[("ps", 1), "sinT"], writes=[("rt", 1)])
                    P.op("vector", lambda e, scl=scl: e.scalar_tensor_tensor(
                        out=rt[:, 2, :], in0=pb[0][:, :T], scalar=scl, in1=sinT[:], op0=ALU.mult, op1=ALU.mult),
                        reads=[("ps", 0), "sinT"], writes=[("rt", 2)])
                    P.op("vector", lambda e, scl=scl: e.scalar_tensor_tensor(
                        out=rt[:, 3, :], in0=pb[1][:, :T], scalar=scl, in1=cosT[:], op0=ALU.mult, op1=ALU.mult),
                        reads=[("ps", 1), "cosT"], writes=[("rt", 3)])
                    P.op("gpsimd", lambda e, dst=dst: e.tensor_tensor(out=dst[:, 0, :], in0=rt[:, 0, :],
                                                                      in1=rt[:, 1, :], op=ALU.subtract),
                         reads=[("rt", 0), ("rt", 1)], writes=[(dkey, 0)])
                    P.op("gpsimd", lambda e, dst=dst: e.tensor_tensor(out=dst[:, 1, :], in0=rt[:, 2, :],
                                                                      in1=rt[:, 3, :], op=ALU.add),
                         reads=[("rt", 2), ("rt", 3)], writes=[(dkey, 1)])
                for cch in range(2):
                    P.op("gpsimd", lambda e, cch=cch, h=h: e.tensor_tensor(
                        out=qxT[:, cch, :].rearrange("p (s i) -> p s i", s=NS),
                        in0=qT[:, cch, :].rearrange("p (s i) -> p s i", s=NS),
                        in1=xi[:, h:h + 1, :].broadcast_to([128, NS, 128]), op=ALU.mult),
                        reads=[("qT", cch), ("xi", h)], writes=[("qxT", cch)])
                for s in range(NS):
                    for cch in range(2):
                        P.op("tensor", lambda e, s=s, cch=cch: e.transpose(
                            pTs[cch][:, 0:128], kT[:, cch, s * 128:(s + 1) * 128], ident[:]),
                            reads=[("kT", cch), "ident"], writes=[("pT", cch)])
                        P.op("scalar", lambda e, s=s, cch=cch, h=h: e.activation(
                            out=ktok[:, s, cch * 128:(cch + 1) * 128], in_=pTs[cch][:, 0:128], func=AF.Copy,
                            scale=zeta[:, h:h + 1]),
                            reads=[("pT", cch), ("zeta", h)], writes=[("ktok", s)])
                if dbg < 3:
                    continue
                wv2, sv2 = load_w(4096 + h * 512, 512, None)
                wg2, sg2 = load_w(8192 + h * 512, 512, None)
                for s in range(NS):
                    bank = pb[2]
                    for kc in range(DC):
                        P.op("tensor", lambda e, kc=kc, s=s, bank=bank, wv2=wv2: e.matmul(
                            bank[:, :], lhsT=hn[:, kc, s * 128:(s + 1) * 128], rhs=wv2[:, kc, :],
                            start=(kc == 0), stop=(kc == DC - 1)),
                            reads=[("w", sv2, kc // 8), ("hn", kc)], writes=[("ps", 2)])
                    P.op("scalar", lambda e, s=s, bank=bank: e.copy(out=vtok[:, s, :], in_=bank[:, :]),
                         reads=[("ps", 2)], writes=[("vtok", s)])
                for s in range(NS):
                    bank = pb[2]
                    for kc in range(DC):
                        P.op("tensor", lambda e, kc=kc, s=s, bank=bank, wg2=wg2: e.matmul(
                            bank[:, :], lhsT=hn[:, kc, s * 128:(s + 1) * 128], rhs=wg2[:, kc, :],
                            start=(kc == 0), stop=(kc == DC - 1)),
                            reads=[("w", sg2, kc // 8), ("hn", kc)], writes=[("ps", 2)])
                    P.op("scalar", lambda e, s=s, bank=bank: e.activation(out=gsil[:, s, :], in_=bank[:, :],
                                                                         func=AF.Silu),
                         reads=[("ps", 2)], writes=[("gsil", s)])
                for s in range(NS if dbg >= 4 else 0):
                    sl = slice(s * 128, (s + 1) * 128)
                    cq = c.chain_n % 2
                    c.chain_n += 1
                    sm, osb, osq, y1, y2, y3, st = sm2[cq], osb2[cq], osq2[cq], y12[cq], y22[cq], y32[cq], st2[cq]
                    for cch in range(2):
                        P.op("tensor", lambda e, cch=cch, sl=sl, sm=sm, osb=osb, osq=osq, y1=y1, y2=y2, y3=y3, st=st: e.matmul(
                            pb[0][:, 0:128], lhsT=kT[:, cch, sl], rhs=qT[:, cch, sl],
                            start=(cch == 0), stop=(cch == 1)),
                            reads=[("kT", cch), ("qT", cch)], writes=[("ps", 0)])
                    P.op("vector", lambda e, h=h, sm=sm, osb=osb, osq=osq, y1=y1, y2=y2, y3=y3, st=st: e.tensor_tensor(out=sm[:], in0=pb[0][:, 0:128],
                                                                  in1=maskT[:, h, :], op=ALU.mult),
                         reads=[("ps", 0), ("mask", h)], writes=[("sm", cq)])
                    P.op("tensor", lambda e, s=s, sm=sm, osb=osb, osq=osq, y1=y1, y2=y2, y3=y3, st=st: e.matmul(po[:, :], lhsT=sm[:], rhs=vtok[:, s, :],
                                                          start=True, stop=False),
                         reads=[("sm", cq), ("vtok", s)], writes=["po"])
                    for cch in range(2):
                        P.op("tensor", lambda e, cch=cch, sl=sl, h=h, sm=sm, osb=osb, osq=osq, y1=y1, y2=y2, y3=y3, st=st: e.matmul(
                            po[:, :], lhsT=qxT[:, cch, sl], rhs=rbf[:, h * 2 + cch, :],
                            start=False, stop=(cch == 1)),
                            reads=[("qxT", cch), ("rbf", h, cch)], writes=["po"])
                    for cch in range(2):
                        P.op("tensor", lambda e, cch=cch, s=s, sm=sm, osb=osb, osq=osq, y1=y1, y2=y2, y3=y3, st=st: e.matmul(
                            pst[cch][:, :], lhsT=ktok[:, s, cch * 128:(cch + 1) * 128], rhs=vtok[:, s, :],
                            start=True, stop=True),
                            reads=[("ktok", s), ("vtok", s)], writes=[("pst", cch)])
                        P.op("vector", lambda e, cch=cch, h=h, sm=sm, osb=osb, osq=osq, y1=y1, y2=y2, y3=y3, st=st: e.scalar_tensor_tensor(
                            out=r32[:, h * 2 + cch, :], in0=r32[:, h * 2 + cch, :], scalar=math.exp(LOGG[h] * 128),
                            in1=pst[cch][:, :], op0=ALU.mult, op1=ALU.add),
                            reads=[("pst", cch), ("r32", h, cch), "r32"], writes=[("r32", h, cch)])
                        P.op("scalar", lambda e, cch=cch, h=h, sm=sm, osb=osb, osq=osq, y1=y1, y2=y2, y3=y3, st=st: e.copy(out=rbf[:, h * 2 + cch, :],
                                                                    in_=r32[:, h * 2 + cch, :]),
                             reads=[("r32", h, cch), "rbf"], writes=[("rbf", h, cch)])
                    P.op("scalar", lambda e, sm=sm, osb=osb, osq=osq, y1=y1, y2=y2, y3=y3, st=st: e.copy(out=osb[:], in_=po[:, :]), reads=["po"], writes=[("osb", cq)])
                    P.op("vector", lambda e, sm=sm, osb=osb, osq=osq, y1=y1, y2=y2, y3=y3, st=st: e.reduce_sum(out=st[:, 0:1], in_=osb[:], axis=AX.X),
                         reads=[("osb", cq)], writes=[("st", cq, 0)])
                    P.op("scalar", lambda e, sm=sm, osb=osb, osq=osq, y1=y1, y2=y2, y3=y3, st=st: e.activation(out=osq[:], in_=osb[:], func=AF.Square),
                         reads=[("osb", cq)], writes=[("osq", cq)])
                    P.op("vector", lambda e, sm=sm, osb=osb, osq=osq, y1=y1, y2=y2, y3=y3, st=st: e.reduce_sum(out=st[:, 1:2], in_=osq[:], axis=AX.X),
                         reads=[("osq", cq)], writes=[("st", cq, 1)])
                    P.op("vector", lambda e, sm=sm, osb=osb, osq=osq, y1=y1, y2=y2, y3=y3, st=st: e.tensor_scalar(out=st[:, 2:3], in0=st[:, 0:1], scalar1=1.0 / 512,
                                                             scalar2=None, op0=ALU.mult),
                         reads=[("st", cq, 0)], writes=[("st", cq, 2)])
                    P.op("vector", lambda e, sm=sm, osb=osb, osq=osq, y1=y1, y2=y2, y3=y3, st=st: e.tensor_tensor(out=st[:, 3:4], in0=st[:, 2:3], in1=st[:, 2:3],
                                                             op=ALU.mult),
                         reads=[("st", cq, 2)], writes=[("st", cq, 3)])
                    P.op("vector", lambda e, sm=sm, osb=osb, osq=osq, y1=y1, y2=y2, y3=y3, st=st: e.scalar_tensor_tensor(out=st[:, 4:5], in0=st[:, 1:2],
                                                                    scalar=1.0 / 512, in1=st[:, 3:4],
                                                                    op0=ALU.mult, op1=ALU.subtract),
                         reads=[("st", cq, 1), ("st", cq, 3)], writes=[("st", cq, 4)])
                    P.op("scalar", lambda e, sm=sm, osb=osb, osq=osq, y1=y1, y2=y2, y3=y3, st=st: e.activation(out=st[:, 5:6], in_=st[:, 4:5], func=AF.Sqrt,
                                                          bias=c.eps_col[:]),
                         reads=[("st", cq, 4), "eps"], writes=[("st", cq, 5)])
                    P.op("vector", lambda e, sm=sm, osb=osb, osq=osq, y1=y1, y2=y2, y3=y3, st=st: e.reciprocal(out=st[:, 6:7], in_=st[:, 5:6]),
                         reads=[("st", cq, 5)], writes=[("st", cq, 6)])
                    P.op("vector", lambda e, sm=sm, osb=osb, osq=osq, y1=y1, y2=y2, y3=y3, st=st: e.tensor_scalar(out=y1[:], in0=osb[:], scalar1=st[:, 2:3],
                                                             scalar2=st[:, 6:7], op0=ALU.subtract, op1=ALU.mult),
                         reads=[("osb", cq), ("st", cq, 2), ("st", cq, 6)], writes=[("y1", cq)])
                    P.op("gpsimd", lambda e, h=h, sm=sm, osb=osb, osq=osq, y1=y1, y2=y2, y3=y3, st=st: e.tensor_tensor(out=y2[:], in0=y1[:],
                                                                  in1=gngt[:, h * 512:(h + 1) * 512], op=ALU.mult),
                         reads=[("y1", cq), ("gngt", h // 2)], writes=[("y2", cq)])
                    P.op("gpsimd", lambda e, s=s, sm=sm, osb=osb, osq=osq, y1=y1, y2=y2, y3=y3, st=st: e.tensor_tensor(out=y3[:], in0=y2[:], in1=gsil[:, s, :],
                                                                  op=ALU.mult),
                         reads=[("y2", cq), ("gsil", s)], writes=[("y3", cq)])
                    for ec in range(4):
                        P.op("tensor", lambda e, ec=ec, sm=sm, osb=osb, osq=osq, y1=y1, y2=y2, y3=y3, st=st: e.transpose(pTs[ec % 2][:, 0:128],
                                                                     y3[:, ec * 128:(ec + 1) * 128], ident[:]),
                             reads=[("y3", cq), "ident"], writes=[("pT", ec % 2)])
                        P.op("vector", lambda e, ec=ec, h=h, sl=sl, sm=sm, osb=osb, osq=osq, y1=y1, y2=y2, y3=y3, st=st: e.tensor_copy(
                            out=yT[:, h * 4 + ec, sl], in_=pTs[ec % 2][:, 0:128]),
                            reads=[("pT", ec % 2)], writes=[("yT", h * 4 + ec)])
            if dbg < 5:
                for dc in range(DC):
                    P.op("vector", lambda e, dc=dc: e.tensor_copy(out=xres[:, dc % 2, :], in_=x_t[:, dc, :T]),
                         reads=["x"], writes=[("xres", dc % 2)])
                    tok = P.op("sync", lambda e, dc=dc, t0=t0: e.dma_start(out=ov[:, dc, t0:t0 + T],
                                                                         in_=xres[:, dc % 2, :]),
                               reads=[("xres", dc % 2)], writes=["x_st"], dma=True)
                    out_toks.append(tok)
            for ob in range(D // 256 if dbg >= 5 else 0):
                slot = c.wslot_n % 3
                c.wslot_n += 1
                wt = wslots[slot]
                wov = wt[:, 0:32 * 256].rearrange("p (mc n) -> p mc n", mc=32)
                for half in range(2):
                    P.op(WQ, lambda e, ob=ob, wov=wov, half=half: e.dma_start(
                        out=wov[:, 16 * half:16 * half + 16, :],
                        in_=w_out_v[:, 16 * half:16 * half + 16, ob * 256:(ob + 1) * 256]),
                        writes=[("w", slot, half)], dma=True)
                for j in range(2):
                    dc = ob * 2 + j
                    bank = pb[dc % 2]
                    for mc in range(32):
                        P.op("tensor", lambda e, mc=mc, j=j, bank=bank, wov=wov: e.matmul(
                            bank[:, :T], lhsT=wov[:, mc, j * 128:(j + 1) * 128], rhs=yT[:, mc, :T],
                            start=(mc == 0), stop=(mc == 31)),
                            reads=[("w", slot, mc // 16), ("yT", mc)], writes=[("ps", dc % 2)])
                    P.op("vector", lambda e, dc=dc, bank=bank: e.scalar_tensor_tensor(
                        out=xres[:, dc % 2, :], in0=bank[:, :T], scalar=mod[:, 32 + dc:33 + dc],
                        in1=x_t[:, dc, :T], op0=ALU.mult, op1=ALU.add),
                        reads=[("ps", dc % 2), modkey, "x"], writes=[("xres", dc % 2)])
                    tok = P.op("sync", lambda e, dc=dc, t0=t0: e.dma_start(out=ov[:, dc, t0:t0 + T],
                                                                         in_=xres[:, dc % 2, :]),
                               reads=[("xres", dc % 2)], writes=["x_st"], dma=True)
                    out_toks.append(tok)
        P.finalize(final_waits=out_toks[-8:])
    return nc


SSD_H = 48
IN0 = 9264


def ssd_consts():
    k = np.arange(128)[:, None]
    l = np.arange(128)[None, :]
    triU = (k <= l).astype(np.float32)
    negm = np.where(l >= k, 0.0, -30000.0).astype(np.float32)
    return {"ident": np.eye(128, dtype=np.float32), "triU": triU,
            "negm4": np.ascontiguousarray(np.tile(negm, (1, 4)))}


def bc3(ap, n_outer, n_inner):
    return ap.rearrange("p (h o) -> p h o", o=1).broadcast_to([128, n_outer, n_inner])


def build_ssd(ntok, T=256, use_barrier=False, max_tiles=None, env=None):
    nc = mk_nc(env)
    dt_in = mk_in(nc, env)
    WQ = "sync" if (env is not None and getattr(env, "bf16w", False)) else "gpsimd"
    xin = dt_in("xT", [D, ntok])
    ccol = dt_in("ccol", [128, DC])
    ada_w = dt_in("ada_w", [D, 3 * D])
    ada_b = dt_in("ada_b", [128, 48])
    ng = dt_in("ng", [128, DC])
    w_in = dt_in("w_in", [D, IN0])
    cw_d = dt_in("cw", [128, 40 * 4])
    cb_d = dt_in("cb", [128, 40])
    dtb_d = dt_in("dtb", [1, 48])
    alog_d = dt_in("alog", [1, 48])
    dsk_d = dt_in("dsk", [1, 48])
    sng_d = dt_in("sng", [128, 24])
    ident_d = dt_in("ident", [128, 128])
    triU_d = dt_in("triU", [128, 128])
    negm_d = dt_in("negm4", [128, 512])
    yout = mk_out(nc, env, "yT", [3072, ntok])
    NS = T // 128
    with ExitStack() as es:
        c = Ctx()
        c.nc, c.es = nc, es
        c.pfx = env.pfx if env is not None else ""
        P = Prog(nc, es, env.G if env is not None else None)
        c.P = P
        c.wslot_n = 0
        common_setup(c)
        wslots = [sb(c, f"wslot{i}", [128, 8192], BF16) for i in range(3)]
        x_t = sb(c, "x_t", [128, DC, T], F32)
        hn = sb(c, "hn", [128, DC, T], BF16)
        sq = sb(c, "sq", [128, DC, T], BF16)
        rstd = sb(c, "rstd", [128, T], F32)
        ya = sb(c, "ya", [128, 24, T], F32)
        ST32 = sb(c, "ST32", [128, 3072], F32)
        STbf = sb(c, "STbf", [128, 3072], BF16)
        xs_tok = sb(c, "xs_tok", [128, NS, 3072], BF16)
        zs_tok = sb(c, "zs_tok", [128, NS, 3072], BF16)
        BT = sb(c, "BT", [128, 8, T], BF16)
        CT = sb(c, "CT", [128, 8, T], BF16)
        Btok = sb(c, "Btok", [128, NS, 8, 128], BF16)
        xsT = sb(c, "xsT", [128, 2, T], BF16)
        hist = sb(c, "hist", [128, 40, 3], F32)
        pre = sb(c, "pre", [128, 2, T + 3], F32)
        acc = sb(c, "acc", [128, 2, T], F32)
        cw = sb(c, "cw", [128, 160], F32)
        cb = sb(c, "cb", [128, 40], F32)
        dtb = sb(c, "dtb", [128, 48], F32)
        abc = sb(c, "abc", [128, 48], F32)
        dsk = sb(c, "dsk", [128, 48], F32)
        sng = sb(c, "sng", [128, 24], F32)
        ident32 = sb(c, "ident32", [128, 128], F32)
        ident = sb(c, "identb", [128, 128], BF16)
        triU = sb(c, "triU", [128, 128], F32)
        negm = sb(c, "negm", [128, 512], F32)
        ones32 = sb(c, "ones32", [128, 128], F32)
        sm48 = sb(c, "sm48", [128, 10, 48], F32)
        R4 = sb(c, "R4", [128, 2, 512], F32)
        Eh = sb(c, "Eh", [128, 2, 128], F32)
        SL = sb(c, "SL", [128, 2, 6, 128], BF16)
        xdt = sb(c, "xdt", [128, 2, 384], BF16)
        xdd = sb(c, "xdd", [128, 2, 384], BF16)
        tmpa = sb(c, "tmpa", [128, 2, 384], F32)
        tmpb = sb(c, "tmpb", [128, 2, 384], F32)
        tmpc = sb(c, "tmpc", [128, 2, 384], F32)
        y3 = sb(c, "y3", [128, 2, 384], BF16)
        st = sb(c, "st", [128, 2, 4], F32)
        pb = [ps(c, f"pb{i}", [128, 512]) for i in range(7)]
        pT = ps(c, "pT", [128, 1024], BF16)
        A, mod, Akey, modkey = emit_adaln(c, P, "m", ccol, ada_w, ada_b, ng, wslots, pb[2], ("ps", 2))
        for (t_, d_, k_) in [(ident32, ident_d, "ident32"), (triU, triU_d, "triU"), (negm, negm_d, "negm"),
                             (cw, cw_d, "cw"), (cb, cb_d, "cb"), (sng, sng_d, "sng")]:
            P.op("sync", lambda e, t_=t_, d_=d_: e.dma_start(out=t_[:], in_=d_), writes=[k_], dma=True)
        for (t_, d_, k_) in [(dtb, dtb_d, "dtb"), (abc, alog_d, "abc"), (dsk, dsk_d, "dsk")]:
            P.op("sync", lambda e, t_=t_, d_=d_: e.dma_start(out=t_[:], in_=d_.partition_broadcast(128)),
                 writes=[k_], dma=True)
        P.op("vector", lambda e: e.tensor_copy(out=ident[:], in_=ident32[:]), reads=["ident32"], writes=["ident"])
        P.op("vector", lambda e: e.memset(ones32[:], 1.0), writes=["ones32"])
        P.op("vector", lambda e: e.memset(hist[:], 0.0), writes=["hist"])
        P.op("vector", lambda e: e.memset(ST32[:], 0.0), writes=["ST32"])
        P.op("vector", lambda e: e.memset(STbf[:], 0.0), writes=["STbf"])
        P.op("scalar", lambda e: e.activation(out=abc[:], in_=abc[:], func=AF.Exp), reads=["abc"], writes=["abc"])
        P.op("vector", lambda e: e.tensor_scalar(out=abc[:], in0=abc[:], scalar1=-1.0, scalar2=None, op0=ALU.mult),
             reads=["abc"], writes=["abc"])
        xv = xin.rearrange("(kc p) t -> p kc t", p=128)
        ov = yout.rearrange("(mc p) t -> p mc t", p=128)
        w_in_v = w_in.rearrange("(kc p) n -> p kc n", p=128)
        out_toks = []
        c.use_barrier = use_barrier

        def load_w(c0, ncols):
            slot = c.wslot_n % 3
            c.wslot_n += 1
            wt = wslots[slot]
            v = wt[:, 0:16 * ncols].rearrange("p (kc n) -> p kc n", kc=16)
            for half in range(2):
                P.op(WQ, lambda e, v=v, half=half, c0=c0, ncols=ncols: e.dma_start(
                    out=v[:, 8 * half:8 * half + 8, :], in_=w_in_v[:, 8 * half:8 * half + 8, c0:c0 + ncols]),
                    writes=[("w", slot, half)], dma=True)
            return v, slot

        def transpose_to(src_ap, dst_ap, rkeys, wkeys, scale=None, eng="scalar"):
            P.op("tensor", lambda e: e.transpose(pT[:, 0:128], src_ap, ident[:]),
                 reads=list(rkeys) + ["ident"], writes=["pT"])
            if scale is None:
                if eng == "scalar":
                    P.op("scalar", lambda e: e.copy(out=dst_ap, in_=pT[:, 0:128]), reads=["pT"], writes=list(wkeys))
                else:
                    P.op("vector", lambda e: e.tensor_copy(out=dst_ap, in_=pT[:, 0:128]), reads=["pT"],
                         writes=list(wkeys))
            else:
                P.op("scalar", lambda e: e.activation(out=dst_ap, in_=pT[:, 0:128], func=AF.Copy, scale=scale),
                     reads=["pT", "sng"], writes=list(wkeys))

        for ti in range(ntok // T if max_tiles is None else max_tiles):
            t0 = ti * T
            for q in range(4):
                P.op("sync", lambda e, q=q, t0=t0: e.dma_start(out=x_t[:, 4 * q:4 * q + 4, :],
                                                             in_=xv[:, 4 * q:4 * q + 4, t0:t0 + T]),
                     writes=["x"], dma=True)
            emit_norm_mod(c, P, x_t, "x", T, A, Akey, lambda kc: mod[:, kc:kc + 1], modkey,
                          hn, "hn", sq, "sq", rstd, pb[2], ("ps", 2))
            for blk in range(10):
                wv_, sl_ = load_w(3072 + blk * 512, 512)
                for j in range(4):
                    ch = blk * 4 + j
                    bank = pb[ch % 2]
                    pk = ("ps", ch % 2)
                    b2 = ch % 2
                    for kc in range(DC):
                        P.op("tensor", lambda e, kc=kc, j=j, wv_=wv_, bank=bank: e.matmul(
                            bank[:, :T], lhsT=wv_[:, kc, j * 128:(j + 1) * 128], rhs=hn[:, kc, :T],
                            start=(kc == 0), stop=(kc == DC - 1)),
                            reads=[("w", sl_, kc // 8), ("hn", kc)], writes=[pk])
                    P.op("gpsimd", lambda e, ch=ch, b2=b2: e.tensor_copy(out=pre[:, b2, 0:3], in_=hist[:, ch, :]),
                         reads=["hist", ("hist", ch)], writes=[("pre", b2)])
                    P.op("scalar", lambda e, bank=bank, b2=b2: e.copy(out=pre[:, b2, 3:3 + T], in_=bank[:, :T]),
                         reads=[pk, ("pre", b2)], writes=[("pre", b2)])
                    P.op("vector", lambda e, ch=ch, b2=b2: e.tensor_scalar(
                        out=acc[:, b2, :], in0=pre[:, b2, 0:T], scalar1=cw[:, ch * 4:ch * 4 + 1], scalar2=None,
                        op0=ALU.mult), reads=[("pre", b2), "cw"], writes=[("acc", b2)])
                    for k in range(1, 4):
                        P.op("vector", lambda e, ch=ch, b2=b2, k=k: e.scalar_tensor_tensor(
                            out=acc[:, b2, :], in0=pre[:, b2, k:k + T], scalar=cw[:, ch * 4 + k:ch * 4 + k + 1],
                            in1=acc[:, b2, :], op0=ALU.mult, op1=ALU.add),
                            reads=[("pre", b2), "cw", ("acc", b2)], writes=[("acc", b2)])
                    P.op("gpsimd", lambda e, ch=ch, b2=b2: e.tensor_copy(out=hist[:, ch, :], in_=pre[:, b2, T:T + 3]),
                         reads=[("pre", b2)], writes=[("hist", ch)])
                    if ch < 24:
                        dst, dkey = xsT[:, b2, :], ("xsT", b2)
                    elif ch < 32:
                        dst, dkey = BT[:, ch - 24, :], ("BT", ch - 24)
                    else:
                        dst, dkey = CT[:, ch - 32, :], ("CT", ch - 32)
                    P.op("scalar", lambda e, ch=ch, b2=b2, dst=dst: e.activation(
                        out=dst, in_=acc[:, b2, :], func=AF.Silu, bias=cb[:, ch:ch + 1]),
                        reads=[("acc", b2), "cb"], writes=[dkey])
                    if ch < 24:
                        for s in range(NS):
                            transpose_to(xsT[:, b2, s * 128:(s + 1) * 128], xs_tok[:, s, ch * 128:(ch + 1) * 128],
                                         [dkey], [("xs_tok", s, ch // 3)], eng="vector")
                    elif ch < 32:
                        for s in range(NS):
                            transpose_to(BT[:, ch - 24, s * 128:(s + 1) * 128], Btok[:, s, ch - 24, :],
                                         [dkey], [("Btok", s, ch - 24)], eng="vector")
            for zb in range(6):
                wv_, sl_ = load_w(zb * 512, 512)
                for s in range(NS):
                    bank = pb[(zb * NS + s) % 2]
                    pk = ("ps", (zb * NS + s) % 2)
                    for kc in range(DC):
                        P.op("tensor", lambda e, kc=kc, s=s, wv_=wv_, bank=bank: e.matmul(
                            bank[:, :], lhsT=hn[:, kc, s * 128:(s + 1) * 128], rhs=wv_[:, kc, :],
                            start=(kc == 0), stop=(kc == DC - 1)),
                            reads=[("w", sl_, kc // 8), ("hn", kc)], writes=[pk])
                    P.op("scalar", lambda e, s=s, zb=zb, bank=bank: e.activation(
                        out=zs_tok[:, s, zb * 512:(zb + 1) * 512], in_=bank[:, :], func=AF.Silu),
                        reads=[pk], writes=[("zs_tok", s, zb)])
            wdt, sdt = load_w(8192, 48)
            for s in range(NS):
                sl = slice(s * 128, (s + 1) * 128)
                for kc in range(DC):
                    P.op("tensor", lambda e, kc=kc, sl=sl, wdt=wdt: e.matmul(
                        pb[2][:, 0:48], lhsT=hn[:, kc, sl], rhs=wdt[:, kc, :], start=(kc == 0), stop=(kc == DC - 1)),
                        reads=[("w", sdt, kc // 8), ("hn", kc)], writes=[("ps", 2)])
                P.op("vector", lambda e: e.tensor_tensor(out=sm48[:, 0, :], in0=pb[2][:, 0:48], in1=dtb[:], op=ALU.add),
                     reads=[("ps", 2), "dtb"], writes=[("sm", 0)])
                P.op("scalar", lambda e: e.activation(out=sm48[:, 1, :], in_=sm48[:, 0, :], func=AF.Exp),
                     reads=[("sm", 0)], writes=[("sm", 1)])
                P.op("scalar", lambda e: e.activation(out=sm48[:, 2, :], in_=sm48[:, 1, :], func=AF.Ln, bias=1.0),
                     reads=[("sm", 1)], writes=[("sm", 2)])
                P.op("vector", lambda e: e.tensor_tensor(out=sm48[:, 3, :], in0=sm48[:, 2, :], in1=abc[:], op=ALU.mult),
                     reads=[("sm", 2), "abc"], writes=[("sm", 3)])
                P.op("tensor", lambda e: e.matmul(pb[2][:, 64:112], lhsT=triU[:], rhs=sm48[:, 3, :],
                                                  start=True, stop=True),
                     reads=["triU", ("sm", 3)], writes=[("ps", 2)])
                P.op("tensor", lambda e: e.matmul(pb[2][:, 128:176], lhsT=ones32[:], rhs=sm48[:, 3, :],
                                                  start=True, stop=True),
                     reads=["ones32", ("sm", 3)], writes=[("ps", 2)])
                P.op("scalar", lambda e: e.copy(out=sm48[:, 4, :], in_=pb[2][:, 64:112]),
                     reads=[("ps", 2)], writes=[("sm", 4)])
                P.op("scalar", lambda e: e.copy(out=sm48[:, 7, :], in_=pb[2][:, 128:176]),
                     reads=[("ps", 2)], writes=[("sm", 7)])
                P.op("vector", lambda e: e.tensor_scalar(out=sm48[:, 5, :], in0=sm48[:, 4, :], scalar1=-1.0,
                                                         scalar2=None, op0=ALU.mult),
                     reads=[("sm", 4)], writes=[("sm", 5)])
                P.op("scalar", lambda e: e.activation(out=sm48[:, 6, :], in_=sm48[:, 4, :], func=AF.Exp),
                     reads=[("sm", 4)], writes=[("sm", 6)])
                P.op("vector", lambda e: e.tensor_tensor(out=sm48[:, 8, :], in0=sm48[:, 7, :], in1=sm48[:, 4, :],
                                                         op=ALU.subtract),
                     reads=[("sm", 7), ("sm", 4)], writes=[("sm", 8)])
                P.op("scalar", lambda e: e.activation(out=sm48[:, 8, :], in_=sm48[:, 8, :], func=AF.Exp),
                     reads=[("sm", 8)], writes=[("sm", 8)])
                P.op("scalar", lambda e: e.activation(out=sm48[:, 9, :], in_=sm48[:, 7, :], func=AF.Exp),
                     reads=[("sm", 7)], writes=[("sm", 9)])
                for g in range(8):
                    if g % 4 == 0:
                        for gg_ in range(4):
                            g2 = g + gg_
                            P.op("tensor", lambda e, g2=g2, gg_=gg_, sl=sl: e.matmul(
                                pb[4][:, gg_ * 128:(gg_ + 1) * 128], lhsT=BT[:, g2, sl], rhs=CT[:, g2, sl],
                                start=True, stop=True),
                                reads=[("BT", g2), ("CT", g2)], writes=[("ps", 4)])
                    gb = g % 2
                    for hh in range(6):
                        h = g * 6 + hh
                        hq, hj = h // 4, h % 4
                        if hj == 0:
                            rb = hq % 2
                            P.op("vector", lambda e, hq=hq, rb=rb: e.tensor_tensor(
                                out=R4[:, rb, :].rearrange("p (a l) -> p a l", a=4),
                                in0=triU[:].rearrange("p (o l) -> p o l", o=1).broadcast_to([128, 4, 128]),
                                in1=bc3(sm48[:, 3, 4 * hq:4 * hq + 4], 4, 128), op=ALU.mult),
                                reads=["triU", ("sm", 3)], writes=[("R4", rb)])
                            P.op("tensor", lambda e, rb=rb: e.matmul(pb[3][:, :], lhsT=ones32[:], rhs=R4[:, rb, :],
                                                                   start=True, stop=False),
                                 reads=["ones32", ("R4", rb)], writes=[("ps", 3)])
                            P.op("tensor", lambda e: e.matmul(pb[3][:, :], lhsT=ident32[:], rhs=negm[:],
                                                              start=False, stop=True),
                                 reads=["ident32", "negm"], writes=[("ps", 3)])
                        eb = h % 2
                        P.op("scalar", lambda e, hj=hj, h=h, eb=eb: e.activation(
                            out=Eh[:, eb, :], in_=pb[3][:, hj * 128:(hj + 1) * 128], func=AF.Exp,
                            bias=sm48[:, 5, h:h + 1]),
                            reads=[("ps", 3), ("sm", 5)], writes=[("Eh", eb)])
                        P.op("vector", lambda e, g=g, hh=hh, eb=eb, gb=gb: e.tensor_tensor(
                            out=SL[:, gb, hh, :], in0=pb[4][:, (g % 4) * 128:(g % 4 + 1) * 128], in1=Eh[:, eb, :],
                            op=ALU.mult),
                            reads=[("ps", 4), ("Eh", eb)], writes=[("SL", gb, hh)])
                    gs = slice(g * 384, (g + 1) * 384)
                    hs = slice(g * 6, (g + 1) * 6)
                    v3 = lambda ap: ap.rearrange("p (h q) -> p h q", h=6)
                    P.op("gpsimd", lambda e, s=s, gs=gs, hs=hs, gb=gb: e.tensor_tensor(
                        out=v3(xdt[:, gb, :]), in0=v3(xs_tok[:, s, gs]), in1=bc3(sm48[:, 2, hs], 6, 64), op=ALU.mult),
                        reads=[("xs_tok", s, g), ("sm", 2)], writes=[("xdt", gb)])
                    P.op("gpsimd", lambda e, hs=hs, gb=gb: e.tensor_tensor(
                        out=v3(xdd[:, gb, :]), in0=v3(xdt[:, gb, :]), in1=bc3(sm48[:, 8, hs], 6, 64), op=ALU.mult),
                        reads=[("xdt", gb), ("sm", 8)], writes=[("xdd", gb)])
                    for hh in range(6):
                        P.op("tensor", lambda e, hh=hh, gb=gb: e.matmul(
                            pb[5][:, hh * 64:(hh + 1) * 64], lhsT=SL[:, gb, hh, :], rhs=xdt[:, gb, hh * 64:(hh + 1) * 64],
                            start=True, stop=True),
                            reads=[("SL", gb, hh), ("xdt", gb)], writes=[("ps", 5)])
                    P.op("tensor", lambda e, g=g, gs=gs, sl=sl: e.matmul(
                        pb[6][:, 0:384], lhsT=CT[:, g, sl], rhs=STbf[:, gs], start=True, stop=True),
                        reads=[("CT", g), ("STbf", g), "STbf"], writes=[("ps", 6)])
                    P.op("vector", lambda e, hs=hs, gb=gb: e.tensor_tensor(
                        out=v3(tmpa[:, gb, :]), in0=v3(pb[6][:, 0:384]), in1=bc3(sm48[:, 6, hs], 6, 64), op=ALU.mult),
                        reads=[("ps", 6), ("sm", 6)], writes=[("tmpa", gb)])
                    P.op("vector", lambda e, gb=gb: e.tensor_tensor(
                        out=tmpa[:, gb, :], in0=pb[5][:, 0:384], in1=tmpa[:, gb, :], op=ALU.add),
                        reads=[("ps", 5), ("tmpa", gb)], writes=[("tmpa", gb)])
                    P.op("gpsimd", lambda e, s=s, gs=gs, hs=hs, gb=gb: e.tensor_tensor(
                        out=v3(tmpb[:, gb, :]), in0=v3(xs_tok[:, s, gs]), in1=bc3(dsk[:, hs], 6, 64), op=ALU.mult),
                        reads=[("xs_tok", s, g), "dsk"], writes=[("tmpb", gb)])
                    P.op("gpsimd", lambda e, gb=gb: e.tensor_tensor(
                        out=tmpb[:, gb, :], in0=tmpb[:, gb, :], in1=tmpa[:, gb, :], op=ALU.add),
                        reads=[("tmpb", gb), ("tmpa", gb)], writes=[("tmpb", gb)])
                    P.op("gpsimd", lambda e, s=s, gs=gs, gb=gb: e.tensor_tensor(
                        out=tmpb[:, gb, :], in0=tmpb[:, gb, :], in1=zs_tok[:, s, gs], op=ALU.mult),
                        reads=[("tmpb", gb)] + [("zs_tok", s, zb) for zb in range(6)], writes=[("tmpb", gb)])
                    P.op("scalar", lambda e, gb=gb: e.activation(out=tmpc[:, gb, :], in_=tmpb[:, gb, :], func=AF.Square),
                         reads=[("tmpb", gb)], writes=[("tmpc", gb)])
                    P.op("vector", lambda e, gb=gb: e.reduce_sum(out=st[:, gb, 0:1], in_=tmpc[:, gb, :], axis=AX.X),
                         reads=[("tmpc", gb)], writes=[("st", gb)])
                    P.op("scalar", lambda e, gb=gb: e.activation(out=st[:, gb, 1:2], in_=st[:, gb, 0:1], func=AF.Sqrt,
                                                                 scale=1.0 / 384.0, bias=c.eps_col[:]),
                         reads=[("st", gb), "eps"], writes=[("st", gb)])
                    P.op("vector", lambda e, gb=gb: e.reciprocal(out=st[:, gb, 2:3], in_=st[:, gb, 1:2]),
                         reads=[("st", gb)], writes=[("st", gb)])
                    P.op("vector", lambda e, gb=gb: e.tensor_scalar(out=y3[:, gb, :], in0=tmpb[:, gb, :],
                                                                    scalar1=st[:, gb, 2:3], scalar2=None, op0=ALU.mult),
                         reads=[("tmpb", gb), ("st", gb)], writes=[("y3", gb)])
                    for j in range(3):
                        mc = g * 3 + j
                        transpose_to(y3[:, gb, j * 128:(j + 1) * 128], ya[:, mc, sl], [("y3", gb)], [("ya", mc)],
                                     scale=sng[:, mc:mc + 1])
                    P.op("tensor", lambda e, s=s, g=g, gb=gb: e.matmul(
                        pb[6][:, 0:384], lhsT=Btok[:, s, g, :], rhs=xdd[:, gb, :], start=True, stop=True),
                        reads=[("Btok", s, g), ("xdd", gb)], writes=[("ps", 6)])
                    P.op("vector", lambda e, gs=gs, hs=hs: e.tensor_tensor(
                        out=v3(ST32[:, gs]), in0=v3(ST32[:, gs]), in1=bc3(sm48[:, 9, hs], 6, 64), op=ALU.mult),
                        reads=[("ST32", g), "ST32", ("sm", 9)], writes=[("ST32", g)])
                    P.op("vector", lambda e, gs=gs: e.tensor_tensor(
                        out=ST32[:, gs], in0=ST32[:, gs], in1=pb[6][:, 0:384], op=ALU.add),
                        reads=[("ST32", g), ("ps", 6)], writes=[("ST32", g)])
                    P.op("scalar", lambda e, gs=gs, g=g: e.copy(out=STbf[:, gs], in_=ST32[:, gs]),
                         reads=[("ST32", g), "STbf"], writes=[("STbf", g)])
            for q in range(4):
                tok = P.op("sync", lambda e, q=q, t0=t0: e.dma_start(out=ov[:, 6 * q:6 * q + 6, t0:t0 + T],
                                                                   in_=ya[:, 6 * q:6 * q + 6, :]),
                           reads=[("ya", mc) for mc in range(6 * q, 6 * q + 6)], writes=["y_st"], dma=True)
                out_toks.append(tok)
                for mc in range(6 * q, 6 * q + 6):
                    P.readers.setdefault(("ya", mc), []).append(tok)
            if c.use_barrier:
                P.barrier(c)
        P.finalize(final_waits=out_toks[-8:])
    return nc


MAGIC = 12582912.0
TWO_PI = 2.0 * math.pi


def s5_layouts(a_re, a_im, log_dt, b_re, b_im, c_re, c_im):
    a_re = np.asarray(a_re, np.float32); a_im = np.asarray(a_im, np.float32)
    log_dt = np.asarray(log_dt, np.float32)
    b_re = np.asarray(b_re, np.float32); b_im = np.asarray(b_im, np.float32)
    c_re = np.asarray(c_re, np.float32); c_im = np.asarray(c_im, np.float32)
    out = {}
    i_ = np.arange(4)[:, None, None, None, None, None]
    gl = np.arange(2)[None, :, None, None, None, None]
    c_ = np.arange(16)[None, None, :, None, None, None]
    mm = np.arange(8)[None, None, None, :, None, None]
    glp = np.arange(2)[None, None, None, None, :, None]
    p_ = np.arange(64)[None, None, None, None, None, :]
    g = 2 * (4 * mm + i_) + glp
    shp = (4, 2, 16, 8, 2, 64)
    gB = np.broadcast_to(g, shp); pB = np.broadcast_to(p_, shp); cB = np.broadcast_to(c_, shp)
    out["are1"] = np.ascontiguousarray(a_re[gB, pB].reshape(128, 1024))
    out["aim1"] = np.ascontiguousarray(a_im[gB, pB].reshape(128, 1024))
    out["ldt1"] = np.ascontiguousarray(log_dt[gB].reshape(128, 1024))
    same = np.broadcast_to(gl == glp, shp)
    bre = np.where(same, b_re[gB, pB, cB], 0.0).astype(np.float32).reshape(128, 8, 128)
    bim = np.where(same, b_im[gB, pB, cB], 0.0).astype(np.float32).reshape(128, 8, 128)
    bre4 = np.zeros((4, 128, 8, 128), np.float32); bim4 = np.zeros((4, 128, 8, 128), np.float32)
    for i in range(4):
        bre4[i, i * 32:(i + 1) * 32] = bre[i * 32:(i + 1) * 32]
        bim4[i, i * 32:(i + 1) * 32] = bim[i * 32:(i + 1) * 32]
    out["bre4"] = bre4.reshape(4 * 128, 1024)
    out["bim4"] = bim4.reshape(4 * 128, 1024)
    gl2 = np.arange(2)[:, None, None]; p2 = np.arange(64)[None, :, None]; m2 = np.arange(32)[None, None, :]
    g2 = np.broadcast_to(2 * m2 + gl2, (2, 64, 32)); p2b = np.broadcast_to(p2, (2, 64, 32))
    out["are2"] = np.ascontiguousarray(a_re[g2, p2b].reshape(128, 32))
    out["aim2"] = np.ascontiguousarray(a_im[g2, p2b].reshape(128, 32))
    out["ldt2"] = np.ascontiguousarray(log_dt[g2].reshape(128, 32))
    cre = np.zeros((2, 64, 32, 4, 2, 16), np.float32); cim = np.zeros((2, 64, 32, 4, 2, 16), np.float32)
    for m in range(32):
        for gl_ in range(2):
            gg_ = 2 * m + gl_
            cre[gl_, :, m, m % 4, gl_, :] = c_re[gg_].T
            cim[gl_, :, m, m % 4, gl_, :] = c_im[gg_].T
    out["cre_l"] = cre.reshape(128, 32 * 128)
    out["cim_l"] = cim.reshape(128, 32 * 128)
    out["iota512"] = np.broadcast_to(np.arange(512, dtype=np.float32), (128, 512)).copy()
    return out


def emit_sincos(c, P, ang_ap, tmp_ap, dst_sin, dst_cos, rkeys, tkey, wkeys_sin, wkeys_cos, tmp2_ap, t2key):
    for (dst, add, wk) in [(dst_sin, 0.0, wkeys_sin), (dst_cos, 0.5 * math.pi, wkeys_cos)]:
        P.op("vector", lambda e, add=add: e.tensor_scalar(out=tmp2_ap, in0=ang_ap, scalar1=add, scalar2=None,
                                                          op0=ALU.add), reads=list(rkeys), writes=[t2key])
        P.op("vector", lambda e: e.tensor_scalar(out=tmp_ap, in0=tmp2_ap, scalar1=1.0 / TWO_PI, scalar2=MAGIC,
                                                 op0=ALU.mult, op1=ALU.add), reads=[t2key], writes=[tkey])
        P.op("vector", lambda e: e.tensor_scalar(out=tmp_ap, in0=tmp_ap, scalar1=MAGIC, scalar2=None,
                                                 op0=ALU.subtract), reads=[tkey], writes=[tkey])
        P.op("vector", lambda e: e.scalar_tensor_tensor(out=tmp_ap, in0=tmp_ap, scalar=-TWO_PI, in1=tmp2_ap,
                                                        op0=ALU.mult, op1=ALU.add),
             reads=[tkey, t2key], writes=[tkey])
        P.op("vector", lambda e: e.tensor_scalar(out=tmp_ap, in0=tmp_ap, scalar1=-math.pi, scalar2=math.pi,
                                                 op0=ALU.max, op1=ALU.min), reads=[tkey], writes=[tkey])
        P.op("scalar", lambda e, dst=dst: e.activation(out=dst, in_=tmp_ap, func=AF.Sin),
             reads=[tkey], writes=list(wk))


def build_s5(ntok, T=256, env=None):
    nc = mk_nc(env)
    dt_in = mk_in(nc, env)
    WQ = "sync" if (env is not None and getattr(env, "bf16w", False)) else "gpsimd"
    xin = dt_in("xT", [D, ntok])
    yain = dt_in("yaT", [3072, ntok])
    ccol = dt_in("ccol", [128, DC])
    ada_w = dt_in("ada_w", [D, 3 * D])
    ada_b = dt_in("ada_b", [128, 48])
    ng = dt_in("ng", [128, DC])
    w_in = dt_in("w_in", [D, IN0])
    w_out = dt_in("w_out", [4096, D])
    glu_w = dt_in("glu_w", [1024, 1024])
    glub_d = dt_in("glub", [128, 8])
    s5d_d = dt_in("s5d", [128, 8])
    are1_d = dt_in("are1", [128, 1024]); aim1_d = dt_in("aim1", [128, 1024]); ldt1_d = dt_in("ldt1", [128, 1024])
    bre4_d = dt_in("bre4", [512, 1024]); bim4_d = dt_in("bim4", [512, 1024])
    are2_d = dt_in("are2", [128, 32]); aim2_d = dt_in("aim2", [128, 32]); ldt2_d = dt_in("ldt2", [128, 32])
    cre_d = dt_in("cre_l", [128, 4096]); cim_d = dt_in("cim_l", [128, 4096])
    iota_d = dt_in("iota512", [128, 512])
    xout = mk_out(nc, env, "yT", [D, ntok])
    with ExitStack() as es:
        c = Ctx()
        c.nc, c.es = nc, es
        c.pfx = env.pfx if env is not None else ""
        P = Prog(nc, es, env.G if env is not None else None)
        c.P = P
        c.wslot_n = 0
        common_setup(c)
        wslots = [sb(c, f"wslot{i}", [128, 8192], BF16) for i in range(2)]
        x_t = sb(c, "x_t", [128, DC, T], F32)
        hn = sb(c, "hn", [128, DC, T], BF16)
        yT = sb(c, "yT_sb", [128, 32, T], BF16)
        rstd = sb(c, "rstd", [128, T], F32)
        bTre = sb(c, "bTre", [128, 32, 128], BF16)
        bTim = sb(c, "bTim", [128, 32, 128], BF16)
        CreT = sb(c, "CreT", [128, 32, 128], BF16)
        CimT = sb(c, "CimT", [128, 32, 128], BF16)
        cosj = sb(c, "cosj", [128, 32, T], BF16)
        sinj = sb(c, "sinj", [128, 32, T], BF16)
        uT = sb(c, "uT", [128, 8, T], BF16)
        yg32 = sb(c, "yg32", [128, 8, T], F32)
        ygb = sb(c, "ygb", [128, 8, T], BF16)
        tt = sb(c, "tt", [128, 12, T], F32)
        xri = sb(c, "xri", [128, 2, 4, 2, T], BF16)
        scr = sb(c, "scr", [128, 8, 512], F32)
        brei = sb(c, "brei", [128, 2, 512], F32)
        l2 = sb(c, "l2", [128, 12, 32], F32)
        iota = sb(c, "iota", [128, 512], F32)
        glub = sb(c, "glub", [128, 8], F32)
        s5d = sb(c, "s5d", [128, 8], F32)
        xres = sb(c, "xres", [128, 2, T], F32)
        pb = [ps(c, f"pb{i}", [128, 512]) for i in range(8)]
        A, mod, Akey, modkey = emit_adaln(c, P, "m", ccol, ada_w, ada_b, ng, wslots, pb[2], ("ps", 2))
        for (t_, d_, k_) in [(iota, iota_d, "iota"), (glub, glub_d, "glub"), (s5d, s5d_d, "s5d"),
                             (l2[:, 0, :], are2_d, ("l2", 0)), (l2[:, 1, :], aim2_d, ("l2", 1)),
                             (l2[:, 2, :], ldt2_d, ("l2", 2))]:
            P.op("sync", lambda e, t_=t_, d_=d_: e.dma_start(out=t_ if not hasattr(t_, "ap") else t_[:], in_=d_),
                 writes=[k_], dma=True)
        for q in range(4):
            P.op("gpsimd", lambda e, q=q: e.dma_start(out=CreT[:, 8 * q:8 * q + 8, :],
                                                     in_=cre_d[:, q * 1024:(q + 1) * 1024].rearrange(
                                                         "p (m k) -> p m k", m=8)),
                 writes=[("CreT", q)], dma=True)
            P.op("gpsimd", lambda e, q=q: e.dma_start(out=CimT[:, 8 * q:8 * q + 8, :],
                                                     in_=cim_d[:, q * 1024:(q + 1) * 1024].rearrange(
                                                         "p (m k) -> p m k", m=8)),
                 writes=[("CimT", q)], dma=True)
            P.op("vector", lambda e, q=q: e.tensor_scalar(out=CimT[:, 8 * q:8 * q + 8, :], in0=CimT[:, 8 * q:8 * q + 8, :],
                                                          scalar1=-1.0, scalar2=None, op0=ALU.mult),
                 reads=[("CimT", q)], writes=[("CimT", q)])
        P.op("scalar", lambda e: e.activation(out=l2[:, 9, :], in_=l2[:, 2, :], func=AF.Exp),
             reads=[("l2", 2)], writes=[("l2", 9)])
        P.op("vector", lambda e: e.tensor_tensor(out=l2[:, 3, :], in0=l2[:, 1, :], in1=l2[:, 9, :], op=ALU.mult),
             reads=[("l2", 1), ("l2", 9)], writes=[("l2", 3)])
        P.op("vector", lambda e: e.tensor_tensor(out=l2[:, 10, :], in0=l2[:, 0, :], in1=l2[:, 9, :], op=ALU.mult),
             reads=[("l2", 0), ("l2", 9)], writes=[("l2", 10)])
        P.op("scalar", lambda e: e.activation(out=l2[:, 4, :], in_=l2[:, 10, :], func=AF.Exp),
             reads=[("l2", 10)], writes=[("l2", 4)])
        P.op("vector", lambda e: e.memset(l2[:, 7, :], 0.0), writes=[("l2", 7)])
        P.op("vector", lambda e: e.memset(l2[:, 8, :], 0.0), writes=[("l2", 8)])
        P.op("vector", lambda e: e.tensor_scalar(out=l2[:, 11, :], in0=l2[:, 3, :], scalar1=float(T), scalar2=None,
                                                 op0=ALU.mult), reads=[("l2", 3)], writes=[("l2", 11)])
        emit_sincos(c, P, l2[:, 11, :], l2[:, 9, :], l2[:, 6, :], l2[:, 5, :], [("l2", 11)], ("l2", 9),
                    [("l2", 6)], [("l2", 5)], l2[:, 10, :], ("l2", 10))
        for m in range(32):
            P.op("vector", lambda e, m=m: e.tensor_scalar(out=tt[:, 0, :], in0=iota[:, :T], scalar1=l2[:, 3, m:m + 1],
                                                          scalar2=None, op0=ALU.mult),
                 reads=["iota", ("l2", 3)], writes=[("tt", 0, 0)])
            emit_sincos(c, P, tt[:, 0, :], tt[:, 1, :], sinj[:, m, :], cosj[:, m, :], [("tt", 0, 0)], ("tt", 0, 1),
                        [("sinj", m)], [("cosj", m)], tt[:, 2, :], ("tt", 0, 2))
        for hf in range(2):
            fs = slice(hf * 512, (hf + 1) * 512)
            S = lambda k: scr[:, k, :]
            for (k, d_) in [(0, are1_d), (1, aim1_d), (2, ldt1_d)]:
                P.op("sync", lambda e, k=k, d_=d_, fs=fs: e.dma_start(out=scr[:, k, :], in_=d_[:, fs]),
                     writes=[("scr", k)], dma=True)
            P.op("scalar", lambda e: e.activation(out=S(2), in_=S(2), func=AF.Exp), reads=[("scr", 2)],
                 writes=[("scr", 2)])
            P.op("vector", lambda e: e.tensor_tensor(out=S(3), in0=S(1), in1=S(2), op=ALU.mult),
                 reads=[("scr", 1), ("scr", 2)], writes=[("scr", 3)])
            P.op("vector", lambda e: e.tensor_tensor(out=S(4), in0=S(0), in1=S(2), op=ALU.mult),
                 reads=[("scr", 0), ("scr", 2)], writes=[("scr", 4)])
            P.op("scalar", lambda e: e.activation(out=S(4), in_=S(4), func=AF.Exp), reads=[("scr", 4)],
                 writes=[("scr", 4)])
            emit_sincos(c, P, S(3), S(7), S(5), S(6), [("scr", 3)], ("scr", 7), [("scr", 5)], [("scr", 6)],
                        S(2), ("scr", 2))
            P.op("vector", lambda e: e.tensor_tensor(out=S(5), in0=S(5), in1=S(4), op=ALU.mult),
                 reads=[("scr", 5), ("scr", 4)], writes=[("scr", 5)])
            P.op("vector", lambda e: e.tensor_tensor(out=S(6), in0=S(6), in1=S(4), op=ALU.mult),
                 reads=[("scr", 6), ("scr", 4)], writes=[("scr", 6)])
            P.op("vector", lambda e: e.tensor_scalar(out=S(6), in0=S(6), scalar1=-1.0, scalar2=None, op0=ALU.add),
                 reads=[("scr", 6)], writes=[("scr", 6)])
            P.op("vector", lambda e: e.tensor_tensor(out=S(2), in0=S(0), in1=S(0), op=ALU.mult),
                 reads=[("scr", 0)], writes=[("scr", 2)])
            P.op("vector", lambda e: e.tensor_tensor(out=S(3), in0=S(1), in1=S(1), op=ALU.mult),
                 reads=[("scr", 1)], writes=[("scr", 3)])
            P.op("vector", lambda e: e.tensor_tensor(out=S(2), in0=S(2), in1=S(3), op=ALU.add),
                 reads=[("scr", 2), ("scr", 3)], writes=[("scr", 2)])
            P.op("vector", lambda e: e.reciprocal(out=S(2), in_=S(2)), reads=[("scr", 2)], writes=[("scr", 2)])
            P.op("vector", lambda e: e.tensor_tensor(out=S(3), in0=S(6), in1=S(0), op=ALU.mult),
                 reads=[("scr", 6), ("scr", 0)], writes=[("scr", 3)])
            P.op("vector", lambda e: e.tensor_tensor(out=S(7), in0=S(5), in1=S(1), op=ALU.mult),
                 reads=[("scr", 5), ("scr", 1)], writes=[("scr", 7)])
            P.op("vector", lambda e: e.tensor_tensor(out=S(3), in0=S(3), in1=S(7), op=ALU.add),
                 reads=[("scr", 3), ("scr", 7)], writes=[("scr", 3)])
            P.op("vector", lambda e: e.tensor_tensor(out=S(3), in0=S(3), in1=S(2), op=ALU.mult),
                 reads=[("scr", 3), ("scr", 2)], writes=[("scr", 3)])
            P.op("vector", lambda e: e.tensor_tensor(out=S(4), in0=S(5), in1=S(0), op=ALU.mult),
                 reads=[("scr", 5), ("scr", 0)], writes=[("scr", 4)])
            P.op("vector", lambda e: e.tensor_tensor(out=S(7), in0=S(6), in1=S(1), op=ALU.mult),
                 reads=[("scr", 6), ("scr", 1)], writes=[("scr", 7)])
            P.op("vector", lambda e: e.tensor_tensor(out=S(4), in0=S(4), in1=S(7), op=ALU.subtract),
                 reads=[("scr", 4), ("scr", 7)], writes=[("scr", 4)])
            P.op("vector", lambda e: e.tensor_tensor(out=S(4), in0=S(4), in1=S(2), op=ALU.mult),
                 reads=[("scr", 4), ("scr", 2)], writes=[("scr", 4)])
            for i in range(4):
                P.op("sync", lambda e, i=i, fs=fs: e.dma_start(out=brei[:, 0, :], in_=bre4_d[i * 128:(i + 1) * 128, fs]),
                     writes=[("brei", 0)], dma=True)
                P.op("sync", lambda e, i=i, fs=fs: e.dma_start(out=brei[:, 1, :], in_=bim4_d[i * 128:(i + 1) * 128, fs]),
                     writes=[("brei", 1)], dma=True)
                P.op("vector", lambda e: e.tensor_tensor(out=S(0), in0=S(3), in1=brei[:, 0, :], op=ALU.mult),
                     reads=[("scr", 3), ("brei", 0)], writes=[("scr", 0)])
                P.op("vector", lambda e: e.tensor_tensor(out=S(1), in0=S(4), in1=brei[:, 1, :], op=ALU.mult),
                     reads=[("scr", 4), ("brei", 1)], writes=[("scr", 1)])
                dre = bTre[:, hf * 16:(hf + 1) * 16, :].rearrange("p (mm i) q -> p mm i q", i=4)[:, :, i, :]
                dim = bTim[:, hf * 16:(hf + 1) * 16, :].rearrange("p (mm i) q -> p mm i q", i=4)[:, :, i, :]
                P.op("vector", lambda e, dre=dre: e.tensor_tensor(
                    out=dre, in0=S(0).rearrange("p (mm q) -> p mm q", mm=4), in1=S(1).rearrange("p (mm q) -> p mm q", mm=4),
                    op=ALU.subtract), reads=[("scr", 0), ("scr", 1)], writes=["bTre"])
                P.op("vector", lambda e: e.tensor_tensor(out=S(0), in0=S(3), in1=brei[:, 1, :], op=ALU.mult),
                     reads=[("scr", 3), ("brei", 1)], writes=[("scr", 0)])
                P.op("vector", lambda e: e.tensor_tensor(out=S(1), in0=S(4), in1=brei[:, 0, :], op=ALU.mult),
                     reads=[("scr", 4), ("brei", 0)], writes=[("scr", 1)])
                P.op("vector", lambda e, dim=dim: e.tensor_tensor(
                    out=dim, in0=S(0).rearrange("p (mm q) -> p mm q", mm=4), in1=S(1).rearrange("p (mm q) -> p mm q", mm=4),
                    op=ALU.add), reads=[("scr", 0), ("scr", 1)], writes=["bTim"])
        xv = xin.rearrange("(kc p) t -> p kc t", p=128)
        yav = yain.rearrange("(mc p) t -> p mc t", p=128)
        ov = xout.rearrange("(kc p) t -> p kc t", p=128)
        w_in_v = w_in.rearrange("(kc p) n -> p kc n", p=128)
        w_out_v = w_out.rearrange("(mc p) n -> p mc n", p=128)
        glu_v = glu_w.rearrange("(kc p) n -> p kc n", p=128)
        out_toks = []
        GC = 2.0 * math.sqrt(2.0 / math.pi)
        tt2 = scr[:].rearrange("p a b -> p (a b)")[:, 0:12 * T].rearrange("p (k t) -> p k t", k=12)
        P.op("vector", lambda e: e.memset(tt2[:, 0, 0:1], 0.0), reads=[("scr", k) for k in range(8)],
             writes=[("tt", 1, k) for k in range(12)])
        for ti in range(ntok // T):
            t0 = ti * T
            for q in range(4):
                P.op("sync", lambda e, q=q, t0=t0: e.dma_start(out=x_t[:, 4 * q:4 * q + 4, :],
                                                             in_=xv[:, 4 * q:4 * q + 4, t0:t0 + T]),
                     writes=["x"], dma=True)
            emit_norm_mod(c, P, x_t, "x", T, A, Akey, lambda kc: mod[:, kc:kc + 1], modkey,
                          hn, "hn", yT, "yT", rstd, pb[2], ("ps", 2))
            for q in range(4):
                P.op("gpsimd", lambda e, q=q, t0=t0: e.dma_start(out=yT[:, 6 * q:6 * q + 6, :],
                                                               in_=yav[:, 6 * q:6 * q + 6, t0:t0 + T]),
                     reads=[("hn", kc) for kc in range(DC)], writes=[("yT", mc) for mc in range(6 * q, 6 * q + 6)],
                     dma=True)
            for ub in range(2):
                slot = c.wslot_n % 2
                c.wslot_n += 1
                wv_ = wslots[slot][:, 0:8192].rearrange("p (kc n) -> p kc n", kc=16)
                for half in range(2):
                    P.op(WQ, lambda e, wv_=wv_, half=half, ub=ub: e.dma_start(
                        out=wv_[:, 8 * half:8 * half + 8, :],
                        in_=w_in_v[:, 8 * half:8 * half + 8, 8240 + ub * 512:8240 + (ub + 1) * 512]),
                        writes=[("w", slot, half)], dma=True)
                for j in range(4):
                    uc = ub * 4 + j
                    bank = pb[uc % 2]
                    for kc in range(DC):
                        P.op("tensor", lambda e, kc=kc, j=j, wv_=wv_, bank=bank: e.matmul(
                            bank[:, :T], lhsT=wv_[:, kc, j * 128:(j + 1) * 128], rhs=hn[:, kc, :T],
                            start=(kc == 0), stop=(kc == DC - 1)),
                            reads=[("w", slot, kc // 8), ("hn", kc)], writes=[("ps", uc % 2)])
                    P.op("scalar", lambda e, uc=uc, bank=bank: e.copy(out=uT[:, uc, :], in_=bank[:, :T]),
                         reads=[("ps", uc % 2)], writes=[("uT", uc)])
            for m in range(32):
                mb, i4 = m // 4, m % 4
                par = m % 2
                ttp = tt if par == 0 else tt2
                TT = (lambda ttp: (lambda k: ttp[:, k, :]))(ttp)
                pre_, pim_ = (pb[3], pb[4]) if par == 0 else (pb[6], pb[7])
                kre, kim = (("ps", 3), ("ps", 4)) if par == 0 else (("ps", 6), ("ps", 7))
                tk = (lambda par: (lambda k: ("tt", par, k)))(par)
                xr = xri[:, mb % 2]
                xk = (lambda mbp: (lambda a, b: ("xri", mbp, a, b)))(mb % 2)
                P.op("tensor", lambda e, m=m, mb=mb, TT=TT, pre_=pre_, pim_=pim_, xr=xr: e.matmul(pre_[:, :T], lhsT=bTre[:, m, :], rhs=uT[:, mb, :],
                                                             start=True, stop=True),
                     reads=["bTre", ("uT", mb)], writes=[kre])
                P.op("tensor", lambda e, m=m, mb=mb, TT=TT, pre_=pre_, pim_=pim_, xr=xr: e.matmul(pim_[:, :T], lhsT=bTim[:, m, :], rhs=uT[:, mb, :],
                                                             start=True, stop=True),
                     reads=["bTim", ("uT", mb)], writes=[kim])
                cs, sn = cosj[:, m, :], sinj[:, m, :]
                ck, sk = ("cosj", m), ("sinj", m)
                P.op("vector", lambda e, cs=cs, TT=TT, pre_=pre_, pim_=pim_, xr=xr: e.tensor_tensor(out=TT(0), in0=pre_[:, :T], in1=cs, op=ALU.mult),
                     reads=[kre, ck], writes=[tk(0)])
                P.op("vector", lambda e, sn=sn, TT=TT, pre_=pre_, pim_=pim_, xr=xr: e.tensor_tensor(out=TT(1), in0=pim_[:, :T], in1=sn, op=ALU.mult),
                     reads=[kim, sk], writes=[tk(1)])
                P.op("gpsimd", lambda e, TT=TT, pre_=pre_, pim_=pim_, xr=xr: e.tensor_tensor(out=TT(4), in0=TT(0), in1=TT(1), op=ALU.add),
                     reads=[tk(0), tk(1)], writes=[tk(4)])
                P.op("vector", lambda e, cs=cs, TT=TT, pre_=pre_, pim_=pim_, xr=xr: e.tensor_tensor(out=TT(2), in0=pim_[:, :T], in1=cs, op=ALU.mult),
                     reads=[kim, ck], writes=[tk(2)])
                P.op("vector", lambda e, sn=sn, TT=TT, pre_=pre_, pim_=pim_, xr=xr: e.tensor_tensor(out=TT(3), in0=pre_[:, :T], in1=sn, op=ALU.mult),
                     reads=[kre, sk], writes=[tk(3)])
                P.op("gpsimd", lambda e, TT=TT, pre_=pre_, pim_=pim_, xr=xr: e.tensor_tensor(out=TT(5), in0=TT(2), in1=TT(3), op=ALU.subtract),
                     reads=[tk(2), tk(3)], writes=[tk(5)])
                P.op("vector", lambda e, m=m, TT=TT, pre_=pre_, pim_=pim_, xr=xr: e.tensor_tensor_scan(
                    out=TT(6), data0=l2[:, 4, m:m + 1].broadcast_to([128, T]), data1=TT(4),
                    initial=l2[:, 7, m:m + 1], op0=ALU.mult, op1=ALU.add),
                    reads=[tk(4), ("l2", 4), ("l2", 7)], writes=[tk(6)])
                P.op("vector", lambda e, m=m, TT=TT, pre_=pre_, pim_=pim_, xr=xr: e.tensor_tensor_scan(
                    out=TT(7), data0=l2[:, 4, m:m + 1].broadcast_to([128, T]), data1=TT(5),
                    initial=l2[:, 8, m:m + 1], op0=ALU.mult, op1=ALU.add),
                    reads=[tk(5), ("l2", 4), ("l2", 8)], writes=[tk(7)])
                P.op("scalar", lambda e, m=m, TT=TT, pre_=pre_, pim_=pim_, xr=xr: e.copy(out=l2[:, 7, m:m + 1], in_=TT(6)[:, T - 1:T]),
                     reads=[tk(6), ("l2", 7)], writes=[("l2", 7)])
                P.op("scalar", lambda e, m=m, TT=TT, pre_=pre_, pim_=pim_, xr=xr: e.copy(out=l2[:, 8, m:m + 1], in_=TT(7)[:, T - 1:T]),
                     reads=[tk(7), ("l2", 8)], writes=[("l2", 8)])
                P.op("vector", lambda e, cs=cs, TT=TT, pre_=pre_, pim_=pim_, xr=xr: e.tensor_tensor(out=TT(8), in0=TT(6), in1=cs, op=ALU.mult),
                     reads=[tk(6), ck], writes=[tk(8)])
                P.op("gpsimd", lambda e, sn=sn, TT=TT, pre_=pre_, pim_=pim_, xr=xr: e.tensor_tensor(out=TT(9), in0=TT(7), in1=sn, op=ALU.mult),
                     reads=[tk(7), sk], writes=[tk(9)])
                P.op("gpsimd", lambda e, i4=i4, TT=TT, pre_=pre_, pim_=pim_, xr=xr: e.tensor_tensor(out=xr[:, i4, 0, :], in0=TT(8), in1=TT(9),
                                                                op=ALU.subtract),
                     reads=[tk(8), tk(9)], writes=[xk(i4, 0)])
                P.op("vector", lambda e, sn=sn, TT=TT, pre_=pre_, pim_=pim_, xr=xr: e.tensor_tensor(out=TT(10), in0=TT(6), in1=sn, op=ALU.mult),
                     reads=[tk(6), sk], writes=[tk(10)])
                P.op("gpsimd", lambda e, cs=cs, TT=TT, pre_=pre_, pim_=pim_, xr=xr: e.tensor_tensor(out=TT(11), in0=TT(7), in1=cs, op=ALU.mult),
                     reads=[tk(7), ck], writes=[tk(11)])
                P.op("gpsimd", lambda e, i4=i4, TT=TT, pre_=pre_, pim_=pim_, xr=xr: e.tensor_tensor(out=xr[:, i4, 1, :], in0=TT(10), in1=TT(11),
                                                                op=ALU.add),
                     reads=[tk(10), tk(11)], writes=[xk(i4, 1)])
                if i4 == 3:
                    for i5 in range(4):
                        m5 = mb * 4 + i5
                        P.op("tensor", lambda e, m5=m5, i5=i5, TT=TT, pre_=pre_, pim_=pim_, xr=xr: e.matmul(
                            pb[5][:, :T], lhsT=CreT[:, m5, :], rhs=xr[:, i5, 0, :], start=(i5 == 0), stop=False),
                            reads=[("CreT", m5 // 8), xk(i5, 0)], writes=[("ps", 5)])
                        P.op("tensor", lambda e, m5=m5, i5=i5, TT=TT, pre_=pre_, pim_=pim_, xr=xr: e.matmul(
                            pb[5][:, :T], lhsT=CimT[:, m5, :], rhs=xr[:, i5, 1, :], start=False, stop=(i5 == 3)),
                            reads=[("CimT", m5 // 8), xk(i5, 1)], writes=[("ps", 5)])
                    P.op("vector", lambda e, mb=mb, TT=TT, pre_=pre_, pim_=pim_, xr=xr: e.scalar_tensor_tensor(
                        out=TT(0), in0=uT[:, mb, :], scalar=s5d[:, mb:mb + 1], in1=pb[5][:, :T],
                        op0=ALU.mult, op1=ALU.add),
                        reads=[("uT", mb), "s5d", ("ps", 5)], writes=[tk(0)])
                    P.op("scalar", lambda e, TT=TT, pre_=pre_, pim_=pim_, xr=xr: e.activation(out=TT(1), in_=TT(0), func=AF.Square),
                         reads=[tk(0)], writes=[tk(1)])
                    P.op("vector", lambda e, TT=TT, pre_=pre_, pim_=pim_, xr=xr: e.tensor_scalar(out=TT(1), in0=TT(1), scalar1=0.044715, scalar2=1.0,
                                                             op0=ALU.mult, op1=ALU.add),
                         reads=[tk(1)], writes=[tk(1)])
                    P.op("vector", lambda e, TT=TT, pre_=pre_, pim_=pim_, xr=xr: e.tensor_tensor(out=TT(1), in0=TT(1), in1=TT(0), op=ALU.mult),
                         reads=[tk(1), tk(0)], writes=[tk(1)])
                    P.op("scalar", lambda e, TT=TT, pre_=pre_, pim_=pim_, xr=xr: e.activation(out=TT(2), in_=TT(1), func=AF.Sigmoid, scale=GC),
                         reads=[tk(1)], writes=[tk(2)])
                    P.op("vector", lambda e, mb=mb, TT=TT, pre_=pre_, pim_=pim_, xr=xr: e.tensor_tensor(out=yg32[:, mb, :], in0=TT(0), in1=TT(2),
                                                                    op=ALU.mult),
                         reads=[tk(0), tk(2)], writes=[("yg32", mb)])
                    P.op("scalar", lambda e, mb=mb, TT=TT, pre_=pre_, pim_=pim_, xr=xr: e.copy(out=ygb[:, mb, :], in_=yg32[:, mb, :]),
                         reads=[("yg32", mb)], writes=[("ygb", mb)])
            L = lambda k: l2[:, k, :]
            P.op("vector", lambda e: e.tensor_tensor(out=L(9), in0=L(7), in1=L(5), op=ALU.mult),
                 reads=[("l2", 7), ("l2", 5)], writes=[("l2", 9)])
            P.op("vector", lambda e: e.tensor_tensor(out=L(10), in0=L(8), in1=L(6), op=ALU.mult),
                 reads=[("l2", 8), ("l2", 6)], writes=[("l2", 10)])
            P.op("vector", lambda e: e.tensor_tensor(out=L(9), in0=L(9), in1=L(10), op=ALU.subtract),
                 reads=[("l2", 9), ("l2", 10)], writes=[("l2", 9)])
            P.op("vector", lambda e: e.tensor_tensor(out=L(10), in0=L(7), in1=L(6), op=ALU.mult),
                 reads=[("l2", 7), ("l2", 6)], writes=[("l2", 10)])
            P.op("vector", lambda e: e.tensor_tensor(out=L(11), in0=L(8), in1=L(5), op=ALU.mult),
                 reads=[("l2", 8), ("l2", 5)], writes=[("l2", 11)])
            P.op("vector", lambda e: e.tensor_tensor(out=L(8), in0=L(10), in1=L(11), op=ALU.add),
                 reads=[("l2", 10), ("l2", 11)], writes=[("l2", 8)])
            P.op("vector", lambda e: e.tensor_copy(out=L(7), in_=L(9)), reads=[("l2", 9)], writes=[("l2", 7)])
            slot = c.wslot_n % 2
            c.wslot_n += 1
            gv = wslots[slot][:, 0:8192].rearrange("p (kc n) -> p kc n", kc=8)
            for half in range(2):
                P.op(WQ, lambda e, gv=gv, half=half: e.dma_start(out=gv[:, 4 * half:4 * half + 4, :],
                                                                      in_=glu_v[:, 4 * half:4 * half + 4, :]),
                     writes=[("w", slot, half)], dma=True)
            for ob in range(8):
                bank = pb[ob % 2]
                for kc in range(8):
                    P.op("tensor", lambda e, kc=kc, ob=ob, gv=gv, bank=bank: e.matmul(
                        bank[:, :T], lhsT=gv[:, kc, ob * 128:(ob + 1) * 128], rhs=ygb[:, kc, :],
                        start=(kc == 0), stop=(kc == 7)),
                        reads=[("w", slot, kc // 4), ("ygb", kc)], writes=[("ps", ob % 2)])
                P.op("scalar", lambda e, ob=ob, bank=bank: e.activation(out=tt[:, 3, :], in_=bank[:, :T], func=AF.Sigmoid,
                                                                       bias=glub[:, ob:ob + 1]),
                     reads=[("ps", ob % 2), "glub"], writes=[("tt", 0, 3)])
                P.op("vector", lambda e, ob=ob: e.tensor_tensor(out=yT[:, 24 + ob, :], in0=yg32[:, ob, :], in1=tt[:, 3, :],
                                                                op=ALU.mult),
                     reads=[("yg32", ob), ("tt", 0, 3)] + [("hn", kc) for kc in range(DC)], writes=[("yT", 24 + ob)])
            for ob in range(D // 256):
                slot = c.wslot_n % 2
                c.wslot_n += 1
                wov = wslots[slot][:, 0:8192].rearrange("p (mc n) -> p mc n", mc=32)
                for half in range(2):
                    P.op(WQ, lambda e, ob=ob, wov=wov, half=half: e.dma_start(
                        out=wov[:, 16 * half:16 * half + 16, :],
                        in_=w_out_v[:, 16 * half:16 * half + 16, ob * 256:(ob + 1) * 256]),
                        writes=[("w", slot, half)], dma=True)
                for j in range(2):
                    dc = ob * 2 + j
                    bank = pb[dc % 2]
                    for mc in range(32):
                        P.op("tensor", lambda e, mc=mc, j=j, bank=bank, wov=wov: e.matmul(
                            bank[:, :T], lhsT=wov[:, mc, j * 128:(j + 1) * 128], rhs=yT[:, mc, :T],
                            start=(mc == 0), stop=(mc == 31)),
                            reads=[("w", slot, mc // 16), ("yT", mc)], writes=[("ps", dc % 2)])
                    P.op("vector", lambda e, dc=dc, bank=bank: e.scalar_tensor_tensor(
                        out=xres[:, dc % 2, :], in0=bank[:, :T], scalar=mod[:, 32 + dc:33 + dc],
                        in1=x_t[:, dc, :T], op0=ALU.mult, op1=ALU.add),
                        reads=[("ps", dc % 2), modkey, "x"], writes=[("xres", dc % 2)])
                    tok = P.op("sync", lambda e, dc=dc, t0=t0: e.dma_start(out=ov[:, dc, t0:t0 + T],
                                                                         in_=xres[:, dc % 2, :]),
                               reads=[("xres", dc % 2)], writes=["x_st"], dma=True)
                    out_toks.append(tok)
        P.finalize(final_waits=out_toks[-8:])
    return nc


WSHAPES = {"ab_w_in": [D, IN0], "ab_w_out": [4096, D], "glu_w": [1024, 1024],
           "f0_w_in": [D, 2 * FFN_H], "f0_w_out": [FFN_H, D], "r_w_in": [D, 12288], "r_w_out": [4096, D],
           "f1_w_in": [D, 2 * FFN_H], "f1_w_out": [FFN_H, D]}


def build_fused(ntok):
    nc = bass.Bass("TRN2", target_bir_lowering=False)
    with ExitStack() as ges:
        env = Env(nc, ges)
        env.bf16w = True
        x0 = nc.dram_tensor("xT", [D, ntok], F32, kind="ExternalInput").ap()
        out = nc.dram_tensor("yT", [D, ntok], F32, kind="ExternalOutput").ap()
        ya = nc.dram_tensor("scr_ya", [3072, ntok], F32, kind="Internal").ap()
        x1 = nc.dram_tensor("scr_x1", [D, ntok], F32, kind="Internal").ap()
        x2 = nc.dram_tensor("scr_x2", [D, ntok], F32, kind="Internal").ap()
        x3 = nc.dram_tensor("scr_x3", [D, ntok], F32, kind="Internal").ap()
        wb = {}
        with ExitStack() as es:
            P = Prog(nc, es, env.G)
            toks = []
            for name, shp in WSHAPES.items():
                src = nc.dram_tensor(name, list(shp), F32, kind="ExternalInput").ap()
                dst = nc.dram_tensor(name + "_bf", list(shp), BF16, kind="Internal").ap()
                wb[name] = dst
                R = 256
                for r0 in range(0, shp[0], R):
                    toks.append(P.op("gpsimd", lambda e, src=src, dst=dst, r0=r0, R=R: e.dma_start(
                        out=dst[r0:r0 + R, :], in_=src[r0:r0 + R, :]), writes=[("cv", name, r0)], dma=True))
            P.finalize(final_waits=toks)
        nc.all_engine_barrier()
        phases = [("a_", build_ssd, {"xT": x0, "yT": ya, "w_in": wb["ab_w_in"]}, {}),
                  ("b_", build_s5, {"xT": x0, "yaT": ya, "yT": x1, "w_in": wb["ab_w_in"], "w_out": wb["ab_w_out"],
                                    "glu_w": wb["glu_w"]}, {}),
                  ("f0_", build_ffn, {"xT": x1, "yT": x2, "w_in": wb["f0_w_in"], "w_out": wb["f0_w_out"]},
                   {"final_norm": False}),
                  ("r_", build_ret, {"xT": x2, "yT": x3, "w_in": wb["r_w_in"], "w_out": wb["r_w_out"]}, {}),
                  ("f1_", build_ffn, {"xT": x3, "yT": out, "w_in": wb["f1_w_in"], "w_out": wb["f1_w_out"]},
                   {"final_norm": True})]
        for pfx, fn, io, kw in phases:
            env.pfx = pfx
            env.io = io
            fn(ntok, env=env, **kw)
            nc.all_engine_barrier()
    return nc


def fused_inputs(b, x, c, norm_mix_g, ada_mix_w, ada_mix_b, norm_ffn_g, ada_ffn_w, ada_ffn_b,
                 ffn_w_in, ffn_w_out, ab_w_in, ssd_conv_w, ssd_conv_b, ssd_dt_bias, ssd_a_log,
                 ssd_d, ssd_norm_g, s5_a_re, s5_a_im, s5_log_dt, s5_b_re, s5_b_im, s5_c_re, s5_c_im,
                 s5_d, s5_glu_w, s5_glu_b, ab_w_out, ret_w_in, ret_gn_g, ret_w_out, final_norm_g, shared=None):
    f = lambda a: np.ascontiguousarray(np.asarray(a, dtype=np.float32))
    if shared is None:
        sh = {}
        rc = ret_consts()
        sc_ = ssd_consts()
        s5l = s5_layouts(s5_a_re[0], s5_a_im[0], s5_log_dt[0], s5_b_re[0], s5_b_im[0], s5_c_re[0], s5_c_im[0])
        sh["ident"] = rc["ident"]
        sh["iota512"] = rc["iota512"]
        for k in ("Emat", "M01", "irow1", "jcol", "pidx"):
            sh["r_" + k] = rc[k]
        for k in ("triU", "negm4"):
            sh["a_" + k] = sc_[k]
        for k, v in s5l.items():
            if k != "iota512":
                sh["b_" + k] = v
        for p_ in ("a_", "b_"):
            sh[p_ + "ada_w"] = f(ada_mix_w[0]); sh[p_ + "ada_b"] = col_layout(ada_mix_b[0])
            sh[p_ + "ng"] = col_layout(norm_mix_g[0])
        sh["ab_w_in"] = f(ab_w_in[0])
        sh["a_cw"] = np.ascontiguousarray(f(ssd_conv_w[0]).T.reshape(40, 128, 4).transpose(1, 0, 2).reshape(128, 160))
        sh["a_cb"] = col_layout(ssd_conv_b[0])
        sh["a_dtb"] = f(ssd_dt_bias[0]).reshape(1, 48); sh["a_alog"] = f(ssd_a_log[0]).reshape(1, 48)
        sh["a_dsk"] = f(ssd_d[0]).reshape(1, 48); sh["a_sng"] = col_layout(ssd_norm_g[0])
        sh["ab_w_out"] = f(ab_w_out[0]); sh["glu_w"] = f(s5_glu_w[0]); sh["b_glub"] = col_layout(s5_glu_b[0])
        sh["b_s5d"] = col_layout(s5_d[0])
        for i, p_ in ((0, "f0_"), (1, "f1_")):
            sh[p_ + "ada_w"] = f(ada_ffn_w[i]); sh[p_ + "ada_b"] = col_layout(ada_ffn_b[i])
            sh[p_ + "ng"] = col_layout(norm_ffn_g[i]); sh[p_ + "w_in"] = f(ffn_w_in[i])
            sh[p_ + "w_out"] = f(ffn_w_out[i]); sh[p_ + "fg"] = col_layout(final_norm_g)
        sh["r_ada_w"] = f(ada_mix_w[1]); sh["r_ada_b"] = col_layout(ada_mix_b[1]); sh["r_ng"] = col_layout(norm_mix_g[1])
        sh["r_w_in"] = f(ret_w_in[0]); sh["r_w_out"] = f(ret_w_out[0]); sh["r_gng"] = f(ret_gn_g[0]).reshape(1, 4096)
        shared = sh
    m = dict(shared)
    m["xT"] = np.ascontiguousarray(f(x[b]).T)
    m["ccol"] = col_layout(np.asarray(c)[b])
    return m, shared


NCORES = 8
_CACHE = {}


def _prog(kind, ntok):
    key = (kind, ntok)
    if key not in _CACHE:
        if kind == "ffn":
            _CACHE[key] = build_ffn(ntok, final_norm=False)
        elif kind == "ffn_final":
            _CACHE[key] = build_ffn(ntok, final_norm=True)
        elif kind == "ssd":
            _CACHE[key] = build_ssd(ntok)
        elif kind == "s5":
            _CACHE[key] = build_s5(ntok)
        else:
            _CACHE[key] = build_ret(ntok)
    return _CACHE[key]


def _launch(nc, maps):
    res = run_bass_kernel_spmd(nc, maps, core_ids=list(range(NCORES)))
    return [np.asarray(r["yT"]) for r in res.results]


def kernel(**inputs):
    x = np.asarray(inputs["x"], dtype=np.float32)
    B, L, _ = x.shape
    key = ("fused", L)
    if key not in _CACHE:
        _CACHE[key] = build_fused(L)
    nc = _CACHE[key]
    maps = []
    shared = None
    per_b = []
    for b in range(B):
        m, shared = fused_inputs(b, shared=shared, **inputs)
        per_b.append(m)
    for k in range(NCORES):
        maps.append(per_b[k % B])
    res = run_bass_kernel_spmd(nc, maps, core_ids=list(range(NCORES)))
    outs = [np.asarray(res.results[b]["yT"]) for b in range(B)]
    out = np.stack([np.ascontiguousarray(o.T) for o in outs], axis=0)
    return out.astype(np.float32)
```
